# Optimizing a Trainium2 kernel written in Bass

```python
import math
import jax, jax.numpy as jnp
from jax import lax
import numpy as np

D_MODEL = 1024
BATCH = 8
SEQ = 2048
DEPTH = 2
DEC_BATCH = 128
DEC_SEQ = 4
PAST_LEN = 16384
PAGE_SIZE = 128

N_MIXERS = 2
N_GLA_LAYERS = (DEPTH + 1) // 2
N_S5_LAYERS = DEPTH // 2
N_SANDWICH_NORMS = 6
D_FF = ((8 * D_MODEL // 3 + 127) // 128) * 128
GLA_HEADS = 4
GLA_DK = D_MODEL // 2 // GLA_HEADS
GLA_DV = D_MODEL // GLA_HEADS
GLA_KEY_DIM = GLA_HEADS * GLA_DK
GLA_VAL_DIM = GLA_HEADS * GLA_DV
GATE_RANK = 16
GATE_TAU = 16.0
GLA_CHUNK = 64
S5_GROUP = 16
S5_GROUPS = D_MODEL // S5_GROUP
S5_STATE = 64
DT_MIN = 1e-3
DT_MAX = 1e-1
EPS = 1e-6

kernel_name = 'hybrid_gla_s5_macaron_step'


def rmsnorm(x, g):
    xf = x.astype(jnp.float32)
    y = xf * lax.rsqrt(jnp.mean(xf * xf, axis=-1, keepdims=True) + EPS)
    return (y * g.astype(jnp.float32)).astype(x.dtype)


def swiglu(x, w_gu, w_down):
    gate, up = jnp.split(x @ w_gu, 2, axis=-1)
    return (jax.nn.silu(gate) * up) @ w_down


def gla_recurrence(q, k, v, logf, s0):
    B, T, H = q.shape[:3]
    c = min(GLA_CHUNK, T)
    n = -(-T // c)
    pad = n * c - T
    f32 = jnp.float32
    q, k, v, logf = (a.astype(f32) for a in (q, k, v, logf))
    if pad:
        pw = ((0, 0), (0, pad), (0, 0), (0, 0))
        q, k, v, logf = (jnp.pad(a, pw) for a in (q, k, v, logf))

    def to_chunks(a):
        return a.reshape(B, n, c, H, a.shape[-1]).transpose(1, 0, 3, 2, 4)

    causal = jnp.tril(jnp.ones((c, c), dtype=bool))

    def step(S, inp):
        qc, kc, vc, gc = inp
        b = jnp.cumsum(gc, axis=2)
        diff = b[:, :, :, None, :] - b[:, :, None, :, :]
        decay = jnp.exp(jnp.where(causal[:, :, None], diff, -jnp.inf))
        att = jnp.einsum('bhtd,bhsd,bhtsd->bhts', qc, kc, decay)
        o = (jnp.einsum('bhts,bhsv->bhtv', att, vc)
             + jnp.einsum('bhtd,bhdv->bhtv', qc * jnp.exp(b), S))
        b_end = b[:, :, -1:, :]
        S_new = (jnp.exp(b_end[:, :, 0, :])[..., None] * S
                 + jnp.einsum('bhsd,bhsv->bhdv', kc * jnp.exp(b_end - b), vc))
        return S_new, o

    S, o = lax.scan(step, s0.astype(f32), (to_chunks(q), to_chunks(k), to_chunks(v), to_chunks(logf)))
    o = o.transpose(1, 0, 3, 2, 4).reshape(B, n * c, H, v.shape[-1])[:, :T]
    return o, S.astype(s0.dtype)


def gla_mixer(x, s0, w_in, w_g2, b_g, g_onorm, w_out):
    B, T, _ = x.shape
    proj = x @ w_in
    q, k, v, r, glr = jnp.split(
        proj, [GLA_KEY_DIM, 2 * GLA_KEY_DIM, 2 * GLA_KEY_DIM + GLA_VAL_DIM,
               2 * GLA_KEY_DIM + 2 * GLA_VAL_DIM], axis=-1)
    logf = jax.nn.log_sigmoid((glr @ w_g2 + b_g).astype(jnp.float32)) / GATE_TAU
    q = q.reshape(B, T, GLA_HEADS, GLA_DK) * (GLA_DK ** -0.5)
    k = k.reshape(B, T, GLA_HEADS, GLA_DK)
    v = v.reshape(B, T, GLA_HEADS, GLA_DV)
    logf = logf.reshape(B, T, GLA_HEADS, GLA_DK)
    o, S = gla_recurrence(q, k, v, logf, s0)
    o = rmsnorm(o.astype(x.dtype), g_onorm).reshape(B, T, GLA_VAL_DIM)
    o = o * jax.nn.silu(r)
    return o @ w_out, S


def s5_discretize(lam_re, lam_im, log_dt, b_re, b_im):
    f32 = jnp.float32
    lam_re, lam_im, b_re, b_im = (a.astype(f32) for a in (lam_re, lam_im, b_re, b_im))
    dt = jnp.exp(log_dt.astype(f32))[:, None]
    mag = jnp.exp(lam_re * dt)
    ang = lam_im * dt
    a_re, a_im = mag * jnp.cos(ang), mag * jnp.sin(ang)
    nr, ni = a_re - 1.0, a_im
    den = lam_re * lam_re + lam_im * lam_im
    f_re = (nr * lam_re + ni * lam_im) / den
    f_im = (ni * lam_re - nr * lam_im) / den
    bb_re = f_re[..., None] * b_re - f_im[..., None] * b_im
    bb_im = f_re[..., None] * b_im + f_im[..., None] * b_re
    return a_re, a_im, bb_re, bb_im


def s5_mixer(x, h0_re, h0_im, lam_re, lam_im, log_dt, b_re, b_im, c_re, c_im, d_skip, w_glu, b_glu):
    B, T, _ = x.shape
    f32 = jnp.float32
    xf = x.astype(f32)
    u = xf.reshape(B, T, S5_GROUPS, S5_GROUP)
    a_re, a_im, bb_re, bb_im = s5_discretize(lam_re, lam_im, log_dt, b_re, b_im)
    bu_re = jnp.einsum('btgc,gpc->btgp', u, bb_re)
    bu_im = jnp.einsum('btgc,gpc->btgp', u, bb_im)
    hr, hi = h0_re.astype(f32), h0_im.astype(f32)
    bu_re = bu_re.at[:, 0].add(a_re * hr - a_im * hi)
    bu_im = bu_im.at[:, 0].add(a_re * hi + a_im * hr)
    A_re = jnp.broadcast_to(a_re[None, None], (1, T, S5_GROUPS, S5_STATE))
    A_im = jnp.broadcast_to(a_im[None, None], (1, T, S5_GROUPS, S5_STATE))

    def combine(e1, e2):
        ar1, ai1, br1, bi1 = e1
        ar2, ai2, br2, bi2 = e2
        return (ar1 * ar2 - ai1 * ai2,
                ar1 * ai2 + ai1 * ar2,
                ar2 * br1 - ai2 * bi1 + br2,
                ar2 * bi1 + ai2 * br1 + bi2)

    _, _, h_re, h_im = lax.associative_scan(combine, (A_re, A_im, bu_re, bu_im), axis=1)
    y = (jnp.einsum('btgp,gcp->btgc', h_re, c_re.astype(f32))
         - jnp.einsum('btgp,gcp->btgc', h_im, c_im.astype(f32)))
    z = (y.reshape(B, T, D_MODEL) + d_skip.astype(f32) * xf).astype(x.dtype)
    val, gate = jnp.split(z @ w_glu + b_glu, 2, axis=-1)
    out = val * jax.nn.sigmoid(gate)
    return out, h_re[:, -1].astype(h0_re.dtype), h_im[:, -1].astype(h0_im.dtype)


def setup_inputs(seed: int = 0) -> dict:
    key = jax.random.key(seed)
    ks = jax.random.split(key, 24)
    f32 = jnp.float32

    def nrm(k, shape, scale):
        return scale * jax.random.normal(k, shape, f32)

    s5_shape = (N_S5_LAYERS, S5_GROUPS, S5_STATE)
    n_idx = jnp.arange(S5_STATE, dtype=f32)
    return {
        'x_prompt': nrm(ks[0], (BATCH, SEQ, D_MODEL), 1.0),
        'x_sample': nrm(ks[1], (DEC_BATCH, DEC_SEQ, D_MODEL), 1.0),
        'state_gla': nrm(ks[2], (N_GLA_LAYERS, DEC_BATCH, GLA_HEADS, GLA_DK, GLA_DV), 0.5),
        'state_s5_re': nrm(ks[3], (N_S5_LAYERS, DEC_BATCH, S5_GROUPS, S5_STATE), 0.1),
        'state_s5_im': nrm(ks[4], (N_S5_LAYERS, DEC_BATCH, S5_GROUPS, S5_STATE), 0.1),
        'norm_g': 1.0 + nrm(ks[5], (DEPTH, N_SANDWICH_NORMS, D_MODEL), 0.05),
        'w_ffn_gu': nrm(ks[6], (DEPTH, 2, D_MODEL, 2 * D_FF), D_MODEL ** -0.5),
        'w_ffn_down': nrm(ks[7], (DEPTH, 2, D_FF, D_MODEL), D_FF ** -0.5),
        'gla_w_in': nrm(ks[8], (N_GLA_LAYERS, D_MODEL, 2 * GLA_KEY_DIM + 2 * GLA_VAL_DIM + GATE_RANK), D_MODEL ** -0.5),
        'gla_w_g2': nrm(ks[9], (N_GLA_LAYERS, GATE_RANK, GLA_KEY_DIM), GATE_RANK ** -0.5),
        'gla_b_g': nrm(ks[10], (N_GLA_LAYERS, GLA_KEY_DIM), 0.1),
        'gla_g_onorm': 1.0 + nrm(ks[11], (N_GLA_LAYERS, GLA_DV), 0.05),
        'gla_w_out': nrm(ks[12], (N_GLA_LAYERS, GLA_VAL_DIM, D_MODEL), GLA_VAL_DIM ** -0.5),
        's5_lam_re': -0.5 * jnp.exp(nrm(ks[13], s5_shape, 0.02)),
        's5_lam_im': math.pi * n_idx + nrm(ks[14], s5_shape, 0.01),
        's5_log_dt': jax.random.uniform(ks[15], (N_S5_LAYERS, S5_GROUPS), f32, math.log(DT_MIN), math.log(DT_MAX)),
        's5_b_re': nrm(ks[16], (N_S5_LAYERS, S5_GROUPS, S5_STATE, S5_GROUP), (2 * S5_GROUP) ** -0.5),
        's5_b_im': nrm(ks[17], (N_S5_LAYERS, S5_GROUPS, S5_STATE, S5_GROUP), (2 * S5_GROUP) ** -0.5),
        's5_c_re': nrm(ks[18], (N_S5_LAYERS, S5_GROUPS, S5_GROUP, S5_STATE), S5_STATE ** -0.5),
        's5_c_im': nrm(ks[19], (N_S5_LAYERS, S5_GROUPS, S5_GROUP, S5_STATE), S5_STATE ** -0.5),
        's5_d': nrm(ks[20], (N_S5_LAYERS, D_MODEL), 1.0),
        's5_w_glu': nrm(ks[21], (N_S5_LAYERS, D_MODEL, 2 * D_MODEL), D_MODEL ** -0.5),
        's5_b_glu': nrm(ks[22], (N_S5_LAYERS, 2 * D_MODEL), 0.01),
    }


def reference(x_prompt, x_sample, state_gla, state_s5_re, state_s5_im, norm_g, w_ffn_gu, w_ffn_down,
              gla_w_in, gla_w_g2, gla_b_g, gla_g_onorm, gla_w_out,
              s5_lam_re, s5_lam_im, s5_log_dt, s5_b_re, s5_b_im, s5_c_re, s5_c_im, s5_d, s5_w_glu, s5_b_glu):

    def run(x, gla_s0, s5_h0_re, s5_h0_im):
        new_gla, new_re, new_im = [], [], []
        for i in range(DEPTH):
            g = norm_g[i]
            x = x + 0.5 * rmsnorm(swiglu(rmsnorm(x, g[0]), w_ffn_gu[i, 0], w_ffn_down[i, 0]), g[1])
            h = rmsnorm(x, g[2])
            j = i // N_MIXERS
            if i % N_MIXERS == 0:
                m, s = gla_mixer(h, gla_s0[j], gla_w_in[j], gla_w_g2[j], gla_b_g[j], gla_g_onorm[j], gla_w_out[j])
                new_gla.append(s)
            else:
                m, hr, hi = s5_mixer(h, s5_h0_re[j], s5_h0_im[j], s5_lam_re[j], s5_lam_im[j], s5_log_dt[j],
                                     s5_b_re[j], s5_b_im[j], s5_c_re[j], s5_c_im[j], s5_d[j],
                                     s5_w_glu[j], s5_b_glu[j])
                new_re.append(hr)
                new_im.append(hi)
            x = x + rmsnorm(m, g[3])
            x = x + 0.5 * rmsnorm(swiglu(rmsnorm(x, g[4]), w_ffn_gu[i, 1], w_ffn_down[i, 1]), g[5])
        return x, jnp.stack(new_gla), jnp.stack(new_re), jnp.stack(new_im)

    zero_gla = jnp.zeros((N_GLA_LAYERS, x_prompt.shape[0], GLA_HEADS, GLA_DK, GLA_DV), x_prompt.dtype)
    zero_s5 = jnp.zeros((N_S5_LAYERS, x_prompt.shape[0], S5_GROUPS, S5_STATE), x_prompt.dtype)
    y_prompt, gla_p, s5re_p, s5im_p = run(x_prompt, zero_gla, zero_s5, zero_s5)
    y_sample, gla_s, s5re_s, s5im_s = run(x_sample, state_gla, state_s5_re, state_s5_im)
    return (y_prompt, y_sample, gla_p, s5re_p, s5im_p, gla_s, s5re_s, s5im_s)
```

```python
import numpy as np
from contextlib import ExitStack
import concourse.bass as bass
import concourse.mybir as mybir
from concourse.bass_utils import run_bass_kernel_spmd

F32 = mybir.dt.float32
BF16 = mybir.dt.bfloat16
AF = mybir.ActivationFunctionType
ALU = mybir.AluOpType
ENGS = ['pe', 'act', 'dve', 'pool', 'sp']

NCORE = 8
D = 1024; KC = 8; DFF = 2816; FC = 22
NB = 4; PB = 512; NS = 64; TBM = 576
EPS = 1e-6
LCH = 32
MAGIC = 12582912.0
TWO_PI = float(2 * np.pi)


class Prog:
    def __init__(self):
        self.ops = {e: [] for e in ENGS}
        self.last_w = {}
        self.reads = {}
        self.seen = {e: {} for e in ENGS}
        self.dma_cnt = {}
        self.flag = {e: set() for e in ENGS}

    def _need(self, eng, tok, waits, is_dma, raw):
        if tok is None:
            return
        if tok[0] == 'eng':
            _, X, idx = tok
            if X == eng and not is_dma:
                if eng == 'pe' or not raw:
                    return
            key = ('eng', X)
            if self.seen[eng].get(key, -1) >= idx:
                return
            self.seen[eng][key] = idx
            self.flag[X].add(idx)
            waits.append(tok)
        else:
            _, k, val = tok
            key = ('dma', k)
            if self.seen[eng].get(key, -1) >= val:
                return
            self.seen[eng][key] = val
            waits.append(tok)

    def op(self, eng, fn, r=(), w=(), dma=None):
        idx = len(self.ops[eng])
        is_dma = dma is not None
        if is_dma:
            self.dma_cnt[dma] = self.dma_cnt.get(dma, 0) + 1
            tok = ('dma', dma, self.dma_cnt[dma] * 16)
        else:
            tok = ('eng', eng, idx)
        waits = []
        cand = {}

        def add(t, raw):
            if t is None:
                return
            if t[0] == 'eng':
                if t[1] == eng and not is_dma and (eng == 'pe' or not raw):
                    return
                k = ('eng', t[1])
            else:
                k = ('dma', t[1])
            if k not in cand or cand[k][2] < t[2]:
                cand[k] = t
        for res in r:
            add(self.last_w.get(res), True)
        for res in w:
            add(self.last_w.get(res), False)
            for t in self.reads.get(res, ()):
                add(t, False)
        for t in cand.values():
            self._need(eng, t, waits, is_dma, True)
        for res in r:
            self.reads.setdefault(res, []).append(tok)
        for res in w:
            self.last_w[res] = tok
            self.reads[res] = []
        self.ops[eng].append(dict(fn=fn, waits=waits, dma=dma))
        return tok

    def wait_all(self, eng):
        waits = []
        for k, c in self.dma_cnt.items():
            self._need(eng, ('dma', k, c * 16), waits, True, True)
        for X in ENGS:
            if X != eng:
                for idx in range(len(self.ops[X]) - 1, -1, -1):
                    if self.ops[X][idx]['dma'] is None and self.ops[X][idx]['fn'] is not None:
                        self._need(eng, ('eng', X, idx), waits, True, True)
                        break
        self.ops[eng].append(dict(fn=None, waits=waits, dma=None))

    def replay(self, block, sems, dma_sems):
        rank = {}
        for e in ENGS:
            rank[e] = {idx: i + 1 for i, idx in enumerate(sorted(self.flag[e]))}
        ops = self.ops
        flag = self.flag

        def run(name, e):
            for idx, o in enumerate(ops[name]):
                for t in o['waits']:
                    if t[0] == 'eng':
                        e.wait_ge(sems[t[1]], rank[t[1]][t[2]])
                    else:
                        e.wait_ge(dma_sems[t[1]], t[2])
                if o['fn'] is None:
                    continue
                ins = o['fn'](e)
                if o['dma'] is not None:
                    ins.then_inc(dma_sems[o['dma']], 16)
                elif idx in flag[name]:
                    ins.then_inc(sems[name], 1)

        block.tensor(lambda e: run('pe', e))
        block.scalar(lambda e: run('act', e))
        block.vector(lambda e: run('dve', e))
        block.gpsimd(lambda e: run('pool', e))
        block.sync(lambda e: run('sp', e))


def weight_tiles():
    t = []
    for l in range(2):
        for f in range(2):
            if f == 1:
                if l == 0:
                    t += [('win%d' % i, 2048) for i in range(8)]
                    t += [('wv%d' % i, 2048) for i in range(4)]
                    t += [('wglr', 128)]
                    t += [('wout%d' % i, 2048) for i in range(4)]
                else:
                    t += [('wglu%d' % i, 2048) for i in range(8)]
            t += [('gu%d_%d_%d' % (l, f, j), 2048) for j in range(FC)]
            t += [('dn%d_%d_%d' % (l, f, m), 2816) for m in range(KC)]
    return t


def build_wstream(w_ffn_gu, w_ffn_down, gla_w_in, gla_w_out, s5_w_glu):
    def kmaj(w):
        C = w.shape[1]
        return w.reshape(KC, 128, C).transpose(1, 0, 2).reshape(128, KC * C)
    parts = []
    win = gla_w_in[0]
    for l in range(2):
        for f in range(2):
            if f == 1:
                if l == 0:
                    fm = np.concatenate([win[:, 0:1024], win[:, 2048:3072]], axis=1)
                    for i in range(8):
                        parts.append(kmaj(fm[:, i * 256:(i + 1) * 256]))
                    for i in range(4):
                        parts.append(kmaj(win[:, 1024 + i * 256:1024 + (i + 1) * 256]))
                    parts.append(kmaj(win[:, 3072:3088]))
                    for i in range(4):
                        parts.append(kmaj(gla_w_out[0][:, i * 256:(i + 1) * 256]))
                else:
                    wg = s5_w_glu[0]
                    for m in range(8):
                        parts.append(kmaj(np.concatenate([wg[:, m * 128:(m + 1) * 128],
                                                          wg[:, 1024 + m * 128:1024 + (m + 1) * 128]], axis=1)))
            gu = w_ffn_gu[l, f]
            for j in range(FC):
                parts.append(kmaj(np.concatenate([gu[:, j * 128:(j + 1) * 128],
                                                  gu[:, DFF + j * 128:DFF + (j + 1) * 128]], axis=1)))
            dn = w_ffn_down[l, f]
            for m in range(KC):
                parts.append(dn[:, m * 128:(m + 1) * 128].reshape(FC, 128, 128).transpose(1, 0, 2).reshape(128, FC * 128))
    return np.ascontiguousarray(np.concatenate(parts, axis=1), dtype=np.float32)


def build_nc(stop_after=None):
    nc = bass.Bass("TRN2", target_bir_lowering=False)
    tiles = weight_tiles()
    offs = np.cumsum([0] + [f for _, f in tiles]).tolist()
    TOT = offs[-1]

    def din(name, shape):
        return nc.dram_tensor(name, shape, F32, kind="ExternalInput").ap()

    def dout(name, shape):
        return nc.dram_tensor(name, shape, F32, kind="ExternalOutput").ap()

    xTp = din("xTp", [D, 2048]); xTs = din("xTs", [D, NS])
    WS = din("WS", [128, TOT])
    prm_d = din("prm", [128, 128]); wg2a_d = din("wg2a", [17, 512])
    cst_d = din("cst", [128, 512])
    sgla_d = din("sgla", [16, 4, 128, 256])
    s5re_d = din("s5re", [512, 128]); s5im_d = din("s5im", [512, 128])
    s5p_d = din("s5p", [32, 384])
    bpad_d = din("bpad", [2, 128, 4096]); cpad_d = din("cpad", [2, 128, 4096])
    yTp = dout("yTp", [D, 2048]); yTs = dout("yTs", [D, NS])
    oglap = dout("oglap", [4, 128, 256]); oglas = dout("oglas", [16, 4, 128, 256])
    o5rep = dout("o5rep", [32, 128]); o5imp = dout("o5imp", [32, 128])
    o5res = dout("o5res", [512, 128]); o5ims = dout("o5ims", [512, 128])

    P = Prog()
    es = ExitStack()
    with es:
        def sb(name, shape, dt=F32):
            return es.enter_context(nc.sbuf_tensor(name, shape, dt))

        x = sb("x", [128, KC, TBM])
        prm = sb("prm_s", [128, 128]); wg2a = sb("wg2a_s", [17, 512]); cst = sb("cst_s", [128, 512])
        identb = sb("identb", [128, 128], BF16); onesb = sb("onesb", [128, 128], BF16)
        cns = sb("cns", [128, 4])
        Sg = sb("Sg", [128, 4, 256]); Sgb = sb("Sgb", [128, 4, 256], BF16)
        WB = [sb("WBre", [128, 32, 128], BF16), sb("WBim", [128, 32, 128], BF16)]
        WC = [sb("WCre", [128, 32, 128], BF16), sb("WCim", [128, 32, 128], BF16)]
        cosT = sb("cosT", [128, 32, LCH], BF16); sinT = sb("sinT", [128, 32, LCH], BF16); sinN = sb("sinN", [128, 32, LCH], BF16)
        RT = sb("RT", [128, 32, LCH])
        s5s = sb("s5s", [128, 12, 32])
        hp = sb("hp", [128, 2, 32])
        ah = sb("ah", [128, 4, 256])
        H0 = sb("H0", [128, 2, 16, 32]); Hn = H0
        NSL = 3
        wsl = [sb("wsl%d" % i, [128, 2816], BF16) for i in range(NSL)]
        R_xn = sb("R_xn", [128, KC, TBM], BF16)
        R_h = sb("R_h", [128, FC * TBM], BF16)
        R_y = sb("R_y", [128, KC, TBM])
        R_s = sb("R_s", [128, 4, TBM])
        sq = sb("sq", [128, 2, TBM], BF16)
        rstd = sb("rstd", [128, TBM]); rsq = sb("rsq", [128, TBM]); tmpn = sb("tmpn", [128, 2, TBM])
        R_t = sb("R_t", [128, 7168])
        S0b = [sb("S0b%d" % i, [128, 4, 256]) for i in range(2)]
        Snb = S0b
        RTALL = ["R%d" % i for i in range(14)]
        def seg(a, b_):
            return ["R%d" % i for i in range(a // 512, (b_ + 511) // 512)]
        PS = [es.enter_context(nc.psum_tensor("B%d" % i, [128, 512], F32)) for i in range(8)]

        identf = cst[:, 0:128]; U = cst[:, 128:256]; Us = cst[:, 256:384]
        iotaj = cst[:, 384:384 + LCH]; Msk = cst[:, 416:432]
        g_ap = lambda l, n, kc: prm[:, (l * 6 + n) * 8 + kc:(l * 6 + n) * 8 + kc + 1]
        gon_ap = lambda vc: prm[:, 96 + vc:97 + vc]
        d_ap = lambda kc: prm[:, 98 + kc:99 + kc]
        bglu_ap = lambda i: prm[:, 106 + i:107 + i]

        def MM(out, lhsT, rhs, start, stop, r, w):
            P.op('pe', lambda e: e.matmul(out, lhsT=lhsT, rhs=rhs, start=start, stop=stop), r=r, w=w)

        def TR(out, in_, ident, r, w):
            P.op('pe', lambda e: e.transpose(out, in_, ident), r=r, w=w)

        def ACT(out, in_, func, r, w, bias=None, scale=None):
            kw = {}
            if bias is not None: kw['bias'] = bias
            if scale is not None: kw['scale'] = scale
            P.op('act', lambda e: e.activation(out=out, in_=in_, func=func, **kw), r=r, w=w)

        def STT(eng, out, in0, scalar, in1, op0, op1, r, w):
            eng = 'dve'
            P.op(eng, lambda e: e.scalar_tensor_tensor(out=out, in0=in0, scalar=scalar, in1=in1, op0=op0, op1=op1), r=r, w=w)

        def TT(eng, out, in0, in1, op, r, w):
            P.op(eng, lambda e: e.tensor_tensor(out=out, in0=in0, in1=in1, op=op), r=r, w=w)

        def TS(eng, out, in0, s1, s2, op0, op1, r, w):
            if s2 is None:
                P.op(eng, lambda e: e.tensor_single_scalar(out=out, in_=in0, scalar=s1, op=op0), r=r, w=w)
            else:
                P.op(eng, lambda e: e.tensor_scalar(out=out, in0=in0, scalar1=s1, scalar2=s2, op0=op0, op1=op1), r=r, w=w)

        def CP(eng, out, in_, r, w):
            if eng == 'act':
                P.op('act', lambda e: e.copy(out=out, in_=in_), r=r, w=w)
            else:
                P.op(eng, lambda e: e.tensor_copy(out=out, in_=in_), r=r, w=w)

        def MS(eng, ap, val, w):
            P.op(eng, lambda e: e.memset(ap, val), w=w)

        def DMA(eng, out, in_, r, w, key):
            P.op(eng, lambda e: e.dma_start(out=out, in_=in_), r=r, w=w, dma=key)

        def RECIP(out, in_, r, w):
            P.op('dve', lambda e: e.reciprocal(out=out, in_=in_), r=r, w=w)

        class WStream:
            def __init__(self):
                self.n = 0
                self.issued = 0
                self.total = NB * len(tiles)

            def _issue(self, i):
                ti = i % len(tiles)
                F = tiles[ti][1]
                s = i % NSL
                DMA('pool', wsl[s][:, 0:F], WS[:, offs[ti]:offs[ti] + F], r=[], w=['ws%d' % s], key='ws%d' % s)

            def get(self, name):
                i = self.n
                assert tiles[i % len(tiles)][0] == name, (tiles[i % len(tiles)][0], name)
                while self.issued < min(i + NSL, self.total):
                    self._issue(self.issued)
                    self.issued += 1
                self.n += 1
                s = i % NSL
                return wsl[s], 'ws%d' % s

        Wst = WStream()

        DMA('sp', prm[:], prm_d, [], ['prm'], 'ld_prm')
        DMA('sp', wg2a[:], wg2a_d, [], ['wg2a'], 'ld_wg2a')
        DMA('sp', cst[:], cst_d, [], ['cst'], 'ld_cst')
        CP('dve', identb[:], identf, ['cst'], ['identb'])
        MS('dve', onesb[:], 1.0, ['onesb'])
        MS('dve', cns[:, 0:1], EPS, ['cns'])
        MS('dve', cns[:, 1:2], 4 * EPS, ['cns'])
        XR = ['x%d' % k for k in range(KC)]
        MS('dve', Sg[:], 0.0, ['Sg%d' % h for h in range(4)])
        MS('dve', Sgb[:], 0.0, ['Sgb%d' % h for h in range(4)])
        MS('dve', hp[:], 0.0, ['hp'])
        eps_ap = cns[:, 0:1]

        def colt(b):
            return [(0, 512)] + ([(512, 64)] if b == 0 else [])

        def wide(banks):
            main, aux = banks
            return [(PS[main][:, 0:512], 'B%d' % main), (PS[aux][:, 0:64], 'B%d' % aux)]

        def s5_setup():
            sp = R_t
            lam_re = sp[0:32, 0:128]; lam_im = sp[0:32, 128:256]; ldt = sp[0:32, 256:384]
            DMA('sp', sp[0:32, 0:384], s5p_d, [], RTALL, 'ld_s5p')
            t = lambda i: sp[0:32, 384 + i * 128:384 + (i + 1) * 128]
            dt_, mag, ang, den, are, aim, fre, fim, tA, tB, sn, cs_ = [t(i) for i in range(12)]
            aLre, aLim, cL1, sL1, c3_, s3_, angx = [t(12 + i) for i in range(7)]
            rw = dict(r=RTALL, w=RTALL)
            ACT(dt_, ldt, AF.Exp, **rw)
            TT('dve', den, lam_re, dt_, ALU.mult, **rw)
            TS('dve', mag, den, 1.0 / 7, 1.0, ALU.mult, ALU.add, **rw)
            for c_ in (6, 5, 4, 3, 2, 1):
                TT('dve', mag, mag, den, ALU.mult, **rw)
                TS('dve', mag, mag, 1.0 / c_ if c_ > 1 else 1.0, 1.0, ALU.mult, ALU.add, **rw)
            TT('dve', ang, lam_im, dt_, ALU.mult, **rw)

            def sincos(dst, src, shift):
                TS('dve', tA, src, float(1 / TWO_PI), float(shift / TWO_PI), ALU.mult, ALU.add, **rw)
                TS('dve', tA, tA, MAGIC, None, ALU.add, None, **rw)
                TS('dve', tA, tA, -MAGIC, -TWO_PI, ALU.add, ALU.mult, **rw)
                STT('dve', tB, src, 1.0, tA, ALU.mult, ALU.add, **rw)
                if shift != 0.0:
                    TS('dve', tB, tB, float(shift), None, ALU.add, None, **rw)
                ACT(dst, tB, AF.Sin, **rw)
            sincos(sn, ang, 0.0)
            sincos(cs_, ang, float(np.pi / 2))
            TT('dve', are, mag, cs_, ALU.mult, **rw)
            TT('dve', aim, mag, sn, ALU.mult, **rw)
            TT('dve', den, lam_re, lam_re, ALU.mult, **rw)
            TT('dve', tA, lam_im, lam_im, ALU.mult, **rw)
            TT('dve', den, den, tA, ALU.add, **rw)
            RECIP(den, den, **rw)
            TS('dve', tB, are, -1.0, None, ALU.add, None, **rw)
            TT('dve', fre, tB, lam_re, ALU.mult, **rw)
            TT('dve', tA, aim, lam_im, ALU.mult, **rw)
            TT('dve', fre, fre, tA, ALU.add, **rw)
            TT('dve', fre, fre, den, ALU.mult, **rw)
            TT('dve', fim, aim, lam_re, ALU.mult, **rw)
            TT('dve', tA, tB, lam_im, ALU.mult, **rw)
            TT('dve', fim, fim, tA, ALU.subtract, **rw)
            TT('dve', fim, fim, den, ALU.mult, **rw)
            for mult_, (sdst, cdst) in ((float(LCH), (aLim, aLre)), (float(LCH - 1), (sL1, cL1)), (3.0, (s3_, c3_))):
                TS('dve', angx, ang, mult_, None, ALU.mult, None, **rw)
                sincos(sdst, angx, 0.0)
                sincos(cdst, angx, float(np.pi / 2))
            TT('dve', aLre, aLre, mag, ALU.mult, **rw)
            TT('dve', aLim, aLim, mag, ALU.mult, **rw)
            for i, src in enumerate([are, aim, fre, fim, ang, mag, aLre, aLim, cL1, sL1, c3_, s3_]):
                TR(PS[0][:, i * 32:(i + 1) * 32], src, identf[0:32, 0:32], RTALL + ['cst'], ['B0'])
            CP('dve', s5s[:].rearrange("p a k -> p (a k)"), PS[0][:, 0:384], ['B0'], ['s5s'])
            th = s5s[:, 4, :]; rr_ = s5s[:, 5, :]
            A3 = R_t[:, 0:1024].rearrange("p (k j) -> p k j", j=LCH)
            B3 = R_t[:, 1024:2048].rearrange("p (k j) -> p k j", j=LCH)
            C3 = R_t[:, 2048:3072].rearrange("p (k j) -> p k j", j=LCH)
            io3 = iotaj.unsqueeze(1).to_broadcast([128, 32, LCH])
            th3 = th.unsqueeze(2).to_broadcast([128, 32, LCH])
            TT('dve', A3, io3, th3, ALU.mult, ['cst', 's5s'] + RTALL, RTALL)
            for dst, shift in ((sinT, 0.0), (cosT, float(np.pi / 2))):
                TS('dve', B3, A3, float(1 / TWO_PI), float(shift / TWO_PI), ALU.mult, ALU.add, **rw)
                TS('dve', B3, B3, MAGIC, None, ALU.add, None, **rw)
                TS('dve', B3, B3, -MAGIC, -TWO_PI, ALU.add, ALU.mult, **rw)
                TT('dve', C3, A3, B3, ALU.add, **rw)
                if shift != 0.0:
                    TS('dve', C3, C3, float(shift), None, ALU.add, None, **rw)
                ACT(dst[:], C3, AF.Sin, RTALL, [dst is sinT and 'sinT' or 'cosT'])
                if dst is sinT:
                    ACT(sinN[:], C3, AF.Sin, RTALL, ['sinN'], scale=-1.0)
            r3 = rr_.unsqueeze(2).to_broadcast([128, 32, LCH])
            CP('dve', RT[:], r3, ['s5s'], ['RT'])
            MS('dve', RT[:, :, 0:1], 0.0, ['RT'])
            bre = R_t[:, 0:1024]; bim = R_t[:, 1024:2048]; o1 = R_t[:, 2048:3072]; o2 = R_t[:, 3072:4096]
            for kg in range(4):
                DMA('sp', bre, bpad_d[0, :, kg * 1024:(kg + 1) * 1024], [], RTALL, 'ld_b0')
                DMA('sp', bim, bpad_d[1, :, kg * 1024:(kg + 1) * 1024], [], RTALL, 'ld_b1')
                v3 = lambda a: a.rearrange("p (k c) -> p k c", c=128)
                f_re3 = s5s[:, 2, kg * 8:(kg + 1) * 8].unsqueeze(2).to_broadcast([128, 8, 128])
                f_im3 = s5s[:, 3, kg * 8:(kg + 1) * 8].unsqueeze(2).to_broadcast([128, 8, 128])
                rs = RTALL + ['s5s']
                TT('dve', v3(o1), v3(bre), f_re3, ALU.mult, rs, RTALL)
                TT('dve', v3(o2), v3(bim), f_im3, ALU.mult, rs, RTALL)
                TT('dve', o1, o1, o2, ALU.subtract, rs, RTALL)
                TT('dve', v3(o2), v3(bre), f_im3, ALU.mult, rs, RTALL)
                TT('dve', v3(bre), v3(bim), f_re3, ALU.mult, rs, RTALL)
                TT('dve', o2, o2, bre, ALU.add, rs, RTALL)
                for ri, src in enumerate((o1, o2)):
                    for kk in range(8):
                        bank = PS[1 + (kk // 4) + 2 * ri]
                        TR(bank[:, (kk % 4) * 128:(kk % 4 + 1) * 128], src[:, kk * 128:(kk + 1) * 128], identf,
                           RTALL + ['cst'], ['B%d' % (1 + (kk // 4) + 2 * ri)])
                    for hh in range(2):
                        bi = 1 + hh + 2 * ri
                        CP('act', WB[ri][:, kg * 8 + hh * 4:kg * 8 + hh * 4 + 4, :].rearrange("p k c -> p (k c)"),
                           PS[bi][:, 0:512], ['B%d' % bi], ['WB'])
            for kg in range(4):
                DMA('sp', bre, cpad_d[0, :, kg * 1024:(kg + 1) * 1024], [], RTALL, 'ld_b0')
                DMA('sp', bim, cpad_d[1, :, kg * 1024:(kg + 1) * 1024], [], RTALL, 'ld_b1')
                CP('act', WC[0][:, kg * 8:(kg + 1) * 8, :].rearrange("p k c -> p (k c)"), bre, RTALL, ['WC'])
                P.op('act', lambda e, kg=kg: e.mul(out=WC[1][:, kg * 8:(kg + 1) * 8, :].rearrange("p k c -> p (k c)"),
                                                    in_=bim, mul=-1.0), r=RTALL, w=['WC'])
            for ri, src in enumerate((s5re_d, s5im_d)):
                for j in range(4):
                    buf = R_t[:, 4096 + j * 128:4096 + (j + 1) * 128]
                    DMA('sp', buf, src[j * 128:(j + 1) * 128, :], [], RTALL, 'ld_h0_%d' % j)
                    TR(PS[5 + ri][:, j * 128:(j + 1) * 128], buf, identf, RTALL + ['cst'], ['B%d' % (5 + ri)])
                CP('dve', H0[:, ri, :, :].rearrange("p s k -> p (s k)"), PS[5 + ri][:, 0:512], ['B%d' % (5 + ri)], ['H0'])

        s5_setup()

        def RECIPF(out, in_, r, w):
            P.op('dve', lambda e: e.reciprocal_approx_fast(out=out, in_=in_), r=r, w=w)

        def stat_sq(kc, src_aps, src_res, b):
            s_ = kc % 2
            for ci, (c0, n) in enumerate(colt(b)):
                ACT(sq[:, s_, c0:c0 + n], src_aps[ci], AF.Square, src_res[ci], ['sq%d' % s_])

        def stat_mm(kc, b, banks):
            W = wide(banks)
            s_ = kc % 2
            for ci, (c0, n) in enumerate(colt(b)):
                MM(W[ci][0][:, 0:n], onesb[:], sq[:, s_, c0:c0 + n], kc == 0, kc == KC - 1,
                   ['onesb', 'sq%d' % s_], [W[ci][1]])

        def stat_fin(b, banks, scale):
            W = wide(banks)
            cts = colt(b)
            nc_ = sum(n for _, n in cts)
            bias = eps_ap if scale == 1.0 else cns[:, 1:2]
            assert scale in (1.0, 0.5)
            for ci, (c0, n) in enumerate(cts):
                ACT(rsq[:, c0:c0 + n], W[ci][0][:, 0:n], AF.Ln, [W[ci][1], 'cns'], ['rsq'], bias=bias,
                    scale=1.0 / (D * scale * scale))
            ACT(rstd[:, 0:nc_], rsq[:, 0:nc_], AF.Exp, ['rsq'], ['rstd'], scale=-0.5)

        def prenorm(b, l, n_idx):
            cts = colt(b)
            nc_ = sum(n for _, n in cts)
            for kc in range(KC):
                stat_sq(kc, [x[:, kc, c0:c0 + n] for c0, n in cts], [['x%d' % kc]] * len(cts), b)
                stat_mm(kc, b, (6, 7))
            stat_fin(b, (6, 7), 1.0)
            for kc in range(KC):
                STT('dve', R_xn[:, kc, 0:nc_], x[:, kc, 0:nc_], g_ap(l, n_idx, kc), rstd[:, 0:nc_],
                    ALU.mult, ALU.mult, ['x%d' % kc, 'prm', 'rstd'], ['xn%d' % kc])

        def postnorm_res(b, l, n_idx, scale, banks=(1, 5), stats_done=True):
            cts = colt(b)
            nc_ = sum(n for _, n in cts)
            if not stats_done:
                for kc in range(KC):
                    stat_sq(kc, [R_y[:, kc, c0:c0 + n] for c0, n in cts], [['y']] * len(cts), b)
                    stat_mm(kc, b, banks)
            stat_fin(b, banks, scale)
            for kc in range(KC):
                s_ = kc % 2
                STT('dve', tmpn[:, s_, 0:nc_], R_y[:, kc, 0:nc_], g_ap(l, n_idx, kc), rstd[:, 0:nc_], ALU.mult, ALU.mult,
                    ['y', 'prm', 'rstd'], ['tmpn%d' % s_])
                TT('dve', x[:, kc, 0:nc_], x[:, kc, 0:nc_], tmpn[:, s_, 0:nc_], ALU.add,
                   ['tmpn%d' % s_, 'x%d' % kc], ['x%d' % kc])

        def ffn(b, l, f):
            cts = colt(b)
            nc_ = sum(n for _, n in cts)
            prenorm(b, l, 0 if f == 0 else 4)
            hview = R_h[:, 0:FC * TBM].rearrange("p (j t) -> p j t", t=TBM)
            for j in range(FC):
                wt, wres = Wst.get('gu%d_%d_%d' % (l, f, j))
                par = j % 2
                Wg = wide((0 + 2 * par, 4 + 2 * par)); Wu = wide((1 + 2 * par, 5 + 2 * par))
                for half, Wd in ((0, Wg), (1, Wu)):
                    for ci, (c0, n) in enumerate(cts):
                        for kc in range(KC):
                            MM(Wd[ci][0][:, 0:n], wt[:, kc * 256 + half * 128:kc * 256 + half * 128 + 128],
                               R_xn[:, kc, c0:c0 + n], kc == 0, kc == KC - 1, [wres, 'xn%d' % kc], [Wd[ci][1]])
                for ci, (c0, n) in enumerate(cts):
                    ACT(R_s[:, par, c0:c0 + n], Wg[ci][0][:, 0:n], AF.Silu, [Wg[ci][1]], ['s%d' % par])
                    TT('dve', hview[:, j, c0:c0 + n], R_s[:, par, c0:c0 + n], Wu[ci][0][:, 0:n], ALU.mult,
                       ['s%d' % par, Wu[ci][1]], ['h%d' % j])
            for m in range(KC):
                wt, wres = Wst.get('dn%d_%d_%d' % (l, f, m))
                par = m % 2
                Wy = wide((0 + 2 * par, 4 + 2 * par))
                for ci, (c0, n) in enumerate(cts):
                    for j in range(FC):
                        MM(Wy[ci][0][:, 0:n], wt[:, j * 128:(j + 1) * 128], hview[:, j, c0:c0 + n], j == 0, j == FC - 1,
                           [wres, 'h%d' % j], [Wy[ci][1]])
                    CP('act', R_y[:, m, c0:c0 + n], Wy[ci][0][:, 0:n], [Wy[ci][1]], ['y'])
                stat_sq(m, [Wy[ci][0][:, 0:n] for ci, (c0, n) in enumerate(cts)], [[Wy[ci][1]] for ci in range(len(cts))], b)
                if m > 0:
                    stat_mm(m - 1, b, (1, 5))
            stat_mm(KC - 1, b, (1, 5))
            postnorm_res(b, l, 1 if f == 0 else 5, 0.5)

        def gla(b):
            cts = colt(b)
            nc_ = sum(n for _, n in cts)
            prenorm(b, 0, 2)
            hn = R_xn
            qT = R_h[:].bitcast(F32)[:, 0:4 * TBM].rearrange("p (h t) -> p h t", t=TBM)
            kT = R_h[:].bitcast(F32)[:, 4 * TBM:8 * TBM].rearrange("p (h t) -> p h t", t=TBM)
            glrA = R_h[:].bitcast(F32)[0:17, 8 * TBM:9 * TBM]
            sr = R_s[:].rearrange("p a t -> p (a t)").bitcast(BF16)[:, 0:KC * TBM].rearrange("p (c t) -> p c t", t=TBM)
            V = R_y[:].rearrange("p a t -> p (a t)").bitcast(BF16)[:, 0:5 * 1024].rearrange("p (a v) -> p a v", v=1024)
            RH = ['h%d' % j for j in range(FC)]
            SR = ['s0', 's1', 's2', 's3']
            XN = ['xn%d' % k for k in range(KC)]
            MS('pool', glrA, 1.0, RH[16:18])
            for i in range(8):
                wt, wres = Wst.get('win%d' % i)
                for cc in range(2):
                    ch = i * 2 + cc
                    par = ch % 2
                    Wd = wide((0 + 2 * par, 4 + 2 * par))
                    for ci, (c0, n) in enumerate(cts):
                        for kc in range(KC):
                            MM(Wd[ci][0][:, 0:n], wt[:, kc * 256 + cc * 128:kc * 256 + cc * 128 + 128], hn[:, kc, c0:c0 + n],
                               kc == 0, kc == KC - 1, [wres, 'xn%d' % kc], [Wd[ci][1]])
                        if ch < 4:
                            ACT(qT[:, ch, c0:c0 + n], Wd[ci][0][:, 0:n], AF.Copy, [Wd[ci][1]], RH[0:8], scale=float(128 ** -0.5))
                        elif ch < 8:
                            CP('act', kT[:, ch - 4, c0:c0 + n], Wd[ci][0][:, 0:n], [Wd[ci][1]], RH[8:16])
                        else:
                            ACT(sr[:, ch - 8, c0:c0 + n], Wd[ci][0][:, 0:n], AF.Silu, [Wd[ci][1]], SR)
            ttiles = [(i * 128, 128, False) for i in range(4)] + ([(512, 64, True)] if b == 0 else [])
            for vt in range(4):
                wt, wres = Wst.get('wv%d' % vt)
                for ti, (c0, n, smp) in enumerate(ttiles):
                    bank = 1 + (ti % 2) * 2
                    for kc in range(KC):
                        MM(PS[bank][0:n, 0:256], hn[:, kc, c0:c0 + n], wt[:, kc * 256:(kc + 1) * 256], kc == 0, kc == KC - 1,
                           [wres, 'xn%d' % kc], ['B%d' % bank])
                    CP('act' if ti % 2 == 0 else 'dve', V[0:n, ti, vt * 256:(vt + 1) * 256], PS[bank][0:n, 0:256], ['B%d' % bank], ['y'])
            wt, wres = Wst.get('wglr')
            for ci, (c0, n) in enumerate(cts):
                Wd = wide((0, 4))
                for kc in range(KC):
                    MM(Wd[ci][0][0:16, 0:n], wt[:, kc * 16:(kc + 1) * 16], hn[:, kc, c0:c0 + n], kc == 0, kc == KC - 1,
                       [wres, 'xn%d' % kc], [Wd[ci][1]])
                CP('act', glrA[0:16, c0:c0 + n], Wd[ci][0][0:16, 0:n], [Wd[ci][1]], RH[16:18])
            go = R_xn
            Rt = R_t
            Lt = Rt[:, 0:512]
            Eq = Rt[:, 512:1024].rearrange("p (h t) -> p h t", h=4)
            Ek = Rt[:, 1024:1536].rearrange("p (h t) -> p h t", h=4)
            Qf = Rt[:, 1536:2048].rearrange("p (h t) -> p h t", h=4)
            oT = Rt[:, 2048:3072].rearrange("p (c t) -> p c t", t=128)
            og = Rt[:, 3072:4096].rearrange("p (c t) -> p c t", t=128)
            ro = Rt[:, 4096:4608].rearrange("p (h t) -> p h t", h=4)
            bfv = Rt[:, 4608:6144].bitcast(BF16)
            Qt = bfv[:, 0:512].rearrange("p (h t) -> p h t", h=4)
            Kt = bfv[:, 512:1024].rearrange("p (h t) -> p h t", h=4)
            attm = bfv[:, 1024:1536].rearrange("p (h t) -> p h t", h=4)
            Ktok = bfv[:, 1536:2048].rearrange("p (h d) -> p h d", h=4)
            sqo = bfv[:, 2048:3072].rearrange("p (c t) -> p c t", t=128)
            Km = sq[:, 0, 0:512].rearrange("p (h d) -> p h d", h=4)
            tmpS = ah
            nLt, nEq, nEk, nQf, noT, nog, nro = ['R0'], ['R1'], ['R2'], ['R3'], ['R4', 'R5'], ['R6', 'R7'], ['R8']
            nQt, nKt, natt, nKtok, nsqo, nKm = ['R9'], ['R9'], ['R10'], ['R10'], ['R11'], ['sq0']
            nqT, nkT, nglr = RH[0:8], RH[8:16], RH[16:18]

            for ti, (c0, n, smp) in enumerate(ttiles):
                Um = Us if smp else U
                MM(PS[0][0:n, 0:512], glrA[0:17, c0:c0 + n], wg2a[0:17, :], True, True, nglr + ['wg2a'], ['B0'])
                ACT(Lt[0:n, :], PS[0][0:n, 0:512], AF.Exp, ['B0'], nLt, scale=-1.0)
                ACT(Lt[0:n, :], Lt[0:n, :], AF.Ln, nLt, nLt, bias=1.0)
                for h in range(4):
                    MM(PS[1][:, h * 128:h * 128 + n], Lt[0:n, h * 128:(h + 1) * 128], Um[0:n, 0:n], True, True,
                       nLt + ['cst'], ['B1'])
                cs3 = PS[1][:, 0:512].rearrange("p (h t) -> p h t", h=4)[:, :, 0:n]
                ACT(Eq[:, :, 0:n], cs3, AF.Exp, ['B1'], nEq, scale=-1.0 / 16)
                ACT(Ek[:, :, 0:n], cs3, AF.Exp, ['B1'], nEk, scale=1.0 / 16)
                TT('dve', Qt[:, :, 0:n], qT[:, :, c0:c0 + n], Eq[:, :, 0:n], ALU.mult, nqT + nEq, nQt)
                TT('dve', Kt[:, :, 0:n], kT[:, :, c0:c0 + n], Ek[:, :, 0:n], ALU.mult, nkT + nEk, nKt)
                if smp:
                    TT('dve', Qf[:, :, 0:n], qT[:, :, c0:c0 + n], Eq[:, :, 0:n], ALU.mult, nqT + nEq, nQf)
                for h in range(4):
                    MM(PS[2][0:n, h * 128:h * 128 + n], Kt[:, h, 0:n], Qt[:, h, 0:n], True, True, nKt + nQt, ['B2'])
                a3 = PS[2][0:n, 0:512].rearrange("p (h t) -> p h t", h=4)[:, :, 0:n]
                TT('dve', attm[0:n, :, 0:n], a3, Um[0:n, 0:n].unsqueeze(1).to_broadcast([n, 4, n]), ALU.mult,
                   ['B2', 'cst'], natt)
                p3b = PS[3][:].bitcast(BF16)
                for h in range(4):
                    TR(p3b[0:n, h * 128:(h + 1) * 128], Kt[:, h, 0:n], identb[:], nKt + ['identb'], ['B3'])
                CP('act', Ktok[0:n, :, :].rearrange("p h d -> p (h d)"), p3b[0:n, 0:512], ['B3'], nKtok)
                if smp:
                    MS('dve', PS[4][:, 0:512], 0.0, ['B4'])
                    MS('dve', PS[5][:, 0:512], 0.0, ['B5'])
                for h in range(4):
                    for vc in range(2):
                        c8 = h * 2 + vc
                        bank = 4 + c8 // 4
                        o_ap = PS[bank][:, (c8 % 4) * 128:(c8 % 4) * 128 + n]
                        MM(o_ap, V[0:n, ti, h * 256 + vc * 128:h * 256 + vc * 128 + 128], attm[0:n, h, 0:n], not smp, smp,
                           ['y'] + natt, ['B%d' % bank])
                        if not smp:
                            MM(o_ap, Sgb[:, h, vc * 128:(vc + 1) * 128], Qt[:, h, 0:n], False, True, ['Sgb%d' % h] + nQt, ['B%d' % bank])
                if smp:
                    for s in range(16):
                        sb_ = S0b[s % 2]
                        DMA('sp', sb_[:], sgla_d[s].rearrange("h d v -> d h v"), [], ['S0b%d' % (s % 2)], 'dm_S0b%d' % (s % 2))
                        for h in range(4):
                            for vc in range(2):
                                c8 = h * 2 + vc
                                bank = 4 + c8 // 4
                                MM(PS[bank][:, (c8 % 4) * 128 + s * 4:(c8 % 4) * 128 + s * 4 + 4],
                                   sb_[:, h, vc * 128:(vc + 1) * 128], Qf[:, h, s * 4:s * 4 + 4], False, True,
                                   ['S0b%d' % (s % 2)] + nQf, ['B%d' % bank])
                        TT('pool', Km[0:n, :, :], Ktok[0:n, :, :], Msk[0:n, s:s + 1].unsqueeze(1).to_broadcast([n, 4, 128]),
                           ALU.mult, nKtok + ['cst'], nKm)
                        sbank = 6 + (s % 2)
                        sn_ = Snb[s % 2]
                        for h in range(4):
                            for hv in range(2):
                                pass
                        for h in range(4):
                            bk = 6 + h // 2
                            MM(PS[bk][:, (h % 2) * 256:(h % 2 + 1) * 256], Km[0:n, h, :], V[0:n, ti, h * 256:(h + 1) * 256],
                               True, True, nKm + ['y'], ['B%d' % bk])
                        for h in range(4):
                            bk = 6 + h // 2
                            eb = Eq[:, h, s * 4 + 3:s * 4 + 4]
                            ACT(tmpS[:, h, :], sb_[:, h, :], AF.Copy, ['S0b%d' % (s % 2)] + nEq, ['ah%d' % h], scale=eb)
                            STT('dve', sn_[:, h, :], PS[bk][:, (h % 2) * 256:(h % 2 + 1) * 256], eb, tmpS[:, h, :], ALU.mult, ALU.add,
                                ['B%d' % bk, 'ah%d' % h] + nEq, ['S0b%d' % (s % 2)])
                        DMA('sp', oglas[s].rearrange("h d v -> d h v"), sn_[:], ['S0b%d' % (s % 2)], [], 'dm_S0b%d' % (s % 2))
                else:
                    for h in range(4):
                        bk = 6 + h // 2
                        MM(PS[bk][:, (h % 2) * 256:(h % 2 + 1) * 256], Ktok[0:n, h, :], V[0:n, ti, h * 256:(h + 1) * 256],
                           True, True, nKtok + ['y'], ['B%d' % bk])
                    for h in range(4):
                        bk = 6 + h // 2
                        eb = Eq[:, h, n - 1:n]
                        ACT(tmpS[:, h, :], Sg[:, h, :], AF.Copy, ['Sg%d' % h] + nEq, ['ah%d' % h], scale=eb)
                        STT('dve', Sg[:, h, :], PS[bk][:, (h % 2) * 256:(h % 2 + 1) * 256], eb, tmpS[:, h, :], ALU.mult, ALU.add,
                            ['B%d' % bk, 'ah%d' % h] + nEq, ['Sg%d' % h])
                        CP('act', Sgb[:, h, :], Sg[:, h, :], ['Sg%d' % h], ['Sgb%d' % h])
                for bk in (4, 5):
                    o4 = PS[bk][:, 0:512].rearrange("p (c t) -> p c t", c=4)[:, :, 0:n]
                    CP('act', oT[:, (bk - 4) * 4:(bk - 4) * 4 + 4, 0:n], o4, ['B%d' % bk], noT)
                    ACT(sqo[:, (bk - 4) * 4:(bk - 4) * 4 + 4, 0:n], o4, AF.Square, ['B%d' % bk], nsqo)
                for h in range(4):
                    for vc in range(2):
                        MM(PS[0][:, h * 128:h * 128 + n], onesb[:], sqo[:, h * 2 + vc, 0:n], vc == 0, vc == 1, ['onesb'] + nsqo, ['B0'])
                n3 = PS[0][:, 0:512].rearrange("p (h t) -> p h t", h=4)[:, :, 0:n]
                ACT(ro[:, :, 0:n], n3, AF.Ln, ['B0', 'cns'], nro, bias=eps_ap, scale=1.0 / 256)
                ACT(ro[:, :, 0:n], ro[:, :, 0:n], AF.Exp, nro, nro, scale=-0.5)
                oT4 = oT.rearrange("p (h v) t -> p h v t", v=2)
                og4 = og.rearrange("p (h v) t -> p h v t", v=2)
                for vc in range(2):
                    STT('dve', og4[:, :, vc, 0:n], oT4[:, :, vc, 0:n], gon_ap(vc), ro[:, :, 0:n], ALU.mult, ALU.mult,
                        noT + ['prm'] + nro, nog)
                TT('pool', go[:, :, c0:c0 + n], og[:, :, 0:n], sr[:, :, c0:c0 + n], ALU.mult, nog + SR, XN)
            for i in range(4):
                wt, wres = Wst.get('wout%d' % i)
                for cc in range(2):
                    m = i * 2 + cc
                    par = m % 2
                    Wy = wide((0 + 2 * par, 4 + 2 * par))
                    for ci, (c0, n) in enumerate(cts):
                        for kc in range(KC):
                            MM(Wy[ci][0][:, 0:n], wt[:, kc * 256 + cc * 128:kc * 256 + cc * 128 + 128], go[:, kc, c0:c0 + n],
                               kc == 0, kc == KC - 1, [wres, 'xn%d' % kc], [Wy[ci][1]])
                        CP('act', R_y[:, m, c0:c0 + n], Wy[ci][0][:, 0:n], [Wy[ci][1]], ['y'])
                    stat_sq(m, [Wy[ci][0][:, 0:n] for ci, (c0, n) in enumerate(cts)], [[Wy[ci][1]] for ci in range(len(cts))], b)
                    if m > 0:
                        stat_mm(m - 1, b, (1, 5))
            stat_mm(KC - 1, b, (1, 5))
            postnorm_res(b, 0, 3, 1.0)
            if b == NB - 1:
                DMA('sp', oglap.rearrange("h d v -> d h v"), Sg[:], ['Sg%d' % h for h in range(4)], [], 'st_Sg')

        def s5(b):
            cts = colt(b)
            nc_ = sum(n for _, n in cts)
            prenorm(b, 1, 2)
            u = R_xn
            RH = ['h%d' % j for j in range(FC)]
            nyS = RH[0:16]
            nzb = ['s0', 's1', 's2', 's3']
            yS = R_h[:].bitcast(F32)[:, 0:KC * TBM].rearrange("p (c t) -> p c t", t=TBM)
            zb = R_s[:].rearrange("p a t -> p (a t)").bitcast(BF16)[:, 0:KC * TBM].rearrange("p (c t) -> p c t", t=TBM)
            XT = [R_t[:, 0:1024], R_t[:, 1024:2048]]
            nXT = [['R0', 'R1'], ['R2', 'R3']]
            qv = R_t[:, 2048:4096].bitcast(BF16)
            Q = [qv[:, i * 1024:(i + 1) * 1024] for i in range(4)]
            nQ = [['R%d' % (4 + i)] for i in range(4)]
            Rper = R_t[:, 2048:3072]
            bv = R_t[:, 4096:6144].bitcast(BF16)
            BUb = [[bv[:, (p_ * 2 + ri) * 1024:(p_ * 2 + ri + 1) * 1024] for ri in range(2)] for p_ in range(2)]
            nBUb = [[['R%d' % (8 + p_ * 2 + ri)] for ri in range(2)] for p_ in range(2)]
            tv = R_t[:, 6144:7168].bitcast(BF16)
            T1 = tv[:, 0:1024]; T2 = tv[:, 1024:2048]
            nT1, nT2 = ['R12'], ['R13']
            HTb = [R_h[:, 16 * TBM:16 * TBM + 1024], R_h[:, 18 * TBM:18 * TBM + 1024]]
            nHTb = [['h16', 'h17'], ['h18', 'h19']]
            v3 = lambda a: a.rearrange("p (k j) -> p k j", j=LCH)
            v4 = lambda a: a.rearrange("p (k s j) -> p k s j", s=8, j=4)
            cosf = cosT[:].rearrange("p k j -> p (k j)"); sinf = sinT[:].rearrange("p k j -> p (k j)")
            sinNf = sinN[:].rearrange("p k j -> p (k j)")
            RTf = RT[:].rearrange("p k j -> p (k j)")
            chunks = [(i * LCH, False, 0) for i in range(PB // LCH)] + ([(512, True, 0), (544, True, 1)] if b == 0 else [])
            are = s5s[:, 0, :]; aim = s5s[:, 1, :]
            aLre = s5s[:, 6, :]; aLim = s5s[:, 7, :]
            cL1 = s5s[:, 8, :]; sL1 = s5s[:, 9, :]; c3_ = s5s[:, 10, :]; s3_ = s5s[:, 11, :]

            def stage_bu(ci_):
                c0, smp, half = chunks[ci_]
                par = ci_ % 2
                for ri in range(2):
                    for k in range(32):
                        bank = ri * 2 + k // 16
                        MM(PS[bank][:, (k % 16) * 32:(k % 16 + 1) * 32], WB[ri][:, k, :], u[:, k // 4, c0:c0 + LCH], True, True,
                           ['WB', 'xn%d' % (k // 4)], ['B%d' % bank])
                for ri in range(2):
                    for hh in range(2):
                        CP('act', BUb[par][ri][:, hh * 512:(hh + 1) * 512], PS[ri * 2 + hh][:, 0:512], ['B%d' % (ri * 2 + hh)], nBUb[par][ri])

            def stage_dve(ci_):
                c0, smp, half = chunks[ci_]
                par = ci_ % 2
                br = BUb[par][0]; bi = BUb[par][1]
                nbr = nBUb[par][0]; nbi = nBUb[par][1]
                if smp:
                    def tab(t3):
                        return t3[:, :, 0:4].unsqueeze(2).to_broadcast([128, 32, 8, 4])
                    cs_t, sn_t, snn_t = tab(cosT), tab(sinT), tab(sinN)
                    vv = v4
                else:
                    cs_t, sn_t, snn_t = cosf, sinf, sinNf
                    vv = lambda a: a
                TT('dve', vv(T1), vv(br), cs_t, ALU.mult, nbr + ['cosT'], nT1)
                TT('dve', vv(T2), vv(bi), sn_t, ALU.mult, nbi + ['sinT'], nT2)
                TT('dve', XT[0], T1, T2, ALU.add, nT1 + nT2, nXT[0])
                TT('dve', vv(T1), vv(bi), cs_t, ALU.mult, nbi + ['cosT'], nT1)
                TT('dve', vv(T2), vv(br), sn_t, ALU.mult, nbr + ['sinT'], nT2)
                TT('dve', XT[1], T1, T2, ALU.subtract, nT1 + nT2, nXT[1])
                if smp:
                    hr = H0[:, 0, half * 8:(half + 1) * 8, :].rearrange("p s k -> p k s")
                    hi = H0[:, 1, half * 8:(half + 1) * 8, :].rearrange("p s k -> p k s")
                    sh = [128, 32, 8]
                    ar3 = are.unsqueeze(2).to_broadcast(sh); ai3 = aim.unsqueeze(2).to_broadcast(sh)
                    w_ = lambda i: ah[:, i, :].rearrange("p (k s) -> p k s", s=8)
                    TT('pool', w_(0), hr, ar3, ALU.mult, ['H0', 's5s'], ['ah0'])
                    TT('pool', w_(1), hi, ai3, ALU.mult, ['H0', 's5s'], ['ah1'])
                    TT('pool', w_(0), w_(0), w_(1), ALU.subtract, ['ah0', 'ah1'], ['ah0'])
                    TT('pool', w_(2), hi, ar3, ALU.mult, ['H0', 's5s'], ['ah2'])
                    TT('pool', w_(3), hr, ai3, ALU.mult, ['H0', 's5s'], ['ah3'])
                    TT('pool', w_(2), w_(2), w_(3), ALU.add, ['ah2', 'ah3'], ['ah2'])
                    Ai = v4(XT[0])[:, :, :, 0]; Ci = v4(XT[1])[:, :, :, 0]
                    TT('dve', Ai, Ai, w_(0), ALU.add, nXT[0] + ['ah0'], nXT[0])
                    TT('dve', Ci, Ci, w_(2), ALU.add, nXT[1] + ['ah2'], nXT[1])
                    CP('pool', v4(Rper), RT[:, :, 0:4].unsqueeze(2).to_broadcast([128, 32, 8, 4]), ['RT'], nQ[0] + nQ[1])
                    rsc = Rper; rres = nQ[0] + nQ[1]
                else:
                    Ai = v3(XT[0])[:, :, 0]; Ci = v3(XT[1])[:, :, 0]
                    TT('dve', Ai, Ai, hp[:, 0, :], ALU.add, nXT[0] + ['hp'], nXT[0])
                    TT('dve', Ci, Ci, hp[:, 1, :], ALU.add, nXT[1] + ['hp'], nXT[1])
                    rsc = RTf; rres = ['RT']
                for ri in range(2):
                    P.op('dve', lambda e, o=XT[ri], d0=rsc: e.tensor_tensor_scan(out=o, data0=d0, data1=o, initial=0.0,
                                                                                  op0=ALU.mult, op1=ALU.add),
                         r=rres + nXT[ri], w=nXT[ri])
                    CP('act', HTb[ri], XT[ri], nXT[ri], nHTb[ri])
                if smp:
                    sh = [128, 32, 8]
                    c3b = c3_.unsqueeze(2).to_broadcast(sh); s3b = s3_.unsqueeze(2).to_broadcast(sh)
                    w_ = lambda i: ah[:, i, :].rearrange("p (k s) -> p k s", s=8)
                    hr3 = v4(XT[0])[:, :, :, 3]; hi3 = v4(XT[1])[:, :, :, 3]
                    Hr = Hn[:, 0, half * 8:(half + 1) * 8, :].rearrange("p s k -> p k s")
                    Hi = Hn[:, 1, half * 8:(half + 1) * 8, :].rearrange("p s k -> p k s")
                    TT('pool', w_(0), hr3, c3b, ALU.mult, nXT[0] + ['s5s'], ['ah0'])
                    TT('pool', w_(1), hi3, s3b, ALU.mult, nXT[1] + ['s5s'], ['ah1'])
                    TT('pool', Hr, w_(0), w_(1), ALU.subtract, ['ah0', 'ah1'], ['H0'])
                    TT('pool', w_(2), hi3, c3b, ALU.mult, nXT[1] + ['s5s'], ['ah2'])
                    TT('pool', w_(3), hr3, s3b, ALU.mult, nXT[0] + ['s5s'], ['ah3'])
                    TT('pool', Hi, w_(2), w_(3), ALU.add, ['ah2', 'ah3'], ['H0'])
                else:
                    w_ = lambda i: ah[:, i, 0:32]
                    hrl = v3(XT[0])[:, :, LCH - 1]; hil = v3(XT[1])[:, :, LCH - 1]
                    last = (b == NB - 1 and ci_ == PB // LCH - 1)
                    cre, cim = (cL1, sL1) if last else (aLre, aLim)
                    TT('pool', w_(0), hrl, cre, ALU.mult, nXT[0] + ['s5s'], ['ah0'])
                    TT('pool', w_(1), hil, cim, ALU.mult, nXT[1] + ['s5s'], ['ah1'])
                    TT('pool', hp[:, 0, :], w_(0), w_(1), ALU.subtract, ['ah0', 'ah1'], ['hp'])
                    TT('pool', w_(2), hil, cre, ALU.mult, nXT[1] + ['s5s'], ['ah2'])
                    TT('pool', w_(3), hrl, cim, ALU.mult, nXT[0] + ['s5s'], ['ah3'])
                    TT('pool', hp[:, 1, :], w_(2), w_(3), ALU.add, ['ah2', 'ah3'], ['hp'])
                TT('dve', vv(Q[0]), vv(HTb[0]), cs_t, ALU.mult, nHTb[0] + ['cosT'], nQ[0])
                TT('dve', vv(Q[3]), vv(HTb[0]), sn_t, ALU.mult, nHTb[0] + ['sinT'], nQ[3])
                TT('dve', vv(Q[1]), vv(HTb[1]), snn_t, ALU.mult, nHTb[1] + ['sinN'], nQ[1])
                TT('dve', vv(Q[2]), vv(HTb[1]), cs_t, ALU.mult, nHTb[1] + ['cosT'], nQ[2])

            def stage_y(ci_):
                c0, smp, half = chunks[ci_]
                ybank = 4 + ci_ % 2
                for kc in range(KC):
                    for q in range(4):
                        k = kc * 4 + q
                        for qi, wc in ((0, 0), (1, 0), (2, 1), (3, 1)):
                            MM(PS[ybank][:, kc * 32:(kc + 1) * 32], WC[wc][:, k, :], v3(Q[qi])[:, k, :],
                               q == 0 and qi == 0, q == 3 and qi == 3, ['WC'] + nQ[qi], ['B%d' % ybank])
                CP('act', yS[:, :, c0:c0 + LCH], PS[ybank][:, 0:256].rearrange("p (c j) -> p c j", j=LCH), ['B%d' % ybank], nyS)

            stage_bu(0)
            for ci_ in range(len(chunks)):
                if ci_ + 1 < len(chunks):
                    stage_bu(ci_ + 1)
                stage_dve(ci_)
                stage_y(ci_)
            for kc in range(KC):
                s = kc % 2
                STT('dve', tmpn[:, s, 0:nc_], x[:, kc, 0:nc_], g_ap(1, 2, kc), rstd[:, 0:nc_], ALU.mult, ALU.mult,
                    ['x%d' % kc, 'prm', 'rstd'], ['tmpn%d' % s])
                STT('dve', zb[:, kc, 0:nc_], tmpn[:, s, 0:nc_], d_ap(kc), yS[:, kc, 0:nc_], ALU.mult, ALU.add,
                    ['tmpn%d' % s, 'prm'] + nyS, nzb)
            sgb = tmpn
            for m in range(KC):
                wt, wres = Wst.get('wglu%d' % m)
                par = m % 2
                Wv = wide((0 + 2 * par, 4 + 2 * par)); Wg = wide((1 + 2 * par, 5 + 2 * par))
                for half_, Wd in ((0, Wv), (1, Wg)):
                    for ci, (c0, n) in enumerate(cts):
                        for kc in range(KC):
                            MM(Wd[ci][0][:, 0:n], wt[:, kc * 256 + half_ * 128:kc * 256 + half_ * 128 + 128], zb[:, kc, c0:c0 + n],
                               kc == 0, kc == KC - 1, [wres] + nzb, [Wd[ci][1]])
                for ci, (c0, n) in enumerate(cts):
                    ACT(sgb[:, par, c0:c0 + n], Wg[ci][0][:, 0:n], AF.Sigmoid, [Wg[ci][1], 'prm'], ['tmpn%d' % par], bias=bglu_ap(8 + m))
                    STT('dve', R_y[:, m, c0:c0 + n], Wv[ci][0][:, 0:n], bglu_ap(m), sgb[:, par, c0:c0 + n], ALU.add, ALU.mult,
                        [Wv[ci][1], 'prm', 'tmpn%d' % par], ['y'])
            postnorm_res(b, 1, 3, 1.0, banks=(6, 7), stats_done=False)
            if b == NB - 1:
                for ri in range(2):
                    TR(PS[0][0:32, ri * 128:(ri + 1) * 128], hp[:, ri, :], identf, ['hp', 'cst'], ['B0'])
                CP('dve', R_t[0:32, 0:256], PS[0][0:32, 0:256], ['B0'], ['R0'])
                DMA('sp', o5rep, R_t[0:32, 0:128], ['R0'], [], 'st_h1')
                DMA('sp', o5imp, R_t[0:32, 128:256], ['R0'], [], 'st_h2')
            if b == 0:
                for ri, dst in enumerate((o5res, o5ims)):
                    for j in range(4):
                        TR(PS[1 + ri][:, j * 128:(j + 1) * 128], Hn[:, ri, j * 4:(j + 1) * 4, :].rearrange("p s k -> p (s k)"), identf,
                           ['H0', 'cst'], ['B%d' % (1 + ri)])
                    buf = R_t[:, 1024 * (1 + ri):1024 * (1 + ri) + 512]
                    nb_ = ['R%d' % (2 * (1 + ri))]
                    CP('dve', buf, PS[1 + ri][:, 0:512], ['B%d' % (1 + ri)], nb_)
                    for j in range(4):
                        DMA('sp', dst[j * 128:(j + 1) * 128, :], buf[:, j * 128:(j + 1) * 128], nb_, [], 'st_hs%d' % ri)

        xres_all = ['x']
        for b in range(NB):
            DMA('sp', x[:, :, 0:512], xTp.rearrange("(kc p) t -> p kc t", p=128)[:, :, b * 512:(b + 1) * 512], [], XR, 'ld_x')
            if b == 0:
                DMA('sp', x[:, :, 512:576], xTs.rearrange("(kc p) t -> p kc t", p=128), [], XR, 'ld_xs')
            stages = [lambda: ffn(b, 0, 0), lambda: gla(b), lambda: ffn(b, 0, 1),
                      lambda: ffn(b, 1, 0), lambda: s5(b), lambda: ffn(b, 1, 1)]
            for si, st in enumerate(stages):
                if stop_after is not None and si > stop_after:
                    continue
                st()
            if stop_after is not None:
                Wst.n = (b + 1) * len(tiles)
                Wst.issued = max(Wst.issued, Wst.n)
            DMA('sp', yTp.rearrange("(kc p) t -> p kc t", p=128)[:, :, b * 512:(b + 1) * 512], x[:, :, 0:512], XR, [], 'st_x')
            if b == 0:
                DMA('sp', yTs.rearrange("(kc p) t -> p kc t", p=128), x[:, :, 512:576], XR, [], 'st_xs')
        P.wait_all('sp')

        sems = {e: es.enter_context(nc.semaphore("s_" + e)) for e in ENGS}
        dsem = {k: es.enter_context(nc.semaphore("d_" + k)) for k in P.dma_cnt}
        block = es.enter_context(nc.Block())
        P.replay(block, sems, dsem)
    return nc


def _consts():
    c = np.zeros((128, 512), np.float32)
    c[:, 0:128] = np.eye(128, dtype=np.float32)
    s = np.arange(128)
    c[:, 128:256] = (s[:, None] <= s[None, :]).astype(np.float32)
    same = (s[:, None] // 4) == (s[None, :] // 4)
    c[:, 256:384] = ((s[:, None] <= s[None, :]) & same).astype(np.float32)
    c[:, 384:384 + LCH] = np.arange(LCH, dtype=np.float32)[None, :]
    c[:, 416:432] = ((s[:, None] // 4) == np.arange(16)[None, :]).astype(np.float32)
    return c


def prepare_inputs(x_prompt, x_sample, state_gla, state_s5_re, state_s5_im, norm_g, w_ffn_gu, w_ffn_down,
                   gla_w_in, gla_w_g2, gla_b_g, gla_g_onorm, gla_w_out,
                   s5_lam_re, s5_lam_im, s5_log_dt, s5_b_re, s5_b_im, s5_c_re, s5_c_im, s5_d, s5_w_glu, s5_b_glu):
    f = lambda a: np.asarray(a, dtype=np.float32)
    WSa = build_wstream(f(w_ffn_gu), f(w_ffn_down), f(gla_w_in), f(gla_w_out), f(s5_w_glu))
    prm = np.zeros((128, 128), np.float32)
    prm[:, 0:96] = f(norm_g).reshape(2, 6, KC, 128).transpose(3, 0, 1, 2).reshape(128, 96)
    prm[:, 96:98] = f(gla_g_onorm)[0].reshape(2, 128).T
    prm[:, 98:106] = f(s5_d)[0].reshape(KC, 128).T
    prm[:, 106:122] = f(s5_b_glu)[0].reshape(16, 128).T
    wg2a = np.concatenate([f(gla_w_g2)[0], f(gla_b_g)[0][None, :]], axis=0)
    cst = _consts()
    s5p = np.concatenate([f(s5_lam_re)[0].reshape(32, 128), f(s5_lam_im)[0].reshape(32, 128),
                          np.repeat(f(s5_log_dt)[0].reshape(32, 2), 64, axis=1)], axis=1)
    bpad = np.zeros((2, 128, 32, 128), np.float32)
    cpad = np.zeros((2, 128, 32, 128), np.float32)
    for ri, (bb, cc) in enumerate(((f(s5_b_re)[0], f(s5_c_re)[0]), (f(s5_b_im)[0], f(s5_c_im)[0]))):
        for k in range(32):
            for g2 in range(2):
                g = 2 * k + g2
                col = 32 * (k % 4) + 16 * g2
                bpad[ri, g2 * 64:(g2 + 1) * 64, k, col:col + 16] = bb[g]
                cpad[ri, g2 * 64:(g2 + 1) * 64, k, col:col + 16] = cc[g].T
    bpad = bpad.reshape(2, 128, 4096); cpad = cpad.reshape(2, 128, 4096)
    in_maps = []
    for c in range(NCORE):
        m = {
            "xTp": np.ascontiguousarray(f(x_prompt)[c].T),
            "xTs": np.ascontiguousarray(f(x_sample)[16 * c:16 * c + 16].reshape(NS, D).T),
            "WS": WSa, "prm": prm, "wg2a": wg2a, "cst": cst,
            "sgla": np.ascontiguousarray(f(state_gla)[0, 16 * c:16 * c + 16]),
            "s5re": np.ascontiguousarray(f(state_s5_re)[0, 16 * c:16 * c + 16].reshape(512, 128)),
            "s5im": np.ascontiguousarray(f(state_s5_im)[0, 16 * c:16 * c + 16].reshape(512, 128)),
            "s5p": np.ascontiguousarray(s5p), "bpad": bpad, "cpad": cpad,
        }
        in_maps.append(m)
    return in_maps


def assemble(results):
    y_p = np.stack([r["yTp"].T for r in results]).astype(np.float32)
    y_s = np.concatenate([r["yTs"].T.reshape(16, 4, D) for r in results]).astype(np.float32)
    gla_p = np.stack([r["oglap"] for r in results])[None].astype(np.float32)
    re_p = np.stack([r["o5rep"].reshape(64, 64) for r in results])[None].astype(np.float32)
    im_p = np.stack([r["o5imp"].reshape(64, 64) for r in results])[None].astype(np.float32)
    gla_s = np.concatenate([r["oglas"] for r in results])[None].astype(np.float32)
    re_s = np.concatenate([r["o5res"].reshape(16, 64, 64) for r in results])[None].astype(np.float32)
    im_s = np.concatenate([r["o5ims"].reshape(16, 64, 64) for r in results])[None].astype(np.float32)
    return (y_p, y_s, gla_p, re_p, im_p, gla_s, re_s, im_s)


_NC_CACHE = {}


def kernel(**inputs):
    in_maps = prepare_inputs(**inputs)
    if 'nc' not in _NC_CACHE:
        _NC_CACHE['nc'] = build_nc()
    res = run_bass_kernel_spmd(_NC_CACHE['nc'], in_maps, core_ids=list(range(NCORE)))
    return assemble(res.results)
```

```python
import numpy as np
from contextlib import ExitStack
import concourse.bass as bass
import concourse.mybir as mybir
from concourse.bass_utils import run_bass_kernel_spmd

F32 = mybir.dt.float32
BF16 = mybir.dt.bfloat16
AF = mybir.ActivationFunctionType
ALU = mybir.AluOpType
ENGS = ['pe', 'act', 'dve', 'pool', 'sp']

NCORE = 8
D = 1024; KC = 8; DFF = 2816; FC = 22
NB = 4; PB = 512; NS = 64; TBM = 576
EPS = 1e-6
LCH = 32
MAGIC = 12582912.0
TWO_PI = float(2 * np.pi)


class Prog:
    def __init__(self):
        self.ops = {e: [] for e in ENGS}
        self.last_w = {}
        self.reads = {}
        self.seen = {e: {} for e in ENGS}
        self.dma_cnt = {}
        self.flag = {e: set() for e in ENGS}

    def _need(self, eng, tok, waits, is_dma, raw):
        if tok is None:
            return
        if tok[0] == 'eng':
            _, X, idx = tok
            if X == eng and not is_dma:
                if eng == 'pe' or not raw:
                    return
            key = ('eng', X)
            if self.seen[eng].get(key, -1) >= idx:
                return
            self.seen[eng][key] = idx
            self.flag[X].add(idx)
            waits.append(tok)
        else:
            _, k, val = tok
            key = ('dma', k)
            if self.seen[eng].get(key, -1) >= val:
                return
            self.seen[eng][key] = val
            waits.append(tok)

    def op(self, eng, fn, r=(), w=(), dma=None):
        idx = len(self.ops[eng])
        is_dma = dma is not None
        if is_dma:
            self.dma_cnt[dma] = self.dma_cnt.get(dma, 0) + 1
            tok = ('dma', dma, self.dma_cnt[dma] * 16)
        else:
            tok = ('eng', eng, idx)
        waits = []
        cand = {}

        def add(t, raw):
            if t is None:
                return
            if t[0] == 'eng':
                if t[1] == eng and not is_dma and (eng == 'pe' or not raw):
                    return
                k = ('eng', t[1])
            else:
                k = ('dma', t[1])
            if k not in cand or cand[k][2] < t[2]:
                cand[k] = t
        for res in r:
            add(self.last_w.get(res), True)
            if res[0] == 'B' and res[1:].isdigit():
                for t in self.reads.get(res, ()):
                    if t[0] == 'eng' and t[1] != eng:
                        add(t, False)
        for res in w:
            add(self.last_w.get(res), False)
            for t in self.reads.get(res, ()):
                add(t, False)
        for t in cand.values():
            self._need(eng, t, waits, is_dma, True)
        for res in r:
            self.reads.setdefault(res, []).append(tok)
        for res in w:
            self.last_w[res] = tok
            self.reads[res] = []
        self.ops[eng].append(dict(fn=fn, waits=waits, dma=dma))
        return tok

    def wait_all(self, eng):
        waits = []
        for k, c in self.dma_cnt.items():
            self._need(eng, ('dma', k, c * 16), waits, True, True)
        for X in ENGS:
            if X != eng:
                for idx in range(len(self.ops[X]) - 1, -1, -1):
                    if self.ops[X][idx]['dma'] is None and self.ops[X][idx]['fn'] is not None:
                        self._need(eng, ('eng', X, idx), waits, True, True)
                        break
        self.ops[eng].append(dict(fn=None, waits=waits, dma=None))

    def replay(self, block, sems, dma_sems):
        rank = {}
        for e in ENGS:
            rank[e] = {idx: i + 1 for i, idx in enumerate(sorted(self.flag[e]))}
        ops = self.ops
        flag = self.flag

        def run(name, e):
            for idx, o in enumerate(ops[name]):
                for t in o['waits']:
                    if t[0] == 'eng':
                        e.wait_ge(sems[t[1]], rank[t[1]][t[2]])
                    else:
                        e.wait_ge(dma_sems[t[1]], t[2])
                if o['fn'] is None:
                    continue
                ins = o['fn'](e)
                if o['dma'] is not None:
                    ins.then_inc(dma_sems[o['dma']], 16)
                elif idx in flag[name]:
                    ins.then_inc(sems[name], 1)

        block.tensor(lambda e: run('pe', e))
        block.scalar(lambda e: run('act', e))
        block.vector(lambda e: run('dve', e))
        block.gpsimd(lambda e: run('pool', e))
        block.sync(lambda e: run('sp', e))


def weight_tiles():
    t = []
    for l in range(2):
        for f in range(2):
            if f == 1:
                if l == 0:
                    t += [('win%d' % i, 2048) for i in range(8)]
                    t += [('wv%d' % i, 2048) for i in range(4)]
                    t += [('wglr', 128)]
                    t += [('wout%d' % i, 2048) for i in range(4)]
                else:
                    t += [('wglu%d' % i, 2048) for i in range(8)]
            t += [('gu%d_%d_%d' % (l, f, j), 2048) for j in range(FC)]
            t += [('dn%d_%d_%d' % (l, f, m), 2816) for m in range(KC)]
    return t


def build_wstream(w_ffn_gu, w_ffn_down, gla_w_in, gla_w_out, s5_w_glu):
    def kmaj(w):
        C = w.shape[1]
        return w.reshape(KC, 128, C).transpose(1, 0, 2).reshape(128, KC * C)
    parts = []
    win = gla_w_in[0]
    for l in range(2):
        for f in range(2):
            if f == 1:
                if l == 0:
                    fm = np.concatenate([win[:, 0:1024], win[:, 2048:3072]], axis=1)
                    for i in range(8):
                        parts.append(kmaj(fm[:, i * 256:(i + 1) * 256]))
                    for i in range(4):
                        parts.append(kmaj(win[:, 1024 + i * 256:1024 + (i + 1) * 256]))
                    parts.append(kmaj(win[:, 3072:3088]))
                    for i in range(4):
                        parts.append(kmaj(gla_w_out[0][:, i * 256:(i + 1) * 256]))
                else:
                    wg = s5_w_glu[0]
                    for m in range(8):
                        parts.append(kmaj(np.concatenate([wg[:, m * 128:(m + 1) * 128],
                                                          wg[:, 1024 + m * 128:1024 + (m + 1) * 128]], axis=1)))
            gu = w_ffn_gu[l, f]
            for j in range(FC):
                parts.append(kmaj(np.concatenate([gu[:, j * 128:(j + 1) * 128],
                                                  gu[:, DFF + j * 128:DFF + (j + 1) * 128]], axis=1)))
            dn = w_ffn_down[l, f]
            for m in range(KC):
                parts.append(dn[:, m * 128:(m + 1) * 128].reshape(FC, 128, 128).transpose(1, 0, 2).reshape(128, FC * 128))
    return np.ascontiguousarray(np.concatenate(parts, axis=1), dtype=np.float32)


def build_nc(stop_after=None):
    nc = bass.Bass("TRN2", target_bir_lowering=False)
    tiles = weight_tiles()
    offs = np.cumsum([0] + [f for _, f in tiles]).tolist()
    TOT = offs[-1]

    def din(name, shape):
        return nc.dram_tensor(name, shape, F32, kind="ExternalInput").ap()

    def dout(name, shape):
        return nc.dram_tensor(name, shape, F32, kind="ExternalOutput").ap()

    xTp = din("xTp", [D, 2048]); xTs = din("xTs", [D, NS])
    WS = din("WS", [128, TOT])
    prm_d = din("prm", [128, 128]); wg2a_d = din("wg2a", [17, 512])
    cst_d = din("cst", [128, 512])
    sgla_d = din("sgla", [16, 4, 128, 256])
    s5re_d = din("s5re", [512, 128]); s5im_d = din("s5im", [512, 128])
    s5p_d = din("s5p", [32, 384])
    bpad_d = din("bpad", [2, 128, 4096]); cpad_d = din("cpad", [2, 128, 4096])
    yTp = dout("yTp", [D, 2048]); yTs = dout("yTs", [D, NS])
    oglap = dout("oglap", [4, 128, 256]); oglas = dout("oglas", [16, 4, 128, 256])
    o5rep = dout("o5rep", [32, 128]); o5imp = dout("o5imp", [32, 128])
    o5res = dout("o5res", [512, 128]); o5ims = dout("o5ims", [512, 128])

    P = Prog()
    es = ExitStack()
    with es:
        def sb(name, shape, dt=F32):
            return es.enter_context(nc.sbuf_tensor(name, shape, dt))

        x = sb("x", [128, KC, TBM])
        prm = sb("prm_s", [128, 128]); wg2a = sb("wg2a_s", [17, 512]); cst = sb("cst_s", [128, 512])
        identb = sb("identb", [128, 128], BF16); onesb = sb("onesb", [128, 128], BF16)
        cns = sb("cns", [128, 4])
        Sg = sb("Sg", [128, 4, 256]); Sgb = sb("Sgb", [128, 4, 256], BF16)
        WB = [sb("WBre", [128, 32, 128], BF16), sb("WBim", [128, 32, 128], BF16)]
        WC = [sb("WCre", [128, 32, 128], BF16), sb("WCim", [128, 32, 128], BF16)]
        cosT = sb("cosT", [128, 32, LCH], BF16); sinT = sb("sinT", [128, 32, LCH], BF16); sinN = sb("sinN", [128, 32, LCH], BF16)
        RT = sb("RT", [128, 32, LCH])
        s5s = sb("s5s", [128, 12, 32])
        hp = sb("hp", [128, 2, 32])
        ah = sb("ah", [128, 4, 256])
        H0 = sb("H0", [128, 2, 16, 32]); Hn = H0
        NSL = 3
        wsl = [sb("wsl%d" % i, [128, 2816], BF16) for i in range(NSL)]
        R_xn = sb("R_xn", [128, KC, TBM], BF16)
        R_h = sb("R_h", [128, FC * TBM], BF16)
        R_y = sb("R_y", [128, KC, TBM])
        R_s = sb("R_s", [128, 4, TBM])
        sq = sb("sq", [128, 2, TBM], BF16)
        rstd = sb("rstd", [128, TBM]); rsq = sb("rsq", [128, TBM]); tmpn = sb("tmpn", [128, 2, TBM])
        R_t = sb("R_t", [128, 7168])
        S0b = [sb("S0b%d" % i, [128, 4, 256]) for i in range(2)]
        Snb = S0b
        RTALL = ["R%d" % i for i in range(14)]
        def seg(a, b_):
            return ["R%d" % i for i in range(a // 512, (b_ + 511) // 512)]
        PS = [es.enter_context(nc.psum_tensor("B%d" % i, [128, 512], F32)) for i in range(8)]

        identf = cst[:, 0:128]; U = cst[:, 128:256]; Us = cst[:, 256:384]
        iotaj = cst[:, 384:384 + LCH]; Msk = cst[:, 416:432]
        g_ap = lambda l, n, kc: prm[:, (l * 6 + n) * 8 + kc:(l * 6 + n) * 8 + kc + 1]
        gon_ap = lambda vc: prm[:, 96 + vc:97 + vc]
        d_ap = lambda kc: prm[:, 98 + kc:99 + kc]
        bglu_ap = lambda i: prm[:, 106 + i:107 + i]

        def MM(out, lhsT, rhs, start, stop, r, w):
            P.op('pe', lambda e: e.matmul(out, lhsT=lhsT, rhs=rhs, start=start, stop=stop), r=r, w=w)

        def TR(out, in_, ident, r, w):
            P.op('pe', lambda e: e.transpose(out, in_, ident), r=r, w=w)

        def ACT(out, in_, func, r, w, bias=None, scale=None):
            kw = {}
            if bias is not None: kw['bias'] = bias
            if scale is not None: kw['scale'] = scale
            P.op('act', lambda e: e.activation(out=out, in_=in_, func=func, **kw), r=r, w=w)

        def STT(eng, out, in0, scalar, in1, op0, op1, r, w):
            eng = 'dve'
            P.op(eng, lambda e: e.scalar_tensor_tensor(out=out, in0=in0, scalar=scalar, in1=in1, op0=op0, op1=op1), r=r, w=w)

        def TT(eng, out, in0, in1, op, r, w):
            P.op(eng, lambda e: e.tensor_tensor(out=out, in0=in0, in1=in1, op=op), r=r, w=w)

        def TS(eng, out, in0, s1, s2, op0, op1, r, w):
            if s2 is None:
                P.op(eng, lambda e: e.tensor_single_scalar(out=out, in_=in0, scalar=s1, op=op0), r=r, w=w)
            else:
                P.op(eng, lambda e: e.tensor_scalar(out=out, in0=in0, scalar1=s1, scalar2=s2, op0=op0, op1=op1), r=r, w=w)

        def CP(eng, out, in_, r, w):
            if eng == 'act':
                P.op('act', lambda e: e.copy(out=out, in_=in_), r=r, w=w)
            else:
                P.op(eng, lambda e: e.tensor_copy(out=out, in_=in_), r=r, w=w)

        def MS(eng, ap, val, w):
            P.op(eng, lambda e: e.memset(ap, val), w=w)

        def DMA(eng, out, in_, r, w, key):
            P.op(eng, lambda e: e.dma_start(out=out, in_=in_), r=r, w=w, dma=key)

        def RECIP(out, in_, r, w):
            P.op('dve', lambda e: e.reciprocal(out=out, in_=in_), r=r, w=w)

        class WStream:
            def __init__(self):
                self.n = 0
                self.issued = 0
                self.total = NB * len(tiles)

            def _issue(self, i):
                ti = i % len(tiles)
                F = tiles[ti][1]
                s = i % NSL
                DMA('pool', wsl[s][:, 0:F], WS[:, offs[ti]:offs[ti] + F], r=[], w=['ws%d' % s], key='ws%d' % s)

            def get(self, name):
                i = self.n
                assert tiles[i % len(tiles)][0] == name, (tiles[i % len(tiles)][0], name)
                while self.issued < min(i + NSL, self.total):
                    self._issue(self.issued)
                    self.issued += 1
                self.n += 1
                s = i % NSL
                return wsl[s], 'ws%d' % s

        Wst = WStream()

        DMA('sp', prm[:], prm_d, [], ['prm'], 'ld_prm')
        DMA('sp', wg2a[:], wg2a_d, [], ['wg2a'], 'ld_wg2a')
        DMA('sp', cst[:], cst_d, [], ['cst'], 'ld_cst')
        CP('dve', identb[:], identf, ['cst'], ['identb'])
        MS('dve', onesb[:], 1.0, ['onesb'])
        MS('dve', cns[:, 0:1], EPS, ['cns'])
        MS('dve', cns[:, 1:2], 4 * EPS, ['cns'])
        XR = ['x%d' % k for k in range(KC)]
        MS('dve', Sg[:], 0.0, ['Sg%d' % h for h in range(4)])
        MS('dve', Sgb[:], 0.0, ['Sgb%d' % h for h in range(4)])
        MS('dve', hp[:], 0.0, ['hp'])
        eps_ap = cns[:, 0:1]

        def colt(b):
            return [(0, 512)] + ([(512, 64)] if b == 0 else [])

        def wide(banks):
            main, aux = banks
            return [(PS[main][:, 0:512], 'B%d' % main), (PS[aux][:, 0:64], 'B%d' % aux)]

        def s5_setup():
            sp = R_t
            lam_re = sp[0:32, 0:128]; lam_im = sp[0:32, 128:256]; ldt = sp[0:32, 256:384]
            DMA('sp', sp[0:32, 0:384], s5p_d, [], RTALL, 'ld_s5p')
            t = lambda i: sp[0:32, 384 + i * 128:384 + (i + 1) * 128]
            dt_, mag, ang, den, are, aim, fre, fim, tA, tB, sn, cs_ = [t(i) for i in range(12)]
            aLre, aLim, cL1, sL1, c3_, s3_, angx = [t(12 + i) for i in range(7)]
            rw = dict(r=RTALL, w=RTALL)
            ACT(dt_, ldt, AF.Exp, **rw)
            TT('dve', den, lam_re, dt_, ALU.mult, **rw)
            TS('dve', mag, den, 1.0 / 7, 1.0, ALU.mult, ALU.add, **rw)
            for c_ in (6, 5, 4, 3, 2, 1):
                TT('dve', mag, mag, den, ALU.mult, **rw)
                TS('dve', mag, mag, 1.0 / c_ if c_ > 1 else 1.0, 1.0, ALU.mult, ALU.add, **rw)
            TT('dve', ang, lam_im, dt_, ALU.mult, **rw)

            def sincos(dst, src, shift):
                TS('dve', tA, src, float(1 / TWO_PI), float(shift / TWO_PI), ALU.mult, ALU.add, **rw)
                TS('dve', tA, tA, MAGIC, None, ALU.add, None, **rw)
                TS('dve', tA, tA, -MAGIC, -TWO_PI, ALU.add, ALU.mult, **rw)
                STT('dve', tB, src, 1.0, tA, ALU.mult, ALU.add, **rw)
                if shift != 0.0:
                    TS('dve', tB, tB, float(shift), None, ALU.add, None, **rw)
                ACT(dst, tB, AF.Sin, **rw)
            sincos(sn, ang, 0.0)
            sincos(cs_, ang, float(np.pi / 2))
            TT('dve', are, mag, cs_, ALU.mult, **rw)
            TT('dve', aim, mag, sn, ALU.mult, **rw)
            TT('dve', den, lam_re, lam_re, ALU.mult, **rw)
            TT('dve', tA, lam_im, lam_im, ALU.mult, **rw)
            TT('dve', den, den, tA, ALU.add, **rw)
            RECIP(den, den, **rw)
            TS('dve', tB, are, -1.0, None, ALU.add, None, **rw)
            TT('dve', fre, tB, lam_re, ALU.mult, **rw)
            TT('dve', tA, aim, lam_im, ALU.mult, **rw)
            TT('dve', fre, fre, tA, ALU.add, **rw)
            TT('dve', fre, fre, den, ALU.mult, **rw)
            TT('dve', fim, aim, lam_re, ALU.mult, **rw)
            TT('dve', tA, tB, lam_im, ALU.mult, **rw)
            TT('dve', fim, fim, tA, ALU.subtract, **rw)
            TT('dve', fim, fim, den, ALU.mult, **rw)
            for mult_, (sdst, cdst) in ((float(LCH), (aLim, aLre)), (float(LCH - 1), (sL1, cL1)), (3.0, (s3_, c3_))):
                TS('dve', angx, ang, mult_, None, ALU.mult, None, **rw)
                sincos(sdst, angx, 0.0)
                sincos(cdst, angx, float(np.pi / 2))
            TT('dve', aLre, aLre, mag, ALU.mult, **rw)
            TT('dve', aLim, aLim, mag, ALU.mult, **rw)
            for i, src in enumerate([are, aim, fre, fim, ang, mag, aLre, aLim, cL1, sL1, c3_, s3_]):
                TR(PS[0][:, i * 32:(i + 1) * 32], src, identf[0:32, 0:32], RTALL + ['cst'], ['B0'])
            CP('dve', s5s[:].rearrange("p a k -> p (a k)"), PS[0][:, 0:384], ['B0'], ['s5s'])
            th = s5s[:, 4, :]; rr_ = s5s[:, 5, :]
            A3 = R_t[:, 0:1024].rearrange("p (k j) -> p k j", j=LCH)
            B3 = R_t[:, 1024:2048].rearrange("p (k j) -> p k j", j=LCH)
            C3 = R_t[:, 2048:3072].rearrange("p (k j) -> p k j", j=LCH)
            io3 = iotaj.unsqueeze(1).to_broadcast([128, 32, LCH])
            th3 = th.unsqueeze(2).to_broadcast([128, 32, LCH])
            TT('dve', A3, io3, th3, ALU.mult, ['cst', 's5s'] + RTALL, RTALL)
            for dst, shift in ((sinT, 0.0), (cosT, float(np.pi / 2))):
                TS('dve', B3, A3, float(1 / TWO_PI), float(shift / TWO_PI), ALU.mult, ALU.add, **rw)
                TS('dve', B3, B3, MAGIC, None, ALU.add, None, **rw)
                TS('dve', B3, B3, -MAGIC, -TWO_PI, ALU.add, ALU.mult, **rw)
                TT('dve', C3, A3, B3, ALU.add, **rw)
                if shift != 0.0:
                    TS('dve', C3, C3, float(shift), None, ALU.add, None, **rw)
                ACT(dst[:], C3, AF.Sin, RTALL, [dst is sinT and 'sinT' or 'cosT'])
                if dst is sinT:
                    ACT(sinN[:], C3, AF.Sin, RTALL, ['sinN'], scale=-1.0)
            r3 = rr_.unsqueeze(2).to_broadcast([128, 32, LCH])
            CP('dve', RT[:], r3, ['s5s'], ['RT'])
            MS('dve', RT[:, :, 0:1], 0.0, ['RT'])
            bre = R_t[:, 0:1024]; bim = R_t[:, 1024:2048]; o1 = R_t[:, 2048:3072]; o2 = R_t[:, 3072:4096]
            for kg in range(4):
                DMA('sp', bre, bpad_d[0, :, kg * 1024:(kg + 1) * 1024], [], RTALL, 'ld_b0')
                DMA('sp', bim, bpad_d[1, :, kg * 1024:(kg + 1) * 1024], [], RTALL, 'ld_b1')
                v3 = lambda a: a.rearrange("p (k c) -> p k c", c=128)
                f_re3 = s5s[:, 2, kg * 8:(kg + 1) * 8].unsqueeze(2).to_broadcast([128, 8, 128])
                f_im3 = s5s[:, 3, kg * 8:(kg + 1) * 8].unsqueeze(2).to_broadcast([128, 8, 128])
                rs = RTALL + ['s5s']
                TT('dve', v3(o1), v3(bre), f_re3, ALU.mult, rs, RTALL)
                TT('dve', v3(o2), v3(bim), f_im3, ALU.mult, rs, RTALL)
                TT('dve', o1, o1, o2, ALU.subtract, rs, RTALL)
                TT('dve', v3(o2), v3(bre), f_im3, ALU.mult, rs, RTALL)
                TT('dve', v3(bre), v3(bim), f_re3, ALU.mult, rs, RTALL)
                TT('dve', o2, o2, bre, ALU.add, rs, RTALL)
                for ri, src in enumerate((o1, o2)):
                    for kk in range(8):
                        bank = PS[1 + (kk // 4) + 2 * ri]
                        TR(bank[:, (kk % 4) * 128:(kk % 4 + 1) * 128], src[:, kk * 128:(kk + 1) * 128], identf,
                           RTALL + ['cst'], ['B%d' % (1 + (kk // 4) + 2 * ri)])
                    for hh in range(2):
                        bi = 1 + hh + 2 * ri
                        CP('act', WB[ri][:, kg * 8 + hh * 4:kg * 8 + hh * 4 + 4, :].rearrange("p k c -> p (k c)"),
                           PS[bi][:, 0:512], ['B%d' % bi], ['WB'])
            for kg in range(4):
                DMA('sp', bre, cpad_d[0, :, kg * 1024:(kg + 1) * 1024], [], RTALL, 'ld_b0')
                DMA('sp', bim, cpad_d[1, :, kg * 1024:(kg + 1) * 1024], [], RTALL, 'ld_b1')
                CP('act', WC[0][:, kg * 8:(kg + 1) * 8, :].rearrange("p k c -> p (k c)"), bre, RTALL, ['WC'])
                P.op('act', lambda e, kg=kg: e.mul(out=WC[1][:, kg * 8:(kg + 1) * 8, :].rearrange("p k c -> p (k c)"),
                                                    in_=bim, mul=-1.0), r=RTALL, w=['WC'])
            for ri, src in enumerate((s5re_d, s5im_d)):
                for j in range(4):
                    buf = R_t[:, 4096 + j * 128:4096 + (j + 1) * 128]
                    DMA('sp', buf, src[j * 128:(j + 1) * 128, :], [], RTALL, 'ld_h0_%d' % j)
                    TR(PS[5 + ri][:, j * 128:(j + 1) * 128], buf, identf, RTALL + ['cst'], ['B%d' % (5 + ri)])
                CP('dve', H0[:, ri, :, :].rearrange("p s k -> p (s k)"), PS[5 + ri][:, 0:512], ['B%d' % (5 + ri)], ['H0'])

        s5_setup()

        def RECIPF(out, in_, r, w):
            P.op('dve', lambda e: e.reciprocal_approx_fast(out=out, in_=in_), r=r, w=w)

        def stat_sq(kc, src_aps, src_res, b):
            s_ = kc % 2
            for ci, (c0, n) in enumerate(colt(b)):
                ACT(sq[:, s_, c0:c0 + n], src_aps[ci], AF.Square, src_res[ci], ['sq%d' % s_])

        def stat_mm(kc, b, banks):
            W = wide(banks)
            s_ = kc % 2
            for ci, (c0, n) in enumerate(colt(b)):
                MM(W[ci][0][:, 0:n], onesb[:], sq[:, s_, c0:c0 + n], kc == 0, kc == KC - 1,
                   ['onesb', 'sq%d' % s_], [W[ci][1]])

        def stat_fin(b, banks, scale):
            W = wide(banks)
            cts = colt(b)
            nc_ = sum(n for _, n in cts)
            bias = eps_ap if scale == 1.0 else cns[:, 1:2]
            assert scale in (1.0, 0.5)
            for ci, (c0, n) in enumerate(cts):
                ACT(rsq[:, c0:c0 + n], W[ci][0][:, 0:n], AF.Ln, [W[ci][1], 'cns'], ['rsq'], bias=bias,
                    scale=1.0 / (D * scale * scale))
            ACT(rstd[:, 0:nc_], rsq[:, 0:nc_], AF.Exp, ['rsq'], ['rstd'], scale=-0.5)

        def prenorm(b, l, n_idx):
            cts = colt(b)
            nc_ = sum(n for _, n in cts)
            for kc in range(KC):
                stat_sq(kc, [x[:, kc, c0:c0 + n] for c0, n in cts], [['x%d' % kc]] * len(cts), b)
                stat_mm(kc, b, (6, 7))
            stat_fin(b, (6, 7), 1.0)
            for kc in range(KC):
                STT('dve', R_xn[:, kc, 0:nc_], x[:, kc, 0:nc_], g_ap(l, n_idx, kc), rstd[:, 0:nc_],
                    ALU.mult, ALU.mult, ['x%d' % kc, 'prm', 'rstd'], ['xn%d' % kc])

        def postnorm_res(b, l, n_idx, scale, banks=(1, 5), stats_done=True):
            cts = colt(b)
            nc_ = sum(n for _, n in cts)
            if not stats_done:
                for kc in range(KC):
                    stat_sq(kc, [R_y[:, kc, c0:c0 + n] for c0, n in cts], [['y']] * len(cts), b)
                    stat_mm(kc, b, banks)
            stat_fin(b, banks, scale)
            for kc in range(KC):
                s_ = kc % 2
                STT('dve', tmpn[:, s_, 0:nc_], R_y[:, kc, 0:nc_], g_ap(l, n_idx, kc), rstd[:, 0:nc_], ALU.mult, ALU.mult,
                    ['y', 'prm', 'rstd'], ['tmpn%d' % s_])
                TT('dve', x[:, kc, 0:nc_], x[:, kc, 0:nc_], tmpn[:, s_, 0:nc_], ALU.add,
                   ['tmpn%d' % s_, 'x%d' % kc], ['x%d' % kc])

        def ffn(b, l, f):
            cts = colt(b)
            nc_ = sum(n for _, n in cts)
            prenorm(b, l, 0 if f == 0 else 4)
            hview = R_h[:, 0:FC * TBM].rearrange("p (j t) -> p j t", t=TBM)
            for j in range(FC):
                wt, wres = Wst.get('gu%d_%d_%d' % (l, f, j))
                par = j % 2
                Wg = wide((0 + 2 * par, 4 + 2 * par)); Wu = wide((1 + 2 * par, 5 + 2 * par))
                for half, Wd in ((0, Wg), (1, Wu)):
                    for ci, (c0, n) in enumerate(cts):
                        for kc in range(KC):
                            MM(Wd[ci][0][:, 0:n], wt[:, kc * 256 + half * 128:kc * 256 + half * 128 + 128],
                               R_xn[:, kc, c0:c0 + n], kc == 0, kc == KC - 1, [wres, 'xn%d' % kc], [Wd[ci][1]])
                for ci, (c0, n) in enumerate(cts):
                    ACT(R_s[:, par, c0:c0 + n], Wg[ci][0][:, 0:n], AF.Silu, [Wg[ci][1]], ['s%d' % par])
                    TT('dve', hview[:, j, c0:c0 + n], R_s[:, par, c0:c0 + n], Wu[ci][0][:, 0:n], ALU.mult,
                       ['s%d' % par, Wu[ci][1]], ['h%d' % j])
            for m in range(KC):
                wt, wres = Wst.get('dn%d_%d_%d' % (l, f, m))
                par = m % 2
                Wy = wide((0 + 2 * par, 4 + 2 * par))
                for ci, (c0, n) in enumerate(cts):
                    for j in range(FC):
                        MM(Wy[ci][0][:, 0:n], wt[:, j * 128:(j + 1) * 128], hview[:, j, c0:c0 + n], j == 0, j == FC - 1,
                           [wres, 'h%d' % j], [Wy[ci][1]])
                    CP('act', R_y[:, m, c0:c0 + n], Wy[ci][0][:, 0:n], [Wy[ci][1]], ['y'])
                stat_sq(m, [Wy[ci][0][:, 0:n] for ci, (c0, n) in enumerate(cts)], [[Wy[ci][1]] for ci in range(len(cts))], b)
                if m > 0:
                    stat_mm(m - 1, b, (1, 5))
            stat_mm(KC - 1, b, (1, 5))
            postnorm_res(b, l, 1 if f == 0 else 5, 0.5)

        def gla(b):
            cts = colt(b)
            nc_ = sum(n for _, n in cts)
            prenorm(b, 0, 2)
            hn = R_xn
            qT = R_h[:].bitcast(F32)[:, 0:4 * TBM].rearrange("p (h t) -> p h t", t=TBM)
            kT = R_h[:].bitcast(F32)[:, 4 * TBM:8 * TBM].rearrange("p (h t) -> p h t", t=TBM)
            glrA = R_h[:].bitcast(F32)[0:17, 8 * TBM:9 * TBM]
            sr = R_s[:].rearrange("p a t -> p (a t)").bitcast(BF16)[:, 0:KC * TBM].rearrange("p (c t) -> p c t", t=TBM)
            V = R_y[:].rearrange("p a t -> p (a t)").bitcast(BF16)[:, 0:5 * 1024].rearrange("p (a v) -> p a v", v=1024)
            RH = ['h%d' % j for j in range(FC)]
            SR = ['s0', 's1', 's2', 's3']
            XN = ['xn%d' % k for k in range(KC)]
            MS('pool', glrA, 1.0, RH[16:18])
            for i in range(8):
                wt, wres = Wst.get('win%d' % i)
                for cc in range(2):
                    ch = i * 2 + cc
                    par = ch % 2
                    Wd = wide((0 + 2 * par, 4 + 2 * par))
                    for ci, (c0, n) in enumerate(cts):
                        for kc in range(KC):
                            MM(Wd[ci][0][:, 0:n], wt[:, kc * 256 + cc * 128:kc * 256 + cc * 128 + 128], hn[:, kc, c0:c0 + n],
                               kc == 0, kc == KC - 1, [wres, 'xn%d' % kc], [Wd[ci][1]])
                        if ch < 4:
                            ACT(qT[:, ch, c0:c0 + n], Wd[ci][0][:, 0:n], AF.Copy, [Wd[ci][1]], RH[0:8], scale=float(128 ** -0.5))
                        elif ch < 8:
                            CP('act', kT[:, ch - 4, c0:c0 + n], Wd[ci][0][:, 0:n], [Wd[ci][1]], RH[8:16])
                        else:
                            ACT(sr[:, ch - 8, c0:c0 + n], Wd[ci][0][:, 0:n], AF.Silu, [Wd[ci][1]], SR)
            ttiles = [(i * 128, 128, False) for i in range(4)] + ([(512, 64, True)] if b == 0 else [])
            for vt in range(4):
                wt, wres = Wst.get('wv%d' % vt)
                for ti, (c0, n, smp) in enumerate(ttiles):
                    bank = 1 + (ti % 2) * 2
                    for kc in range(KC):
                        MM(PS[bank][0:n, 0:256], hn[:, kc, c0:c0 + n], wt[:, kc * 256:(kc + 1) * 256], kc == 0, kc == KC - 1,
                           [wres, 'xn%d' % kc], ['B%d' % bank])
                    CP('act' if ti % 2 == 0 else 'dve', V[0:n, ti, vt * 256:(vt + 1) * 256], PS[bank][0:n, 0:256], ['B%d' % bank], ['y'])
            wt, wres = Wst.get('wglr')
            for ci, (c0, n) in enumerate(cts):
                Wd = wide((0, 4))
                for kc in range(KC):
                    MM(Wd[ci][0][0:16, 0:n], wt[:, kc * 16:(kc + 1) * 16], hn[:, kc, c0:c0 + n], kc == 0, kc == KC - 1,
                       [wres, 'xn%d' % kc], [Wd[ci][1]])
                CP('act', glrA[0:16, c0:c0 + n], Wd[ci][0][0:16, 0:n], [Wd[ci][1]], RH[16:18])
            go = R_xn
            Rt = R_t
            Lt = Rt[:, 0:512]
            Eq = Rt[:, 512:1024].rearrange("p (h t) -> p h t", h=4)
            Ek = Rt[:, 1024:1536].rearrange("p (h t) -> p h t", h=4)
            Qf = Rt[:, 1536:2048].rearrange("p (h t) -> p h t", h=4)
            oT = Rt[:, 2048:3072].rearrange("p (c t) -> p c t", t=128)
            og = Rt[:, 3072:4096].rearrange("p (c t) -> p c t", t=128)
            ro = Rt[:, 4096:4608].rearrange("p (h t) -> p h t", h=4)
            bfv = Rt[:, 4608:6144].bitcast(BF16)
            Qt = bfv[:, 0:512].rearrange("p (h t) -> p h t", h=4)
            Kt = bfv[:, 512:1024].rearrange("p (h t) -> p h t", h=4)
            attm = bfv[:, 1024:1536].rearrange("p (h t) -> p h t", h=4)
            Ktok = bfv[:, 1536:2048].rearrange("p (h d) -> p h d", h=4)
            sqo = bfv[:, 2048:3072].rearrange("p (c t) -> p c t", t=128)
            Km = sq[:, 0, 0:512].rearrange("p (h d) -> p h d", h=4)
            tmpS = ah
            nLt, nEq, nEk, nQf, noT, nog, nro = ['R0'], ['R1'], ['R2'], ['R3'], ['R4', 'R5'], ['R6', 'R7'], ['R8']
            nQt, nKt, natt, nKtok, nsqo, nKm = ['R9'], ['R9'], ['R10'], ['R10'], ['R11'], ['sq0']
            nqT, nkT, nglr = RH[0:8], RH[8:16], RH[16:18]

            def stageA(ti):
                c0, n, smp = ttiles[ti]
                Um = Us if smp else U
                MM(PS[0][0:n, 0:512], glrA[0:17, c0:c0 + n], wg2a[0:17, :], True, True, nglr + ['wg2a'], ['B0'])
                ACT(Lt[0:n, :], PS[0][0:n, 0:512], AF.Exp, ['B0'], nLt, scale=-1.0)
                ACT(Lt[0:n, :], Lt[0:n, :], AF.Ln, nLt, nLt, bias=1.0)
                for h in range(4):
                    MM(PS[1][:, h * 128:h * 128 + n], Lt[0:n, h * 128:(h + 1) * 128], Um[0:n, 0:n], True, True,
                       nLt + ['cst'], ['B1'])
                cs3 = PS[1][:, 0:512].rearrange("p (h t) -> p h t", h=4)[:, :, 0:n]
                ACT(Eq[:, :, 0:n], cs3, AF.Exp, ['B1'], nEq, scale=-1.0 / 16)
                ACT(Ek[:, :, 0:n], cs3, AF.Exp, ['B1'], nEk, scale=1.0 / 16)
                TT('dve', Qt[:, :, 0:n], qT[:, :, c0:c0 + n], Eq[:, :, 0:n], ALU.mult, nqT + nEq, nQt)
                TT('dve', Kt[:, :, 0:n], kT[:, :, c0:c0 + n], Ek[:, :, 0:n], ALU.mult, nkT + nEk, nKt)
                if smp:
                    TT('dve', Qf[:, :, 0:n], qT[:, :, c0:c0 + n], Eq[:, :, 0:n], ALU.mult, nqT + nEq, nQf)
                for h in range(4):
                    MM(PS[2][0:n, h * 128:h * 128 + n], Kt[:, h, 0:n], Qt[:, h, 0:n], True, True, nKt + nQt, ['B2'])
                a3 = PS[2][0:n, 0:512].rearrange("p (h t) -> p h t", h=4)[:, :, 0:n]
                TT('dve', attm[0:n, :, 0:n], a3, Um[0:n, 0:n].unsqueeze(1).to_broadcast([n, 4, n]), ALU.mult,
                   ['B2', 'cst'], natt)
                p3b = PS[3][:].bitcast(BF16)
                for h in range(4):
                    TR(p3b[0:n, h * 128:(h + 1) * 128], Kt[:, h, 0:n], identb[:], nKt + ['identb'], ['B3'])
                CP('act', Ktok[0:n, :, :].rearrange("p h d -> p (h d)"), p3b[0:n, 0:512], ['B3'], nKtok)
            def stageB1(ti):
                c0, n, smp = ttiles[ti]
                Um = Us if smp else U
                if smp:
                    MS('dve', PS[4][:, 0:512], 0.0, ['B4'])
                    MS('dve', PS[5][:, 0:512], 0.0, ['B5'])
                for h in range(4):
                    for vc in range(2):
                        c8 = h * 2 + vc
                        bank = 4 + c8 // 4
                        o_ap = PS[bank][:, (c8 % 4) * 128:(c8 % 4) * 128 + n]
                        MM(o_ap, V[0:n, ti, h * 256 + vc * 128:h * 256 + vc * 128 + 128], attm[0:n, h, 0:n], not smp, smp,
                           ['y'] + natt, ['B%d' % bank])
                        if not smp:
                            MM(o_ap, Sgb[:, h, vc * 128:(vc + 1) * 128], Qt[:, h, 0:n], False, True, ['Sgb%d' % h] + nQt, ['B%d' % bank])
                if smp:
                    for s in range(16):
                        sb_ = S0b[s % 2]
                        DMA('sp', sb_[:], sgla_d[s].rearrange("h d v -> d h v"), [], ['S0b%d' % (s % 2)], 'dm_S0b%d' % (s % 2))
                        for h in range(4):
                            for vc in range(2):
                                c8 = h * 2 + vc
                                bank = 4 + c8 // 4
                                MM(PS[bank][:, (c8 % 4) * 128 + s * 4:(c8 % 4) * 128 + s * 4 + 4],
                                   sb_[:, h, vc * 128:(vc + 1) * 128], Qf[:, h, s * 4:s * 4 + 4], False, True,
                                   ['S0b%d' % (s % 2)] + nQf, ['B%d' % bank])
                        TT('pool', Km[0:n, :, :], Ktok[0:n, :, :], Msk[0:n, s:s + 1].unsqueeze(1).to_broadcast([n, 4, 128]),
                           ALU.mult, nKtok + ['cst'], nKm)
                        sbank = 6 + (s % 2)
                        sn_ = Snb[s % 2]
                        for h in range(4):
                            for hv in range(2):
                                pass
                        for h in range(4):
                            bk = 6 + h // 2
                            MM(PS[bk][:, (h % 2) * 256:(h % 2 + 1) * 256], Km[0:n, h, :], V[0:n, ti, h * 256:(h + 1) * 256],
                               True, True, nKm + ['y'], ['B%d' % bk])
                        for h in range(4):
                            bk = 6 + h // 2
                            eb = Eq[:, h, s * 4 + 3:s * 4 + 4]
                            ACT(tmpS[:, h, :], sb_[:, h, :], AF.Copy, ['S0b%d' % (s % 2)] + nEq, ['ah%d' % h], scale=eb)
                            STT('dve', sn_[:, h, :], PS[bk][:, (h % 2) * 256:(h % 2 + 1) * 256], eb, tmpS[:, h, :], ALU.mult, ALU.add,
                                ['B%d' % bk, 'ah%d' % h] + nEq, ['S0b%d' % (s % 2)])
                        DMA('sp', oglas[s].rearrange("h d v -> d h v"), sn_[:], ['S0b%d' % (s % 2)], [], 'dm_S0b%d' % (s % 2))
                else:
                    for h in range(4):
                        bk = 6 + h // 2
                        MM(PS[bk][:, (h % 2) * 256:(h % 2 + 1) * 256], Ktok[0:n, h, :], V[0:n, ti, h * 256:(h + 1) * 256],
                           True, True, nKtok + ['y'], ['B%d' % bk])
                    for h in range(4):
                        eb = Eq[:, h, n - 1:n]
                        ACT(tmpS[:, h, :], Sg[:, h, :], AF.Copy, ['Sg%d' % h] + nEq, ['ah%d' % h], scale=eb)
                    for h in range(4):
                        bk = 6 + h // 2
                        eb = Eq[:, h, n - 1:n]
                        STT('dve', Sg[:, h, :], PS[bk][:, (h % 2) * 256:(h % 2 + 1) * 256], eb, tmpS[:, h, :], ALU.mult, ALU.add,
                            ['B%d' % bk, 'ah%d' % h] + nEq, ['Sg%d' % h])
                        CP('act', Sgb[:, h, :], Sg[:, h, :], ['Sg%d' % h], ['Sgb%d' % h])
            def stageB2(ti):
                c0, n, smp = ttiles[ti]
                Um = Us if smp else U
                o4 = lambda bk: PS[bk][:, 0:512].rearrange("p (c t) -> p c t", c=4)[:, :, 0:n]
                for bk in (4, 5):
                    CP('act', oT[:, (bk - 4) * 4:(bk - 4) * 4 + 4, 0:n], o4(bk), ['B%d' % bk], noT)
                    ACT(sqo[:, (bk - 4) * 4:(bk - 4) * 4 + 4, 0:n], o4(bk), AF.Square, ['B%d' % bk], nsqo)
                for h in range(4):
                    for vc in range(2):
                        MM(PS[0][:, h * 128:h * 128 + n], onesb[:], sqo[:, h * 2 + vc, 0:n], vc == 0, vc == 1, ['onesb'] + nsqo, ['B0'])
                n3 = PS[0][:, 0:512].rearrange("p (h t) -> p h t", h=4)[:, :, 0:n]
                ACT(ro[:, :, 0:n], n3, AF.Ln, ['B0', 'cns'], nro, bias=eps_ap, scale=1.0 / 256)
                ACT(ro[:, :, 0:n], ro[:, :, 0:n], AF.Exp, nro, nro, scale=-0.5)
                oT4 = oT.rearrange("p (h v) t -> p h v t", v=2)
                og4 = og.rearrange("p (h v) t -> p h v t", v=2)
                for vc in range(2):
                    STT('dve', og4[:, :, vc, 0:n], oT4[:, :, vc, 0:n], gon_ap(vc), ro[:, :, 0:n], ALU.mult, ALU.mult,
                        noT + ['prm'] + nro, nog)
                TT('pool', go[:, :, c0:c0 + n], og[:, :, 0:n], sr[:, :, c0:c0 + n], ALU.mult, nog + SR, XN)
            stageA(0)
            for ti in range(len(ttiles)):
                stageB1(ti)
                if ti + 1 < len(ttiles):
                    stageA(ti + 1)
                stageB2(ti)
            for i in range(4):
                wt, wres = Wst.get('wout%d' % i)
                for cc in range(2):
                    m = i * 2 + cc
                    par = m % 2
                    Wy = wide((0 + 2 * par, 4 + 2 * par))
                    for ci, (c0, n) in enumerate(cts):
                        for kc in range(KC):
                            MM(Wy[ci][0][:, 0:n], wt[:, kc * 256 + cc * 128:kc * 256 + cc * 128 + 128], go[:, kc, c0:c0 + n],
                               kc == 0, kc == KC - 1, [wres, 'xn%d' % kc], [Wy[ci][1]])
                        CP('act', R_y[:, m, c0:c0 + n], Wy[ci][0][:, 0:n], [Wy[ci][1]], ['y'])
                    stat_sq(m, [Wy[ci][0][:, 0:n] for ci, (c0, n) in enumerate(cts)], [[Wy[ci][1]] for ci in range(len(cts))], b)
                    if m > 0:
                        stat_mm(m - 1, b, (1, 5))
            stat_mm(KC - 1, b, (1, 5))
            postnorm_res(b, 0, 3, 1.0)
            if b == NB - 1:
                DMA('sp', oglap.rearrange("h d v -> d h v"), Sg[:], ['Sg%d' % h for h in range(4)], [], 'st_Sg')

        def s5(b):
            cts = colt(b)
            nc_ = sum(n for _, n in cts)
            prenorm(b, 1, 2)
            u = R_xn
            RH = ['h%d' % j for j in range(FC)]
            nyS = RH[0:16]
            nzb = ['s0', 's1', 's2', 's3']
            yS = R_h[:].bitcast(F32)[:, 0:KC * TBM].rearrange("p (c t) -> p c t", t=TBM)
            zb = R_s[:].rearrange("p a t -> p (a t)").bitcast(BF16)[:, 0:KC * TBM].rearrange("p (c t) -> p c t", t=TBM)
            XT = [R_t[:, 0:1024], R_t[:, 1024:2048]]
            nXT = [['R0', 'R1'], ['R2', 'R3']]
            qv = R_t[:, 2048:4096].bitcast(BF16)
            Q = [qv[:, i * 1024:(i + 1) * 1024] for i in range(4)]
            nQ = [['R%d' % (4 + i)] for i in range(4)]
            Rper = R_t[:, 2048:3072]
            bv = R_t[:, 4096:6144].bitcast(BF16)
            BUb = [[bv[:, (p_ * 2 + ri) * 1024:(p_ * 2 + ri + 1) * 1024] for ri in range(2)] for p_ in range(2)]
            nBUb = [[['R%d' % (8 + p_ * 2 + ri)] for ri in range(2)] for p_ in range(2)]
            tv = R_t[:, 6144:7168].bitcast(BF16)
            T1 = tv[:, 0:1024]; T2 = tv[:, 1024:2048]
            nT1, nT2 = ['R12'], ['R13']
            HTb = [R_h[:, 16 * TBM:16 * TBM + 1024], R_h[:, 18 * TBM:18 * TBM + 1024]]
            nHTb = [['h16', 'h17'], ['h18', 'h19']]
            v3 = lambda a: a.rearrange("p (k j) -> p k j", j=LCH)
            v4 = lambda a: a.rearrange("p (k s j) -> p k s j", s=8, j=4)
            cosf = cosT[:].rearrange("p k j -> p (k j)"); sinf = sinT[:].rearrange("p k j -> p (k j)")
            sinNf = sinN[:].rearrange("p k j -> p (k j)")
            RTf = RT[:].rearrange("p k j -> p (k j)")
            chunks = [(i * LCH, False, 0) for i in range(PB // LCH)] + ([(512, True, 0), (544, True, 1)] if b == 0 else [])
            are = s5s[:, 0, :]; aim = s5s[:, 1, :]
            aLre = s5s[:, 6, :]; aLim = s5s[:, 7, :]
            cL1 = s5s[:, 8, :]; sL1 = s5s[:, 9, :]; c3_ = s5s[:, 10, :]; s3_ = s5s[:, 11, :]

            def stage_bu(ci_):
                c0, smp, half = chunks[ci_]
                par = ci_ % 2
                for ri in range(2):
                    for k in range(32):
                        bank = ri * 2 + k // 16
                        MM(PS[bank][:, (k % 16) * 32:(k % 16 + 1) * 32], WB[ri][:, k, :], u[:, k // 4, c0:c0 + LCH], True, True,
                           ['WB', 'xn%d' % (k // 4)], ['B%d' % bank])
                for ri in range(2):
                    for hh in range(2):
                        CP('act', BUb[par][ri][:, hh * 512:(hh + 1) * 512], PS[ri * 2 + hh][:, 0:512], ['B%d' % (ri * 2 + hh)], nBUb[par][ri])

            def stage_dve(ci_):
                c0, smp, half = chunks[ci_]
                par = ci_ % 2
                br = BUb[par][0]; bi = BUb[par][1]
                nbr = nBUb[par][0]; nbi = nBUb[par][1]
                if smp:
                    def tab(t3):
                        return t3[:, :, 0:4].unsqueeze(2).to_broadcast([128, 32, 8, 4])
                    cs_t, sn_t, snn_t = tab(cosT), tab(sinT), tab(sinN)
                    vv = v4
                else:
                    cs_t, sn_t, snn_t = cosf, sinf, sinNf
                    vv = lambda a: a
                TT('dve', vv(T1), vv(br), cs_t, ALU.mult, nbr + ['cosT'], nT1)
                TT('dve', vv(T2), vv(bi), sn_t, ALU.mult, nbi + ['sinT'], nT2)
                TT('dve', XT[0], T1, T2, ALU.add, nT1 + nT2, nXT[0])
                TT('dve', vv(T1), vv(bi), cs_t, ALU.mult, nbi + ['cosT'], nT1)
                TT('dve', vv(T2), vv(br), sn_t, ALU.mult, nbr + ['sinT'], nT2)
                TT('dve', XT[1], T1, T2, ALU.subtract, nT1 + nT2, nXT[1])
                if smp:
                    hr = H0[:, 0, half * 8:(half + 1) * 8, :].rearrange("p s k -> p k s")
                    hi = H0[:, 1, half * 8:(half + 1) * 8, :].rearrange("p s k -> p k s")
                    sh = [128, 32, 8]
                    ar3 = are.unsqueeze(2).to_broadcast(sh); ai3 = aim.unsqueeze(2).to_broadcast(sh)
                    w_ = lambda i: ah[:, i, :].rearrange("p (k s) -> p k s", s=8)
                    TT('pool', w_(0), hr, ar3, ALU.mult, ['H0', 's5s'], ['ah0'])
                    TT('pool', w_(1), hi, ai3, ALU.mult, ['H0', 's5s'], ['ah1'])
                    TT('pool', w_(0), w_(0), w_(1), ALU.subtract, ['ah0', 'ah1'], ['ah0'])
                    TT('pool', w_(2), hi, ar3, ALU.mult, ['H0', 's5s'], ['ah2'])
                    TT('pool', w_(3), hr, ai3, ALU.mult, ['H0', 's5s'], ['ah3'])
                    TT('pool', w_(2), w_(2), w_(3), ALU.add, ['ah2', 'ah3'], ['ah2'])
                    Ai = v4(XT[0])[:, :, :, 0]; Ci = v4(XT[1])[:, :, :, 0]
                    TT('dve', Ai, Ai, w_(0), ALU.add, nXT[0] + ['ah0'], nXT[0])
                    TT('dve', Ci, Ci, w_(2), ALU.add, nXT[1] + ['ah2'], nXT[1])
                    CP('pool', v4(Rper), RT[:, :, 0:4].unsqueeze(2).to_broadcast([128, 32, 8, 4]), ['RT'], nQ[0] + nQ[1])
                    rsc = Rper; rres = nQ[0] + nQ[1]
                else:
                    Ai = v3(XT[0])[:, :, 0]; Ci = v3(XT[1])[:, :, 0]
                    TT('dve', Ai, Ai, hp[:, 0, :], ALU.add, nXT[0] + ['hp'], nXT[0])
                    TT('dve', Ci, Ci, hp[:, 1, :], ALU.add, nXT[1] + ['hp'], nXT[1])
                    rsc = RTf; rres = ['RT']
                for ri in range(2):
                    P.op('dve', lambda e, o=XT[ri], d0=rsc: e.tensor_tensor_scan(out=o, data0=d0, data1=o, initial=0.0,
                                                                                  op0=ALU.mult, op1=ALU.add),
                         r=rres + nXT[ri], w=nXT[ri])
                    CP('act', HTb[ri], XT[ri], nXT[ri], nHTb[ri])
                if smp:
                    sh = [128, 32, 8]
                    c3b = c3_.unsqueeze(2).to_broadcast(sh); s3b = s3_.unsqueeze(2).to_broadcast(sh)
                    w_ = lambda i: ah[:, i, :].rearrange("p (k s) -> p k s", s=8)
                    hr3 = v4(XT[0])[:, :, :, 3]; hi3 = v4(XT[1])[:, :, :, 3]
                    Hr = Hn[:, 0, half * 8:(half + 1) * 8, :].rearrange("p s k -> p k s")
                    Hi = Hn[:, 1, half * 8:(half + 1) * 8, :].rearrange("p s k -> p k s")
                    TT('pool', w_(0), hr3, c3b, ALU.mult, nXT[0] + ['s5s'], ['ah0'])
                    TT('pool', w_(1), hi3, s3b, ALU.mult, nXT[1] + ['s5s'], ['ah1'])
                    TT('pool', Hr, w_(0), w_(1), ALU.subtract, ['ah0', 'ah1'], ['H0'])
                    TT('pool', w_(2), hi3, c3b, ALU.mult, nXT[1] + ['s5s'], ['ah2'])
                    TT('pool', w_(3), hr3, s3b, ALU.mult, nXT[0] + ['s5s'], ['ah3'])
                    TT('pool', Hi, w_(2), w_(3), ALU.add, ['ah2', 'ah3'], ['H0'])
                else:
                    w_ = lambda i: ah[:, i, 0:32]
                    hrl = v3(XT[0])[:, :, LCH - 1]; hil = v3(XT[1])[:, :, LCH - 1]
                    last = (b == NB - 1 and ci_ == PB // LCH - 1)
                    cre, cim = (cL1, sL1) if last else (aLre, aLim)
                    TT('pool', w_(0), hrl, cre, ALU.mult, nXT[0] + ['s5s'], ['ah0'])
                    TT('pool', w_(1), hil, cim, ALU.mult, nXT[1] + ['s5s'], ['ah1'])
                    TT('pool', hp[:, 0, :], w_(0), w_(1), ALU.subtract, ['ah0', 'ah1'], ['hp'])
                    TT('pool', w_(2), hil, cre, ALU.mult, nXT[1] + ['s5s'], ['ah2'])
                    TT('pool', w_(3), hrl, cim, ALU.mult, nXT[0] + ['s5s'], ['ah3'])
                    TT('pool', hp[:, 1, :], w_(2), w_(3), ALU.add, ['ah2', 'ah3'], ['hp'])
                TT('dve', vv(Q[0]), vv(HTb[0]), cs_t, ALU.mult, nHTb[0] + ['cosT'], nQ[0])
                TT('dve', vv(Q[3]), vv(HTb[0]), sn_t, ALU.mult, nHTb[0] + ['sinT'], nQ[3])
                TT('dve', vv(Q[1]), vv(HTb[1]), snn_t, ALU.mult, nHTb[1] + ['sinN'], nQ[1])
                TT('dve', vv(Q[2]), vv(HTb[1]), cs_t, ALU.mult, nHTb[1] + ['cosT'], nQ[2])

            def stage_y(ci_):
                c0, smp, half = chunks[ci_]
                ybank = 4 + ci_ % 2
                for kc in range(KC):
                    for q in range(4):
                        k = kc * 4 + q
                        for qi, wc in ((0, 0), (1, 0), (2, 1), (3, 1)):
                            MM(PS[ybank][:, kc * 32:(kc + 1) * 32], WC[wc][:, k, :], v3(Q[qi])[:, k, :],
                               q == 0 and qi == 0, q == 3 and qi == 3, ['WC'] + nQ[qi], ['B%d' % ybank])
                CP('act', yS[:, :, c0:c0 + LCH], PS[ybank][:, 0:256].rearrange("p (c j) -> p c j", j=LCH), ['B%d' % ybank], nyS)

            stage_bu(0)
            for ci_ in range(len(chunks)):
                if ci_ + 1 < len(chunks):
                    stage_bu(ci_ + 1)
                stage_dve(ci_)
                stage_y(ci_)
            for kc in range(KC):
                s = kc % 2
                STT('dve', tmpn[:, s, 0:nc_], x[:, kc, 0:nc_], g_ap(1, 2, kc), rstd[:, 0:nc_], ALU.mult, ALU.mult,
                    ['x%d' % kc, 'prm', 'rstd'], ['tmpn%d' % s])
                STT('dve', zb[:, kc, 0:nc_], tmpn[:, s, 0:nc_], d_ap(kc), yS[:, kc, 0:nc_], ALU.mult, ALU.add,
                    ['tmpn%d' % s, 'prm'] + nyS, nzb)
            sgb = tmpn
            for m in range(KC):
                wt, wres = Wst.get('wglu%d' % m)
                par = m % 2
                Wv = wide((0 + 2 * par, 4 + 2 * par)); Wg = wide((1 + 2 * par, 5 + 2 * par))
                for half_, Wd in ((0, Wv), (1, Wg)):
                    for ci, (c0, n) in enumerate(cts):
                        for kc in range(KC):
                            MM(Wd[ci][0][:, 0:n], wt[:, kc * 256 + half_ * 128:kc * 256 + half_ * 128 + 128], zb[:, kc, c0:c0 + n],
                               kc == 0, kc == KC - 1, [wres] + nzb, [Wd[ci][1]])
                for ci, (c0, n) in enumerate(cts):
                    ACT(sgb[:, par, c0:c0 + n], Wg[ci][0][:, 0:n], AF.Sigmoid, [Wg[ci][1], 'prm'], ['tmpn%d' % par], bias=bglu_ap(8 + m))
                    STT('dve', R_y[:, m, c0:c0 + n], Wv[ci][0][:, 0:n], bglu_ap(m), sgb[:, par, c0:c0 + n], ALU.add, ALU.mult,
                        [Wv[ci][1], 'prm', 'tmpn%d' % par], ['y'])
            postnorm_res(b, 1, 3, 1.0, banks=(6, 7), stats_done=False)
            if b == NB - 1:
                for ri in range(2):
                    TR(PS[0][0:32, ri * 128:(ri + 1) * 128], hp[:, ri, :], identf, ['hp', 'cst'], ['B0'])
                CP('dve', R_t[0:32, 0:256], PS[0][0:32, 0:256], ['B0'], ['R0'])
                DMA('sp', o5rep, R_t[0:32, 0:128], ['R0'], [], 'st_h1')
                DMA('sp', o5imp, R_t[0:32, 128:256], ['R0'], [], 'st_h2')
            if b == 0:
                for ri, dst in enumerate((o5res, o5ims)):
                    for j in range(4):
                        TR(PS[1 + ri][:, j * 128:(j + 1) * 128], Hn[:, ri, j * 4:(j + 1) * 4, :].rearrange("p s k -> p (s k)"), identf,
                           ['H0', 'cst'], ['B%d' % (1 + ri)])
                    buf = R_t[:, 1024 * (1 + ri):1024 * (1 + ri) + 512]
                    nb_ = ['R%d' % (2 * (1 + ri))]
                    CP('dve', buf, PS[1 + ri][:, 0:512], ['B%d' % (1 + ri)], nb_)
                    for j in range(4):
                        DMA('sp', dst[j * 128:(j + 1) * 128, :], buf[:, j * 128:(j + 1) * 128], nb_, [], 'st_hs%d' % ri)

        xres_all = ['x']
        for b in range(NB):
            DMA('sp', x[:, :, 0:512], xTp.rearrange("(kc p) t -> p kc t", p=128)[:, :, b * 512:(b + 1) * 512], [], XR, 'ld_x')
            if b == 0:
                DMA('sp', x[:, :, 512:576], xTs.rearrange("(kc p) t -> p kc t", p=128), [], XR, 'ld_xs')
            stages = [lambda: ffn(b, 0, 0), lambda: gla(b), lambda: ffn(b, 0, 1),
                      lambda: ffn(b, 1, 0), lambda: s5(b), lambda: ffn(b, 1, 1)]
            for si, st in enumerate(stages):
                if stop_after is not None and si > stop_after:
                    continue
                st()
            if stop_after is not None:
                Wst.n = (b + 1) * len(tiles)
                Wst.issued = max(Wst.issued, Wst.n)
            DMA('sp', yTp.rearrange("(kc p) t -> p kc t", p=128)[:, :, b * 512:(b + 1) * 512], x[:, :, 0:512], XR, [], 'st_x')
            if b == 0:
                DMA('sp', yTs.rearrange("(kc p) t -> p kc t", p=128), x[:, :, 512:576], XR, [], 'st_xs')
        P.wait_all('sp')

        sems = {e: es.enter_context(nc.semaphore("s_" + e)) for e in ENGS}
        dsem = {k: es.enter_context(nc.semaphore("d_" + k)) for k in P.dma_cnt}
        block = es.enter_context(nc.Block())
        P.replay(block, sems, dsem)
    return nc


def _consts():
    c = np.zeros((128, 512), np.float32)
    c[:, 0:128] = np.eye(128, dtype=np.float32)
    s = np.arange(128)
    c[:, 128:256] = (s[:, None] <= s[None, :]).astype(np.float32)
    same = (s[:, None] // 4) == (s[None, :] // 4)
    c[:, 256:384] = ((s[:, None] <= s[None, :]) & same).astype(np.float32)
    c[:, 384:384 + LCH] = np.arange(LCH, dtype=np.float32)[None, :]
    c[:, 416:432] = ((s[:, None] // 4) == np.arange(16)[None, :]).astype(np.float32)
    return c


def prepare_inputs(x_prompt, x_sample, state_gla, state_s5_re, state_s5_im, norm_g, w_ffn_gu, w_ffn_down,
                   gla_w_in, gla_w_g2, gla_b_g, gla_g_onorm, gla_w_out,
                   s5_lam_re, s5_lam_im, s5_log_dt, s5_b_re, s5_b_im, s5_c_re, s5_c_im, s5_d, s5_w_glu, s5_b_glu):
    f = lambda a: np.asarray(a, dtype=np.float32)
    WSa = build_wstream(f(w_ffn_gu), f(w_ffn_down), f(gla_w_in), f(gla_w_out), f(s5_w_glu))
    prm = np.zeros((128, 128), np.float32)
    prm[:, 0:96] = f(norm_g).reshape(2, 6, KC, 128).transpose(3, 0, 1, 2).reshape(128, 96)
    prm[:, 96:98] = f(gla_g_onorm)[0].reshape(2, 128).T
    prm[:, 98:106] = f(s5_d)[0].reshape(KC, 128).T
    prm[:, 106:122] = f(s5_b_glu)[0].reshape(16, 128).T
    wg2a = np.concatenate([f(gla_w_g2)[0], f(gla_b_g)[0][None, :]], axis=0)
    cst = _consts()
    s5p = np.concatenate([f(s5_lam_re)[0].reshape(32, 128), f(s5_lam_im)[0].reshape(32, 128),
                          np.repeat(f(s5_log_dt)[0].reshape(32, 2), 64, axis=1)], axis=1)
    bpad = np.zeros((2, 128, 32, 128), np.float32)
    cpad = np.zeros((2, 128, 32, 128), np.float32)
    for ri, (bb, cc) in enumerate(((f(s5_b_re)[0], f(s5_c_re)[0]), (f(s5_b_im)[0], f(s5_c_im)[0]))):
        for k in range(32):
            for g2 in range(2):
                g = 2 * k + g2
                col = 32 * (k % 4) + 16 * g2
                bpad[ri, g2 * 64:(g2 + 1) * 64, k, col:col + 16] = bb[g]
                cpad[ri, g2 * 64:(g2 + 1) * 64, k, col:col + 16] = cc[g].T
    bpad = bpad.reshape(2, 128, 4096); cpad = cpad.reshape(2, 128, 4096)
    in_maps = []
    for c in range(NCORE):
        m = {
            "xTp": np.ascontiguousarray(f(x_prompt)[c].T),
            "xTs": np.ascontiguousarray(f(x_sample)[16 * c:16 * c + 16].reshape(NS, D).T),
            "WS": WSa, "prm": prm, "wg2a": wg2a, "cst": cst,
            "sgla": np.ascontiguousarray(f(state_gla)[0, 16 * c:16 * c + 16]),
            "s5re": np.ascontiguousarray(f(state_s5_re)[0, 16 * c:16 * c + 16].reshape(512, 128)),
            "s5im": np.ascontiguousarray(f(state_s5_im)[0, 16 * c:16 * c + 16].reshape(512, 128)),
            "s5p": np.ascontiguousarray(s5p), "bpad": bpad, "cpad": cpad,
        }
        in_maps.append(m)
    return in_maps


def assemble(results):
    y_p = np.stack([r["yTp"].T for r in results]).astype(np.float32)
    y_s = np.concatenate([r["yTs"].T.reshape(16, 4, D) for r in results]).astype(np.float32)
    gla_p = np.stack([r["oglap"] for r in results])[None].astype(np.float32)
    re_p = np.stack([r["o5rep"].reshape(64, 64) for r in results])[None].astype(np.float32)
    im_p = np.stack([r["o5imp"].reshape(64, 64) for r in results])[None].astype(np.float32)
    gla_s = np.concatenate([r["oglas"] for r in results])[None].astype(np.float32)
    re_s = np.concatenate([r["o5res"].reshape(16, 64, 64) for r in results])[None].astype(np.float32)
    im_s = np.concatenate([r["o5ims"].reshape(16, 64, 64) for r in results])[None].astype(np.float32)
    return (y_p, y_s, gla_p, re_p, im_p, gla_s, re_s, im_s)


_NC_CACHE = {}


def kernel(**inputs):
    in_maps = prepare_inputs(**inputs)
    if 'nc' not in _NC_CACHE:
        _NC_CACHE['nc'] = build_nc()
    res = run_bass_kernel_spmd(_NC_CACHE['nc'], in_maps, core_ids=list(range(NCORE)))
    return assemble(res.results)
```

```python
import numpy as np
from contextlib import ExitStack
import concourse.bass as bass
import concourse.mybir as mybir
from concourse.bass_utils import run_bass_kernel_spmd

F32 = mybir.dt.float32
BF16 = mybir.dt.bfloat16
AF = mybir.ActivationFunctionType
ALU = mybir.AluOpType
ENGS = ['pe', 'act', 'dve', 'pool', 'sp']

NCORE = 8
D = 1024; KC = 8; DFF = 2816; FC = 22
NB = 4; PB = 512; NS = 64; TBM = 576
EPS = 1e-6
LCH = 32
MAGIC = 12582912.0
TWO_PI = float(2 * np.pi)


class Prog:
    def __init__(self):
        self.ops = {e: [] for e in ENGS}
        self.last_w = {}
        self.reads = {}
        self.seen = {e: {} for e in ENGS}
        self.dma_cnt = {}
        self.flag = {e: set() for e in ENGS}
        self.defer = None
        self.bg = []
        self.atomic_depth = 0

    def mark(self):
        if self.defer is not None:
            self.defer.append('P')

    def atomic_begin(self):
        self.atomic_depth += 1

    def atomic_end(self):
        self.atomic_depth -= 1
        if self.defer is not None and not self.atomic_depth:
            self.defer.append('G')

    def bg_step(self, k):
        while k > 0 and self.bg:
            it = self.bg.pop(0)
            if it == 'G':
                k -= 1
            elif it != 'P':
                self.op(*it)

    def bg_to_mark(self):
        while self.bg:
            it = self.bg.pop(0)
            if it == 'P':
                return
            if it != 'G':
                self.op(*it)

    def bg_all(self):
        while self.bg:
            it = self.bg.pop(0)
            if it not in ('G', 'P'):
                self.op(*it)

    def _need(self, eng, tok, waits, is_dma, raw):
        if tok is None:
            return
        if tok[0] == 'eng':
            _, X, idx = tok
            if X == eng and not is_dma:
                if eng == 'pe' or not raw:
                    return
            key = ('eng', X)
            if self.seen[eng].get(key, -1) >= idx:
                return
            self.seen[eng][key] = idx
            self.flag[X].add(idx)
            waits.append(tok)
        else:
            _, k, val = tok
            key = ('dma', k)
            if self.seen[eng].get(key, -1) >= val:
                return
            self.seen[eng][key] = val
            waits.append(tok)

    def op(self, eng, fn, r=(), w=(), dma=None):
        if self.defer is not None:
            self.defer.append((eng, fn, tuple(r), tuple(w), dma))
            if not self.atomic_depth:
                self.defer.append('G')
            return None
        idx = len(self.ops[eng])
        is_dma = dma is not None
        if is_dma:
            self.dma_cnt[dma] = self.dma_cnt.get(dma, 0) + 1
            tok = ('dma', dma, self.dma_cnt[dma] * 16)
        else:
            tok = ('eng', eng, idx)
        waits = []
        cand = {}

        def add(t, raw):
            if t is None:
                return
            if t[0] == 'eng':
                if t[1] == eng and not is_dma and (eng == 'pe' or not raw):
                    return
                k = ('eng', t[1])
            else:
                k = ('dma', t[1])
            if k not in cand or cand[k][2] < t[2]:
                cand[k] = t
        for res in r:
            add(self.last_w.get(res), True)
            if res[0] == 'B' and res[1:].isdigit():
                for t in self.reads.get(res, ()):
                    if t[0] == 'eng' and t[1] != eng:
                        add(t, False)
        for res in w:
            add(self.last_w.get(res), False)
            for t in self.reads.get(res, ()):
                add(t, False)
        for t in cand.values():
            self._need(eng, t, waits, is_dma, True)
        for res in r:
            self.reads.setdefault(res, []).append(tok)
        for res in w:
            self.last_w[res] = tok
            self.reads[res] = []
        self.ops[eng].append(dict(fn=fn, waits=waits, dma=dma))
        return tok

    def wait_all(self, eng):
        waits = []
        for k, c in self.dma_cnt.items():
            self._need(eng, ('dma', k, c * 16), waits, True, True)
        for X in ENGS:
            if X != eng:
                for idx in range(len(self.ops[X]) - 1, -1, -1):
                    if self.ops[X][idx]['dma'] is None and self.ops[X][idx]['fn'] is not None:
                        self._need(eng, ('eng', X, idx), waits, True, True)
                        break
        self.ops[eng].append(dict(fn=None, waits=waits, dma=None))

    def replay(self, block, sems, dma_sems):
        rank = {}
        for e in ENGS:
            rank[e] = {idx: i + 1 for i, idx in enumerate(sorted(self.flag[e]))}
        ops = self.ops
        flag = self.flag

        def run(name, e):
            for idx, o in enumerate(ops[name]):
                for t in o['waits']:
                    if t[0] == 'eng':
                        e.wait_ge(sems[t[1]], rank[t[1]][t[2]])
                    else:
                        e.wait_ge(dma_sems[t[1]], t[2])
                if o['fn'] is None:
                    continue
                ins = o['fn'](e)
                if o['dma'] is not None:
                    ins.then_inc(dma_sems[o['dma']], 16)
                elif idx in flag[name]:
                    ins.then_inc(sems[name], 1)

        block.tensor(lambda e: run('pe', e))
        block.scalar(lambda e: run('act', e))
        block.vector(lambda e: run('dve', e))
        block.gpsimd(lambda e: run('pool', e))
        block.sync(lambda e: run('sp', e))


def weight_tiles():
    t = []
    for l in range(2):
        for f in range(2):
            if f == 1:
                if l == 0:
                    t += [('win%d' % i, 2048) for i in range(8)]
                    t += [('wv%d' % i, 2048) for i in range(4)]
                    t += [('wglr', 128)]
                    t += [('wout%d' % i, 2048) for i in range(4)]
                else:
                    t += [('wglu%d' % i, 2048) for i in range(8)]
            t += [('gu%d_%d_%d' % (l, f, j), 2048) for j in range(FC)]
            t += [('dn%d_%d_%d' % (l, f, m), 2816) for m in range(KC)]
    return t


def build_wstream(w_ffn_gu, w_ffn_down, gla_w_in, gla_w_out, s5_w_glu):
    def kmaj(w):
        C = w.shape[1]
        return w.reshape(KC, 128, C).transpose(1, 0, 2).reshape(128, KC * C)
    parts = []
    win = gla_w_in[0]
    for l in range(2):
        for f in range(2):
            if f == 1:
                if l == 0:
                    fm = np.concatenate([win[:, 0:1024], win[:, 2048:3072]], axis=1)
                    for i in range(8):
                        parts.append(kmaj(fm[:, i * 256:(i + 1) * 256]))
                    for i in range(4):
                        parts.append(kmaj(win[:, 1024 + i * 256:1024 + (i + 1) * 256]))
                    parts.append(kmaj(win[:, 3072:3088]))
                    for i in range(4):
                        parts.append(kmaj(gla_w_out[0][:, i * 256:(i + 1) * 256]))
                else:
                    wg = s5_w_glu[0]
                    for m in range(8):
                        parts.append(kmaj(np.concatenate([wg[:, m * 128:(m + 1) * 128],
                                                          wg[:, 1024 + m * 128:1024 + (m + 1) * 128]], axis=1)))
            gu = w_ffn_gu[l, f]
            for j in range(FC):
                parts.append(kmaj(np.concatenate([gu[:, j * 128:(j + 1) * 128],
                                                  gu[:, DFF + j * 128:DFF + (j + 1) * 128]], axis=1)))
            dn = w_ffn_down[l, f]
            for m in range(KC):
                parts.append(dn[:, m * 128:(m + 1) * 128].reshape(FC, 128, 128).transpose(1, 0, 2).reshape(128, FC * 128))
    return np.ascontiguousarray(np.concatenate(parts, axis=1), dtype=np.float32)


def build_nc(stop_after=None):
    nc = bass.Bass("TRN2", target_bir_lowering=False)
    tiles = weight_tiles()
    offs = np.cumsum([0] + [f for _, f in tiles]).tolist()
    TOT = offs[-1]

    def din(name, shape):
        return nc.dram_tensor(name, shape, F32, kind="ExternalInput").ap()

    def dout(name, shape):
        return nc.dram_tensor(name, shape, F32, kind="ExternalOutput").ap()

    xTp = din("xTp", [D, 2048]); xTs = din("xTs", [D, NS])
    WS = din("WS", [128, TOT])
    prm_d = din("prm", [128, 128]); wg2a_d = din("wg2a", [17, 512])
    cst_d = din("cst", [128, 512])
    sgla_d = din("sgla", [16, 4, 128, 256])
    s5re_d = din("s5re", [512, 128]); s5im_d = din("s5im", [512, 128])
    s5p_d = din("s5p", [32, 384])
    bpad_d = din("bpad", [2, 128, 4096]); cpad_d = din("cpad", [2, 128, 4096])
    yTp = dout("yTp", [D, 2048]); yTs = dout("yTs", [D, NS])
    oglap = dout("oglap", [4, 128, 256]); oglas = dout("oglas", [16, 4, 128, 256])
    o5rep = dout("o5rep", [32, 128]); o5imp = dout("o5imp", [32, 128])
    o5res = dout("o5res", [512, 128]); o5ims = dout("o5ims", [512, 128])

    P = Prog()
    es = ExitStack()
    with es:
        def sb(name, shape, dt=F32):
            return es.enter_context(nc.sbuf_tensor(name, shape, dt))

        x = sb("x", [128, KC, TBM])
        prm = sb("prm_s", [128, 128]); wg2a = sb("wg2a_s", [17, 512]); cst = sb("cst_s", [128, 512])
        identb = sb("identb", [128, 128], BF16); onesb = sb("onesb", [128, 128], BF16)
        cns = sb("cns", [128, 4])
        Sg = sb("Sg", [128, 4, 256]); Sgb = sb("Sgb", [128, 4, 256], BF16)
        WB = [sb("WBre", [128, 32, 128], BF16), sb("WBim", [128, 32, 128], BF16)]
        WC = [sb("WCre", [128, 32, 128], BF16), sb("WCim", [128, 32, 128], BF16)]
        cosT = sb("cosT", [128, 32, LCH], BF16); sinT = sb("sinT", [128, 32, LCH], BF16); sinN = sb("sinN", [128, 32, LCH], BF16)
        RT = sb("RT", [128, 32, LCH])
        s5s = sb("s5s", [128, 12, 32])
        hp = sb("hp", [128, 2, 32])
        ah = sb("ah", [128, 4, 256])
        H0 = sb("H0", [128, 2, 16, 32]); Hn = H0
        NSL = 3
        wsl = [sb("wsl%d" % i, [128, 2816], BF16) for i in range(NSL)]
        R_xn = sb("R_xn", [128, KC, TBM], BF16)
        R_h = sb("R_h", [128, FC * TBM], BF16)
        R_y = sb("R_y", [128, KC, TBM])
        R_s = sb("R_s", [128, 4, TBM])
        sq = sb("sq", [128, 2, TBM], BF16)
        rstd = sb("rstd", [128, TBM]); rsq = sb("rsq", [128, TBM]); tmpn = sb("tmpn", [128, 2, TBM])
        R_t = sb("R_t", [128, 7168])
        S0b = [sb("S0b%d" % i, [128, 4, 256]) for i in range(2)]
        Snb = S0b
        RTALL = ["R%d" % i for i in range(14)]
        def seg(a, b_):
            return ["R%d" % i for i in range(a // 512, (b_ + 511) // 512)]
        PS = [es.enter_context(nc.psum_tensor("B%d" % i, [128, 512], F32)) for i in range(8)]

        identf = cst[:, 0:128]; U = cst[:, 128:256]; Us = cst[:, 256:384]
        iotaj = cst[:, 384:384 + LCH]; Msk = cst[:, 416:432]
        g_ap = lambda l, n, kc: prm[:, (l * 6 + n) * 8 + kc:(l * 6 + n) * 8 + kc + 1]
        gon_ap = lambda vc: prm[:, 96 + vc:97 + vc]
        d_ap = lambda kc: prm[:, 98 + kc:99 + kc]
        bglu_ap = lambda i: prm[:, 106 + i:107 + i]

        def MM(out, lhsT, rhs, start, stop, r, w):
            P.op('pe', lambda e: e.matmul(out, lhsT=lhsT, rhs=rhs, start=start, stop=stop), r=r, w=w)

        def TR(out, in_, ident, r, w):
            P.op('pe', lambda e: e.transpose(out, in_, ident), r=r, w=w)

        def ACT(out, in_, func, r, w, bias=None, scale=None):
            kw = {}
            if bias is not None: kw['bias'] = bias
            if scale is not None: kw['scale'] = scale
            P.op('act', lambda e: e.activation(out=out, in_=in_, func=func, **kw), r=r, w=w)

        def STT(eng, out, in0, scalar, in1, op0, op1, r, w):
            eng = 'dve'
            P.op(eng, lambda e: e.scalar_tensor_tensor(out=out, in0=in0, scalar=scalar, in1=in1, op0=op0, op1=op1), r=r, w=w)

        def TT(eng, out, in0, in1, op, r, w):
            P.op(eng, lambda e: e.tensor_tensor(out=out, in0=in0, in1=in1, op=op), r=r, w=w)

        def TS(eng, out, in0, s1, s2, op0, op1, r, w):
            if s2 is None:
                P.op(eng, lambda e: e.tensor_single_scalar(out=out, in_=in0, scalar=s1, op=op0), r=r, w=w)
            else:
                P.op(eng, lambda e: e.tensor_scalar(out=out, in0=in0, scalar1=s1, scalar2=s2, op0=op0, op1=op1), r=r, w=w)

        def CP(eng, out, in_, r, w):
            if eng == 'act':
                P.op('act', lambda e: e.copy(out=out, in_=in_), r=r, w=w)
            else:
                P.op(eng, lambda e: e.tensor_copy(out=out, in_=in_), r=r, w=w)

        def MS(eng, ap, val, w):
            P.op(eng, lambda e: e.memset(ap, val), w=w)

        def DMA(eng, out, in_, r, w, key):
            P.op(eng, lambda e: e.dma_start(out=out, in_=in_), r=r, w=w, dma=key)

        def RECIP(out, in_, r, w):
            P.op('dve', lambda e: e.reciprocal(out=out, in_=in_), r=r, w=w)

        class WStream:
            def __init__(self):
                self.n = 0
                self.issued = 0
                self.total = NB * len(tiles)

            def _issue(self, i):
                ti = i % len(tiles)
                F = tiles[ti][1]
                s = i % NSL
                DMA('pool', wsl[s][:, 0:F], WS[:, offs[ti]:offs[ti] + F], r=[], w=['ws%d' % s], key='ws%d' % s)

            def get(self, name):
                i = self.n
                assert tiles[i % len(tiles)][0] == name, (tiles[i % len(tiles)][0], name)
                while self.issued < min(i + NSL, self.total):
                    self._issue(self.issued)
                    self.issued += 1
                self.n += 1
                s = i % NSL
                return wsl[s], 'ws%d' % s

        Wst = WStream()

        DMA('sp', prm[:], prm_d, [], ['prm'], 'ld_prm')
        DMA('sp', wg2a[:], wg2a_d, [], ['wg2a'], 'ld_wg2a')
        DMA('sp', cst[:], cst_d, [], ['cst'], 'ld_cst')
        CP('dve', identb[:], identf, ['cst'], ['identb'])
        MS('dve', onesb[:], 1.0, ['onesb'])
        MS('dve', cns[:, 0:1], EPS, ['cns'])
        MS('dve', cns[:, 1:2], 4 * EPS, ['cns'])
        XR = ['x%d' % k for k in range(KC)]
        MS('dve', Sg[:], 0.0, ['Sg%d' % h for h in range(4)])
        MS('dve', Sgb[:], 0.0, ['Sgb%d' % h for h in range(4)])
        MS('dve', hp[:], 0.0, ['hp'])
        eps_ap = cns[:, 0:1]

        def colt(b):
            return [(0, 512)] + ([(512, 64)] if b == 0 else [])

        def wide(banks):
            main, aux = banks
            return [(PS[main][:, 0:512], 'B%d' % main), (PS[aux][:, 0:64], 'B%d' % aux)]

        def s5_setup():
            sp = R_t
            lam_re = sp[0:32, 0:128]; lam_im = sp[0:32, 128:256]; ldt = sp[0:32, 256:384]
            DMA('sp', sp[0:32, 0:384], s5p_d, [], RTALL, 'ld_s5p')
            t = lambda i: sp[0:32, 384 + i * 128:384 + (i + 1) * 128]
            dt_, mag, ang, den, are, aim, fre, fim, tA, tB, sn, cs_ = [t(i) for i in range(12)]
            aLre, aLim, cL1, sL1, c3_, s3_, angx = [t(12 + i) for i in range(7)]
            rw = dict(r=RTALL, w=RTALL)
            ACT(dt_, ldt, AF.Exp, **rw)
            TT('dve', den, lam_re, dt_, ALU.mult, **rw)
            TS('dve', mag, den, 1.0 / 7, 1.0, ALU.mult, ALU.add, **rw)
            for c_ in (6, 5, 4, 3, 2, 1):
                TT('dve', mag, mag, den, ALU.mult, **rw)
                TS('dve', mag, mag, 1.0 / c_ if c_ > 1 else 1.0, 1.0, ALU.mult, ALU.add, **rw)
            TT('dve', ang, lam_im, dt_, ALU.mult, **rw)

            def sincos(dst, src, shift):
                TS('dve', tA, src, float(1 / TWO_PI), float(shift / TWO_PI), ALU.mult, ALU.add, **rw)
                TS('dve', tA, tA, MAGIC, None, ALU.add, None, **rw)
                TS('dve', tA, tA, -MAGIC, -TWO_PI, ALU.add, ALU.mult, **rw)
                STT('dve', tB, src, 1.0, tA, ALU.mult, ALU.add, **rw)
                if shift != 0.0:
                    TS('dve', tB, tB, float(shift), None, ALU.add, None, **rw)
                ACT(dst, tB, AF.Sin, **rw)
            sincos(sn, ang, 0.0)
            sincos(cs_, ang, float(np.pi / 2))
            TT('dve', are, mag, cs_, ALU.mult, **rw)
            TT('dve', aim, mag, sn, ALU.mult, **rw)
            TT('dve', den, lam_re, lam_re, ALU.mult, **rw)
            TT('dve', tA, lam_im, lam_im, ALU.mult, **rw)
            TT('dve', den, den, tA, ALU.add, **rw)
            RECIP(den, den, **rw)
            TS('dve', tB, are, -1.0, None, ALU.add, None, **rw)
            TT('dve', fre, tB, lam_re, ALU.mult, **rw)
            TT('dve', tA, aim, lam_im, ALU.mult, **rw)
            TT('dve', fre, fre, tA, ALU.add, **rw)
            TT('dve', fre, fre, den, ALU.mult, **rw)
            TT('dve', fim, aim, lam_re, ALU.mult, **rw)
            TT('dve', tA, tB, lam_im, ALU.mult, **rw)
            TT('dve', fim, fim, tA, ALU.subtract, **rw)
            TT('dve', fim, fim, den, ALU.mult, **rw)
            for mult_, (sdst, cdst) in ((float(LCH), (aLim, aLre)), (float(LCH - 1), (sL1, cL1)), (3.0, (s3_, c3_))):
                TS('dve', angx, ang, mult_, None, ALU.mult, None, **rw)
                sincos(sdst, angx, 0.0)
                sincos(cdst, angx, float(np.pi / 2))
            TT('dve', aLre, aLre, mag, ALU.mult, **rw)
            TT('dve', aLim, aLim, mag, ALU.mult, **rw)
            P.atomic_begin()
            for i, src in enumerate([are, aim, fre, fim, ang, mag, aLre, aLim, cL1, sL1, c3_, s3_]):
                TR(PS[0][:, i * 32:(i + 1) * 32], src, identf[0:32, 0:32], RTALL + ['cst'], ['B0'])
            CP('dve', s5s[:].rearrange("p a k -> p (a k)"), PS[0][:, 0:384], ['B0'], ['s5s'])
            P.atomic_end()
            P.mark()
            th = s5s[:, 4, :]; rr_ = s5s[:, 5, :]
            A3 = R_t[:, 0:1024].rearrange("p (k j) -> p k j", j=LCH)
            B3 = R_t[:, 1024:2048].rearrange("p (k j) -> p k j", j=LCH)
            C3 = R_t[:, 2048:3072].rearrange("p (k j) -> p k j", j=LCH)
            io3 = iotaj.unsqueeze(1).to_broadcast([128, 32, LCH])
            th3 = th.unsqueeze(2).to_broadcast([128, 32, LCH])
            TT('dve', A3, io3, th3, ALU.mult, ['cst', 's5s'] + RTALL, RTALL)
            for dst, shift in ((sinT, 0.0), (cosT, float(np.pi / 2))):
                TS('dve', B3, A3, float(1 / TWO_PI), float(shift / TWO_PI), ALU.mult, ALU.add, **rw)
                TS('dve', B3, B3, MAGIC, None, ALU.add, None, **rw)
                TS('dve', B3, B3, -MAGIC, -TWO_PI, ALU.add, ALU.mult, **rw)
                TT('dve', C3, A3, B3, ALU.add, **rw)
                if shift != 0.0:
                    TS('dve', C3, C3, float(shift), None, ALU.add, None, **rw)
                ACT(dst[:], C3, AF.Sin, RTALL, [dst is sinT and 'sinT' or 'cosT'])
                if dst is sinT:
                    ACT(sinN[:], C3, AF.Sin, RTALL, ['sinN'], scale=-1.0)
            r3 = rr_.unsqueeze(2).to_broadcast([128, 32, LCH])
            CP('dve', RT[:], r3, ['s5s'], ['RT'])
            MS('dve', RT[:, :, 0:1], 0.0, ['RT'])
            P.mark()
            bre = R_t[:, 0:1024]; bim = R_t[:, 1024:2048]; o1 = R_t[:, 2048:3072]; o2 = R_t[:, 3072:4096]
            for kg in range(4):
                DMA('sp', bre, bpad_d[0, :, kg * 1024:(kg + 1) * 1024], [], RTALL, 'ld_b0')
                DMA('sp', bim, bpad_d[1, :, kg * 1024:(kg + 1) * 1024], [], RTALL, 'ld_b1')
                v3 = lambda a: a.rearrange("p (k c) -> p k c", c=128)
                f_re3 = s5s[:, 2, kg * 8:(kg + 1) * 8].unsqueeze(2).to_broadcast([128, 8, 128])
                f_im3 = s5s[:, 3, kg * 8:(kg + 1) * 8].unsqueeze(2).to_broadcast([128, 8, 128])
                rs = RTALL + ['s5s']
                TT('dve', v3(o1), v3(bre), f_re3, ALU.mult, rs, RTALL)
                TT('dve', v3(o2), v3(bim), f_im3, ALU.mult, rs, RTALL)
                TT('dve', o1, o1, o2, ALU.subtract, rs, RTALL)
                TT('dve', v3(o2), v3(bre), f_im3, ALU.mult, rs, RTALL)
                TT('dve', v3(bre), v3(bim), f_re3, ALU.mult, rs, RTALL)
                TT('dve', o2, o2, bre, ALU.add, rs, RTALL)
                for ri, src in enumerate((o1, o2)):
                    P.atomic_begin()
                    WBB = ((2, 3), (4, 6))
                    for kk in range(8):
                        bi = WBB[ri][kk // 4]
                        TR(PS[bi][:, (kk % 4) * 128:(kk % 4 + 1) * 128], src[:, kk * 128:(kk + 1) * 128], identf,
                           RTALL + ['cst'], ['B%d' % bi])
                    for hh in range(2):
                        bi = WBB[ri][hh]
                        CP('act', WB[ri][:, kg * 8 + hh * 4:kg * 8 + hh * 4 + 4, :].rearrange("p k c -> p (k c)"),
                           PS[bi][:, 0:512], ['B%d' % bi], ['WB'])
                    P.atomic_end()
                P.mark()
            for kg in range(4):
                DMA('sp', bre, cpad_d[0, :, kg * 1024:(kg + 1) * 1024], [], RTALL, 'ld_b0')
                DMA('sp', bim, cpad_d[1, :, kg * 1024:(kg + 1) * 1024], [], RTALL, 'ld_b1')
                CP('act', WC[0][:, kg * 8:(kg + 1) * 8, :].rearrange("p k c -> p (k c)"), bre, RTALL, ['WC'])
                P.op('act', lambda e, kg=kg: e.mul(out=WC[1][:, kg * 8:(kg + 1) * 8, :].rearrange("p k c -> p (k c)"),
                                                    in_=bim, mul=-1.0), r=RTALL, w=['WC'])
                P.mark()
            for ri, src in enumerate((s5re_d, s5im_d)):
                bufs = [R_t[:, 4096 + j * 128:4096 + (j + 1) * 128] for j in range(4)]
                for j in range(4):
                    DMA('sp', bufs[j], src[j * 128:(j + 1) * 128, :], [], RTALL, 'ld_h0_%d' % j)
                P.atomic_begin()
                hb_ = (7, 0)[ri]
                for j in range(4):
                    TR(PS[hb_][:, j * 128:(j + 1) * 128], bufs[j], identf, RTALL + ['cst'], ['B%d' % hb_])
                CP('dve', H0[:, ri, :, :].rearrange("p s k -> p (s k)"), PS[hb_][:, 0:512], ['B%d' % hb_], ['H0'])
                P.atomic_end()
                P.mark()

        P.defer = []
        s5_setup()
        P.bg, P.defer = P.defer, None

        def RECIPF(out, in_, r, w):
            P.op('dve', lambda e: e.reciprocal_approx_fast(out=out, in_=in_), r=r, w=w)

        def stat_sq(kc, src_aps, src_res, b):
            s_ = kc % 2
            for ci, (c0, n) in enumerate(colt(b)):
                ACT(sq[:, s_, c0:c0 + n], src_aps[ci], AF.Square, src_res[ci], ['sq%d' % s_])

        def stat_mm(kc, b, banks):
            W = wide(banks)
            s_ = kc % 2
            for ci, (c0, n) in enumerate(colt(b)):
                MM(W[ci][0][:, 0:n], onesb[:], sq[:, s_, c0:c0 + n], kc == 0, kc == KC - 1,
                   ['onesb', 'sq%d' % s_], [W[ci][1]])

        def stat_fin(b, banks, scale):
            W = wide(banks)
            cts = colt(b)
            nc_ = sum(n for _, n in cts)
            bias = eps_ap if scale == 1.0 else cns[:, 1:2]
            assert scale in (1.0, 0.5)
            for ci, (c0, n) in enumerate(cts):
                ACT(rsq[:, c0:c0 + n], W[ci][0][:, 0:n], AF.Ln, [W[ci][1], 'cns'], ['rsq'], bias=bias,
                    scale=1.0 / (D * scale * scale))
            ACT(rstd[:, 0:nc_], rsq[:, 0:nc_], AF.Exp, ['rsq'], ['rstd'], scale=-0.5)

        def prenorm(b, l, n_idx):
            cts = colt(b)
            nc_ = sum(n for _, n in cts)
            for kc in range(KC):
                stat_sq(kc, [x[:, kc, c0:c0 + n] for c0, n in cts], [['x%d' % kc]] * len(cts), b)
                stat_mm(kc, b, (6, 7))
            stat_fin(b, (6, 7), 1.0)
            for kc in range(KC):
                STT('dve', R_xn[:, kc, 0:nc_], x[:, kc, 0:nc_], g_ap(l, n_idx, kc), rstd[:, 0:nc_],
                    ALU.mult, ALU.mult, ['x%d' % kc, 'prm', 'rstd'], ['xn%d' % kc])

        def postnorm_res(b, l, n_idx, scale, banks=(1, 5), stats_done=True):
            cts = colt(b)
            nc_ = sum(n for _, n in cts)
            if not stats_done:
                for kc in range(KC):
                    stat_sq(kc, [R_y[:, kc, c0:c0 + n] for c0, n in cts], [['y']] * len(cts), b)
                    stat_mm(kc, b, banks)
            stat_fin(b, banks, scale)
            for kc in range(KC):
                s_ = kc % 2
                STT('dve', tmpn[:, s_, 0:nc_], R_y[:, kc, 0:nc_], g_ap(l, n_idx, kc), rstd[:, 0:nc_], ALU.mult, ALU.mult,
                    ['y', 'prm', 'rstd'], ['tmpn%d' % s_])
                TT('dve', x[:, kc, 0:nc_], x[:, kc, 0:nc_], tmpn[:, s_, 0:nc_], ALU.add,
                   ['tmpn%d' % s_, 'x%d' % kc], ['x%d' % kc])

        def ffn(b, l, f):
            cts = colt(b)
            nc_ = sum(n for _, n in cts)
            prenorm(b, l, 0 if f == 0 else 4)
            hview = R_h[:, 0:FC * TBM].rearrange("p (j t) -> p j t", t=TBM)
            for j in range(FC):
                P.bg_step(4)
                wt, wres = Wst.get('gu%d_%d_%d' % (l, f, j))
                par = j % 2
                Wg = wide((0 + 2 * par, 4 + 2 * par)); Wu = wide((1 + 2 * par, 5 + 2 * par))
                for half, Wd in ((0, Wg), (1, Wu)):
                    for ci, (c0, n) in enumerate(cts):
                        for kc in range(KC):
                            MM(Wd[ci][0][:, 0:n], wt[:, kc * 256 + half * 128:kc * 256 + half * 128 + 128],
                               R_xn[:, kc, c0:c0 + n], kc == 0, kc == KC - 1, [wres, 'xn%d' % kc], [Wd[ci][1]])
                for ci, (c0, n) in enumerate(cts):
                    ACT(R_s[:, par, c0:c0 + n], Wg[ci][0][:, 0:n], AF.Silu, [Wg[ci][1]], ['s%d' % par])
                    TT('dve', hview[:, j, c0:c0 + n], R_s[:, par, c0:c0 + n], Wu[ci][0][:, 0:n], ALU.mult,
                       ['s%d' % par, Wu[ci][1]], ['h%d' % j])
            for m in range(KC):
                P.bg_step(4)
                wt, wres = Wst.get('dn%d_%d_%d' % (l, f, m))
                par = m % 2
                Wy = wide((0 + 2 * par, 4 + 2 * par))
                for ci, (c0, n) in enumerate(cts):
                    for j in range(FC):
                        MM(Wy[ci][0][:, 0:n], wt[:, j * 128:(j + 1) * 128], hview[:, j, c0:c0 + n], j == 0, j == FC - 1,
                           [wres, 'h%d' % j], [Wy[ci][1]])
                    CP('act', R_y[:, m, c0:c0 + n], Wy[ci][0][:, 0:n], [Wy[ci][1]], ['y'])
                stat_sq(m, [Wy[ci][0][:, 0:n] for ci, (c0, n) in enumerate(cts)], [[Wy[ci][1]] for ci in range(len(cts))], b)
                if m > 0:
                    stat_mm(m - 1, b, (1, 5))
            stat_mm(KC - 1, b, (1, 5))
            postnorm_res(b, l, 1 if f == 0 else 5, 0.5)

        def gla(b):
            P.bg_to_mark()
            cts = colt(b)
            nc_ = sum(n for _, n in cts)
            prenorm(b, 0, 2)
            hn = R_xn
            qT = R_h[:].bitcast(F32)[:, 0:4 * TBM].rearrange("p (h t) -> p h t", t=TBM)
            kT = R_h[:].bitcast(F32)[:, 4 * TBM:8 * TBM].rearrange("p (h t) -> p h t", t=TBM)
            glrA = R_h[:].bitcast(F32)[0:17, 8 * TBM:9 * TBM]
            sr = R_s[:].rearrange("p a t -> p (a t)").bitcast(BF16)[:, 0:KC * TBM].rearrange("p (c t) -> p c t", t=TBM)
            V = R_y[:].rearrange("p a t -> p (a t)").bitcast(BF16)[:, 0:5 * 1024].rearrange("p (a v) -> p a v", v=1024)
            RH = ['h%d' % j for j in range(FC)]
            SR = ['s0', 's1', 's2', 's3']
            XN = ['xn%d' % k for k in range(KC)]
            MS('pool', glrA, 1.0, RH[16:18])
            for i in range(8):
                wt, wres = Wst.get('win%d' % i)
                for cc in range(2):
                    ch = i * 2 + cc
                    par = ch % 2
                    Wd = wide((0 + 2 * par, 4 + 2 * par))
                    for ci, (c0, n) in enumerate(cts):
                        for kc in range(KC):
                            MM(Wd[ci][0][:, 0:n], wt[:, kc * 256 + cc * 128:kc * 256 + cc * 128 + 128], hn[:, kc, c0:c0 + n],
                               kc == 0, kc == KC - 1, [wres, 'xn%d' % kc], [Wd[ci][1]])
                        if ch < 4:
                            ACT(qT[:, ch, c0:c0 + n], Wd[ci][0][:, 0:n], AF.Copy, [Wd[ci][1]], RH[0:8], scale=float(128 ** -0.5))
                        elif ch < 8:
                            CP('act', kT[:, ch - 4, c0:c0 + n], Wd[ci][0][:, 0:n], [Wd[ci][1]], RH[8:16])
                        else:
                            ACT(sr[:, ch - 8, c0:c0 + n], Wd[ci][0][:, 0:n], AF.Silu, [Wd[ci][1]], SR)
            ttiles = [(i * 128, 128, False) for i in range(4)] + ([(512, 64, True)] if b == 0 else [])
            for vt in range(4):
                wt, wres = Wst.get('wv%d' % vt)
                for ti, (c0, n, smp) in enumerate(ttiles):
                    bank = 1 + (ti % 2) * 2
                    for kc in range(KC):
                        MM(PS[bank][0:n, 0:256], hn[:, kc, c0:c0 + n], wt[:, kc * 256:(kc + 1) * 256], kc == 0, kc == KC - 1,
                           [wres, 'xn%d' % kc], ['B%d' % bank])
                    CP('act' if ti % 2 == 0 else 'dve', V[0:n, ti, vt * 256:(vt + 1) * 256], PS[bank][0:n, 0:256], ['B%d' % bank], ['y'])
            wt, wres = Wst.get('wglr')
            for ci, (c0, n) in enumerate(cts):
                Wd = wide((0, 4))
                for kc in range(KC):
                    MM(Wd[ci][0][0:16, 0:n], wt[:, kc * 16:(kc + 1) * 16], hn[:, kc, c0:c0 + n], kc == 0, kc == KC - 1,
                       [wres, 'xn%d' % kc], [Wd[ci][1]])
                CP('act', glrA[0:16, c0:c0 + n], Wd[ci][0][0:16, 0:n], [Wd[ci][1]], RH[16:18])
            go = R_xn
            Rt = R_t
            Lt = Rt[:, 0:512]
            Eq = Rt[:, 512:1024].rearrange("p (h t) -> p h t", h=4)
            Ek = Rt[:, 1024:1536].rearrange("p (h t) -> p h t", h=4)
            Qf = Rt[:, 1536:2048].rearrange("p (h t) -> p h t", h=4)
            oT = Rt[:, 2048:3072].rearrange("p (c t) -> p c t", t=128)
            og = Rt[:, 3072:4096].rearrange("p (c t) -> p c t", t=128)
            ro = Rt[:, 4096:4608].rearrange("p (h t) -> p h t", h=4)
            bfv = Rt[:, 4608:6144].bitcast(BF16)
            Qt = bfv[:, 0:512].rearrange("p (h t) -> p h t", h=4)
            Kt = bfv[:, 512:1024].rearrange("p (h t) -> p h t", h=4)
            attm = bfv[:, 1024:1536].rearrange("p (h t) -> p h t", h=4)
            Ktok = bfv[:, 1536:2048].rearrange("p (h d) -> p h d", h=4)
            sqo = bfv[:, 2048:3072].rearrange("p (c t) -> p c t", t=128)
            Km = sq[:, 0, 0:512].rearrange("p (h d) -> p h d", h=4)
            tmpS = ah
            nLt, nEq, nEk, nQf, noT, nog, nro = ['R0'], ['R1'], ['R2'], ['R3'], ['R4', 'R5'], ['R6', 'R7'], ['R8']
            nQt, nKt, natt, nKtok, nsqo, nKm = ['R9'], ['R9'], ['R10'], ['R10'], ['R11'], ['sq0']
            nqT, nkT, nglr = RH[0:8], RH[8:16], RH[16:18]

            def stageA(ti):
                c0, n, smp = ttiles[ti]
                Um = Us if smp else U
                MM(PS[0][0:n, 0:512], glrA[0:17, c0:c0 + n], wg2a[0:17, :], True, True, nglr + ['wg2a'], ['B0'])
                ACT(Lt[0:n, :], PS[0][0:n, 0:512], AF.Exp, ['B0'], nLt, scale=-1.0)
                ACT(Lt[0:n, :], Lt[0:n, :], AF.Ln, nLt, nLt, bias=1.0)
                for h in range(4):
                    MM(PS[1][:, h * 128:h * 128 + n], Lt[0:n, h * 128:(h + 1) * 128], Um[0:n, 0:n], True, True,
                       nLt + ['cst'], ['B1'])
                cs3 = PS[1][:, 0:512].rearrange("p (h t) -> p h t", h=4)[:, :, 0:n]
                ACT(Eq[:, :, 0:n], cs3, AF.Exp, ['B1'], nEq, scale=-1.0 / 16)
                ACT(Ek[:, :, 0:n], cs3, AF.Exp, ['B1'], nEk, scale=1.0 / 16)
                TT('dve', Qt[:, :, 0:n], qT[:, :, c0:c0 + n], Eq[:, :, 0:n], ALU.mult, nqT + nEq, nQt)
                TT('dve', Kt[:, :, 0:n], kT[:, :, c0:c0 + n], Ek[:, :, 0:n], ALU.mult, nkT + nEk, nKt)
                if smp:
                    TT('dve', Qf[:, :, 0:n], qT[:, :, c0:c0 + n], Eq[:, :, 0:n], ALU.mult, nqT + nEq, nQf)
                for h in range(4):
                    MM(PS[2][0:n, h * 128:h * 128 + n], Kt[:, h, 0:n], Qt[:, h, 0:n], True, True, nKt + nQt, ['B2'])
                a3 = PS[2][0:n, 0:512].rearrange("p (h t) -> p h t", h=4)[:, :, 0:n]
                TT('dve', attm[0:n, :, 0:n], a3, Um[0:n, 0:n].unsqueeze(1).to_broadcast([n, 4, n]), ALU.mult,
                   ['B2', 'cst'], natt)
                p3b = PS[3][:].bitcast(BF16)
                for h in range(4):
                    TR(p3b[0:n, h * 128:(h + 1) * 128], Kt[:, h, 0:n], identb[:], nKt + ['identb'], ['B3'])
                CP('act', Ktok[0:n, :, :].rearrange("p h d -> p (h d)"), p3b[0:n, 0:512], ['B3'], nKtok)
            def stageB1(ti):
                c0, n, smp = ttiles[ti]
                Um = Us if smp else U
                if smp:
                    MS('dve', PS[4][:, 0:512], 0.0, ['B4'])
                    MS('dve', PS[5][:, 0:512], 0.0, ['B5'])
                for h in range(4):
                    for vc in range(2):
                        c8 = h * 2 + vc
                        bank = 4 + c8 // 4
                        o_ap = PS[bank][:, (c8 % 4) * 128:(c8 % 4) * 128 + n]
                        MM(o_ap, V[0:n, ti, h * 256 + vc * 128:h * 256 + vc * 128 + 128], attm[0:n, h, 0:n], not smp, smp,
                           ['y'] + natt, ['B%d' % bank])
                        if not smp:
                            MM(o_ap, Sgb[:, h, vc * 128:(vc + 1) * 128], Qt[:, h, 0:n], False, True, ['Sgb%d' % h] + nQt, ['B%d' % bank])
                if smp:
                    def s0names(s_):
                        return ['S0b%d_%d' % (s_ % 2, h_) for h_ in range(4)]

                    def s0load(s_):
                        DMA('sp', S0b[s_ % 2][:], sgla_d[s_].rearrange("h d v -> d h v"), [], s0names(s_), 'dm_S0b%d' % (s_ % 2))
                    s0load(0); s0load(1)
                    for s in range(16):
                        sb_ = S0b[s % 2]
                        for h in range(4):
                            for vc in range(2):
                                c8 = h * 2 + vc
                                bank = 4 + c8 // 4
                                MM(PS[bank][:, (c8 % 4) * 128 + s * 4:(c8 % 4) * 128 + s * 4 + 4],
                                   sb_[:, h, vc * 128:(vc + 1) * 128], Qf[:, h, s * 4:s * 4 + 4], False, True,
                                   ['S0b%d_%d' % (s % 2, h)] + nQf, ['B%d' % bank])
                        TT('pool', Km[0:n, :, :], Ktok[0:n, :, :], Msk[0:n, s:s + 1].unsqueeze(1).to_broadcast([n, 4, 128]),
                           ALU.mult, nKtok + ['cst'], nKm)
                        sbank = 6 + (s % 2)
                        sn_ = Snb[s % 2]
                        for h in range(4):
                            for hv in range(2):
                                pass
                        for h in range(4):
                            bk = 6 + h // 2
                            MM(PS[bk][:, (h % 2) * 256:(h % 2 + 1) * 256], Km[0:n, h, :], V[0:n, ti, h * 256:(h + 1) * 256],
                               True, True, nKm + ['y'], ['B%d' % bk])
                        for h in range(4):
                            bk = 6 + h // 2
                            eb = Eq[:, h, s * 4 + 3:s * 4 + 4]
                            ACT(tmpS[:, h, :], sb_[:, h, :], AF.Copy, ['S0b%d_%d' % (s % 2, h)] + nEq, ['ah%d' % h], scale=eb)
                            STT('dve', sn_[:, h, :], PS[bk][:, (h % 2) * 256:(h % 2 + 1) * 256], eb, tmpS[:, h, :], ALU.mult, ALU.add,
                                ['B%d' % bk, 'ah%d' % h] + nEq, ['S0b%d_%d' % (s % 2, h)])
                        DMA('sp', oglas[s].rearrange("h d v -> d h v"), sn_[:], s0names(s), [], 'dm_S0b%d' % (s % 2))
                        if s + 2 < 16:
                            s0load(s + 2)
                else:
                    for h in range(4):
                        bk = 6 + h // 2
                        MM(PS[bk][:, (h % 2) * 256:(h % 2 + 1) * 256], Ktok[0:n, h, :], V[0:n, ti, h * 256:(h + 1) * 256],
                           True, True, nKtok + ['y'], ['B%d' % bk])
                    for h in range(4):
                        eb = Eq[:, h, n - 1:n]
                        ACT(tmpS[:, h, :], Sg[:, h, :], AF.Copy, ['Sg%d' % h] + nEq, ['ah%d' % h], scale=eb)
                    for h in range(4):
                        bk = 6 + h // 2
                        eb = Eq[:, h, n - 1:n]
                        STT('dve', Sg[:, h, :], PS[bk][:, (h % 2) * 256:(h % 2 + 1) * 256], eb, tmpS[:, h, :], ALU.mult, ALU.add,
                            ['B%d' % bk, 'ah%d' % h] + nEq, ['Sg%d' % h])
                        CP('act', Sgb[:, h, :], Sg[:, h, :], ['Sg%d' % h], ['Sgb%d' % h])
            def stageB2(ti):
                c0, n, smp = ttiles[ti]
                Um = Us if smp else U
                o4 = lambda bk: PS[bk][:, 0:512].rearrange("p (c t) -> p c t", c=4)[:, :, 0:n]
                for bk in (4, 5):
                    CP('act', oT[:, (bk - 4) * 4:(bk - 4) * 4 + 4, 0:n], o4(bk), ['B%d' % bk], noT)
                    ACT(sqo[:, (bk - 4) * 4:(bk - 4) * 4 + 4, 0:n], o4(bk), AF.Square, ['B%d' % bk], nsqo)
                for h in range(4):
                    for vc in range(2):
                        MM(PS[0][:, h * 128:h * 128 + n], onesb[:], sqo[:, h * 2 + vc, 0:n], vc == 0, vc == 1, ['onesb'] + nsqo, ['B0'])
                n3 = PS[0][:, 0:512].rearrange("p (h t) -> p h t", h=4)[:, :, 0:n]
                ACT(ro[:, :, 0:n], n3, AF.Ln, ['B0', 'cns'], nro, bias=eps_ap, scale=1.0 / 256)
                ACT(ro[:, :, 0:n], ro[:, :, 0:n], AF.Exp, nro, nro, scale=-0.5)
                oT4 = oT.rearrange("p (h v) t -> p h v t", v=2)
                og4 = og.rearrange("p (h v) t -> p h v t", v=2)
                for vc in range(2):
                    STT('dve', og4[:, :, vc, 0:n], oT4[:, :, vc, 0:n], gon_ap(vc), ro[:, :, 0:n], ALU.mult, ALU.mult,
                        noT + ['prm'] + nro, nog)
                TT('pool', go[:, :, c0:c0 + n], og[:, :, 0:n], sr[:, :, c0:c0 + n], ALU.mult, nog + SR, XN)
            stageA(0)
            for ti in range(len(ttiles)):
                stageB1(ti)
                if ti + 1 < len(ttiles):
                    stageA(ti + 1)
                stageB2(ti)
            for i in range(4):
                wt, wres = Wst.get('wout%d' % i)
                for cc in range(2):
                    m = i * 2 + cc
                    par = m % 2
                    Wy = wide((0 + 2 * par, 4 + 2 * par))
                    for ci, (c0, n) in enumerate(cts):
                        for kc in range(KC):
                            MM(Wy[ci][0][:, 0:n], wt[:, kc * 256 + cc * 128:kc * 256 + cc * 128 + 128], go[:, kc, c0:c0 + n],
                               kc == 0, kc == KC - 1, [wres, 'xn%d' % kc], [Wy[ci][1]])
                        CP('act', R_y[:, m, c0:c0 + n], Wy[ci][0][:, 0:n], [Wy[ci][1]], ['y'])
                    stat_sq(m, [Wy[ci][0][:, 0:n] for ci, (c0, n) in enumerate(cts)], [[Wy[ci][1]] for ci in range(len(cts))], b)
                    if m > 0:
                        stat_mm(m - 1, b, (1, 5))
            stat_mm(KC - 1, b, (1, 5))
            postnorm_res(b, 0, 3, 1.0)
            if b == NB - 1:
                DMA('sp', oglap.rearrange("h d v -> d h v"), Sg[:], ['Sg%d' % h for h in range(4)], [], 'st_Sg')

        def s5(b):
            P.bg_all()
            cts = colt(b)
            nc_ = sum(n for _, n in cts)
            prenorm(b, 1, 2)
            u = R_xn
            RH = ['h%d' % j for j in range(FC)]
            nyS = RH[0:16]
            nzb = ['s0', 's1', 's2', 's3']
            yS = R_h[:].bitcast(F32)[:, 0:KC * TBM].rearrange("p (c t) -> p c t", t=TBM)
            zb = R_s[:].rearrange("p a t -> p (a t)").bitcast(BF16)[:, 0:KC * TBM].rearrange("p (c t) -> p c t", t=TBM)
            XT = [R_t[:, 0:1024], R_t[:, 1024:2048]]
            nXT = [['R0', 'R1'], ['R2', 'R3']]
            qv = R_t[:, 2048:4096].bitcast(BF16)
            Q = [qv[:, i * 1024:(i + 1) * 1024] for i in range(4)]
            nQ = [['R%d' % (4 + i)] for i in range(4)]
            Rper = R_t[:, 2048:3072]
            bv = R_t[:, 4096:6144].bitcast(BF16)
            BUb = [[bv[:, (p_ * 2 + ri) * 1024:(p_ * 2 + ri + 1) * 1024] for ri in range(2)] for p_ in range(2)]
            nBUb = [[['R%d' % (8 + p_ * 2 + ri)] for ri in range(2)] for p_ in range(2)]
            tv = R_t[:, 6144:7168].bitcast(BF16)
            T1 = tv[:, 0:1024]; T2 = tv[:, 1024:2048]
            nT1, nT2 = ['R12'], ['R13']
            HTb = [R_h[:, 16 * TBM:16 * TBM + 1024], R_h[:, 18 * TBM:18 * TBM + 1024]]
            nHTb = [['h16', 'h17'], ['h18', 'h19']]
            v3 = lambda a: a.rearrange("p (k j) -> p k j", j=LCH)
            v4 = lambda a: a.rearrange("p (k s j) -> p k s j", s=8, j=4)
            cosf = cosT[:].rearrange("p k j -> p (k j)"); sinf = sinT[:].rearrange("p k j -> p (k j)")
            sinNf = sinN[:].rearrange("p k j -> p (k j)")
            RTf = RT[:].rearrange("p k j -> p (k j)")
            chunks = [(i * LCH, False, 0) for i in range(PB // LCH)] + ([(512, True, 0), (544, True, 1)] if b == 0 else [])
            are = s5s[:, 0, :]; aim = s5s[:, 1, :]
            aLre = s5s[:, 6, :]; aLim = s5s[:, 7, :]
            cL1 = s5s[:, 8, :]; sL1 = s5s[:, 9, :]; c3_ = s5s[:, 10, :]; s3_ = s5s[:, 11, :]

            def stage_bu(ci_):
                c0, smp, half = chunks[ci_]
                par = ci_ % 2
                for ri in range(2):
                    for k in range(32):
                        bank = ri * 2 + k // 16
                        MM(PS[bank][:, (k % 16) * 32:(k % 16 + 1) * 32], WB[ri][:, k, :], u[:, k // 4, c0:c0 + LCH], True, True,
                           ['WB', 'xn%d' % (k // 4)], ['B%d' % bank])
                for ri in range(2):
                    for hh in range(2):
                        CP('act', BUb[par][ri][:, hh * 512:(hh + 1) * 512], PS[ri * 2 + hh][:, 0:512], ['B%d' % (ri * 2 + hh)], nBUb[par][ri])

            def stage_dve(ci_):
                c0, smp, half = chunks[ci_]
                par = ci_ % 2
                br = BUb[par][0]; bi = BUb[par][1]
                nbr = nBUb[par][0]; nbi = nBUb[par][1]
                if smp:
                    def tab(t3):
                        return t3[:, :, 0:4].unsqueeze(2).to_broadcast([128, 32, 8, 4])
                    cs_t, sn_t, snn_t = tab(cosT), tab(sinT), tab(sinN)
                    vv = v4
                else:
                    cs_t, sn_t, snn_t = cosf, sinf, sinNf
                    vv = lambda a: a
                TT('dve', vv(T1), vv(br), cs_t, ALU.mult, nbr + ['cosT'], nT1)
                TT('dve', vv(T2), vv(bi), sn_t, ALU.mult, nbi + ['sinT'], nT2)
                TT('dve', XT[0], T1, T2, ALU.add, nT1 + nT2, nXT[0])
                TT('dve', vv(T1), vv(bi), cs_t, ALU.mult, nbi + ['cosT'], nT1)
                TT('dve', vv(T2), vv(br), sn_t, ALU.mult, nbr + ['sinT'], nT2)
                TT('dve', XT[1], T1, T2, ALU.subtract, nT1 + nT2, nXT[1])
                if smp:
                    hr = H0[:, 0, half * 8:(half + 1) * 8, :].rearrange("p s k -> p k s")
                    hi = H0[:, 1, half * 8:(half + 1) * 8, :].rearrange("p s k -> p k s")
                    sh = [128, 32, 8]
                    ar3 = are.unsqueeze(2).to_broadcast(sh); ai3 = aim.unsqueeze(2).to_broadcast(sh)
                    w_ = lambda i: ah[:, i, :].rearrange("p (k s) -> p k s", s=8)
                    TT('pool', w_(0), hr, ar3, ALU.mult, ['H0', 's5s'], ['ah0'])
                    TT('pool', w_(1), hi, ai3, ALU.mult, ['H0', 's5s'], ['ah1'])
                    TT('pool', w_(0), w_(0), w_(1), ALU.subtract, ['ah0', 'ah1'], ['ah0'])
                    TT('pool', w_(2), hi, ar3, ALU.mult, ['H0', 's5s'], ['ah2'])
                    TT('pool', w_(3), hr, ai3, ALU.mult, ['H0', 's5s'], ['ah3'])
                    TT('pool', w_(2), w_(2), w_(3), ALU.add, ['ah2', 'ah3'], ['ah2'])
                    Ai = v4(XT[0])[:, :, :, 0]; Ci = v4(XT[1])[:, :, :, 0]
                    TT('dve', Ai, Ai, w_(0), ALU.add, nXT[0] + ['ah0'], nXT[0])
                    TT('dve', Ci, Ci, w_(2), ALU.add, nXT[1] + ['ah2'], nXT[1])
                    CP('pool', v4(Rper), RT[:, :, 0:4].unsqueeze(2).to_broadcast([128, 32, 8, 4]), ['RT'], nQ[0] + nQ[1])
                    rsc = Rper; rres = nQ[0] + nQ[1]
                else:
                    Ai = v3(XT[0])[:, :, 0]; Ci = v3(XT[1])[:, :, 0]
                    TT('dve', Ai, Ai, hp[:, 0, :], ALU.add, nXT[0] + ['hp'], nXT[0])
                    TT('dve', Ci, Ci, hp[:, 1, :], ALU.add, nXT[1] + ['hp'], nXT[1])
                    rsc = RTf; rres = ['RT']
                for ri in range(2):
                    P.op('dve', lambda e, o=XT[ri], d0=rsc: e.tensor_tensor_scan(out=o, data0=d0, data1=o, initial=0.0,
                                                                                  op0=ALU.mult, op1=ALU.add),
                         r=rres + nXT[ri], w=nXT[ri])
                    CP('act', HTb[ri], XT[ri], nXT[ri], nHTb[ri])
                if smp:
                    sh = [128, 32, 8]
                    c3b = c3_.unsqueeze(2).to_broadcast(sh); s3b = s3_.unsqueeze(2).to_broadcast(sh)
                    w_ = lambda i: ah[:, i, :].rearrange("p (k s) -> p k s", s=8)
                    hr3 = v4(XT[0])[:, :, :, 3]; hi3 = v4(XT[1])[:, :, :, 3]
                    Hr = Hn[:, 0, half * 8:(half + 1) * 8, :].rearrange("p s k -> p k s")
                    Hi = Hn[:, 1, half * 8:(half + 1) * 8, :].rearrange("p s k -> p k s")
                    TT('pool', w_(0), hr3, c3b, ALU.mult, nXT[0] + ['s5s'], ['ah0'])
                    TT('pool', w_(1), hi3, s3b, ALU.mult, nXT[1] + ['s5s'], ['ah1'])
                    TT('pool', Hr, w_(0), w_(1), ALU.subtract, ['ah0', 'ah1'], ['H0'])
                    TT('pool', w_(2), hi3, c3b, ALU.mult, nXT[1] + ['s5s'], ['ah2'])
                    TT('pool', w_(3), hr3, s3b, ALU.mult, nXT[0] + ['s5s'], ['ah3'])
                    TT('pool', Hi, w_(2), w_(3), ALU.add, ['ah2', 'ah3'], ['H0'])
                else:
                    w_ = lambda i: ah[:, i, 0:32]
                    hrl = v3(XT[0])[:, :, LCH - 1]; hil = v3(XT[1])[:, :, LCH - 1]
                    last = (b == NB - 1 and ci_ == PB // LCH - 1)
                    cre, cim = (cL1, sL1) if last else (aLre, aLim)
                    TT('pool', w_(0), hrl, cre, ALU.mult, nXT[0] + ['s5s'], ['ah0'])
                    TT('pool', w_(1), hil, cim, ALU.mult, nXT[1] + ['s5s'], ['ah1'])
                    TT('pool', hp[:, 0, :], w_(0), w_(1), ALU.subtract, ['ah0', 'ah1'], ['hp'])
                    TT('pool', w_(2), hil, cre, ALU.mult, nXT[1] + ['s5s'], ['ah2'])
                    TT('pool', w_(3), hrl, cim, ALU.mult, nXT[0] + ['s5s'], ['ah3'])
                    TT('pool', hp[:, 1, :], w_(2), w_(3), ALU.add, ['ah2', 'ah3'], ['hp'])
                TT('dve', vv(Q[0]), vv(HTb[0]), cs_t, ALU.mult, nHTb[0] + ['cosT'], nQ[0])
                TT('dve', vv(Q[3]), vv(HTb[0]), sn_t, ALU.mult, nHTb[0] + ['sinT'], nQ[3])
                TT('dve', vv(Q[1]), vv(HTb[1]), snn_t, ALU.mult, nHTb[1] + ['sinN'], nQ[1])
                TT('dve', vv(Q[2]), vv(HTb[1]), cs_t, ALU.mult, nHTb[1] + ['cosT'], nQ[2])

            def stage_y(ci_):
                c0, smp, half = chunks[ci_]
                ybank = 4 + ci_ % 2
                for kc in range(KC):
                    for q in range(4):
                        k = kc * 4 + q
                        for qi, wc in ((0, 0), (1, 0), (2, 1), (3, 1)):
                            MM(PS[ybank][:, kc * 32:(kc + 1) * 32], WC[wc][:, k, :], v3(Q[qi])[:, k, :],
                               q == 0 and qi == 0, q == 3 and qi == 3, ['WC'] + nQ[qi], ['B%d' % ybank])
                CP('act', yS[:, :, c0:c0 + LCH], PS[ybank][:, 0:256].rearrange("p (c j) -> p c j", j=LCH), ['B%d' % ybank], nyS)

            stage_bu(0)
            for ci_ in range(len(chunks)):
                if ci_ + 1 < len(chunks):
                    stage_bu(ci_ + 1)
                stage_dve(ci_)
                stage_y(ci_)
            for kc in range(KC):
                s = kc % 2
                STT('dve', tmpn[:, s, 0:nc_], x[:, kc, 0:nc_], g_ap(1, 2, kc), rstd[:, 0:nc_], ALU.mult, ALU.mult,
                    ['x%d' % kc, 'prm', 'rstd'], ['tmpn%d' % s])
                STT('dve', zb[:, kc, 0:nc_], tmpn[:, s, 0:nc_], d_ap(kc), yS[:, kc, 0:nc_], ALU.mult, ALU.add,
                    ['tmpn%d' % s, 'prm'] + nyS, nzb)
            sgb = tmpn
            for m in range(KC):
                wt, wres = Wst.get('wglu%d' % m)
                par = m % 2
                Wv = wide((0 + 2 * par, 4 + 2 * par)); Wg = wide((1 + 2 * par, 5 + 2 * par))
                for half_, Wd in ((0, Wv), (1, Wg)):
                    for ci, (c0, n) in enumerate(cts):
                        for kc in range(KC):
                            MM(Wd[ci][0][:, 0:n], wt[:, kc * 256 + half_ * 128:kc * 256 + half_ * 128 + 128], zb[:, kc, c0:c0 + n],
                               kc == 0, kc == KC - 1, [wres] + nzb, [Wd[ci][1]])
                for ci, (c0, n) in enumerate(cts):
                    ACT(sgb[:, par, c0:c0 + n], Wg[ci][0][:, 0:n], AF.Sigmoid, [Wg[ci][1], 'prm'], ['tmpn%d' % par], bias=bglu_ap(8 + m))
                    STT('dve', R_y[:, m, c0:c0 + n], Wv[ci][0][:, 0:n], bglu_ap(m), sgb[:, par, c0:c0 + n], ALU.add, ALU.mult,
                        [Wv[ci][1], 'prm', 'tmpn%d' % par], ['y'])
            postnorm_res(b, 1, 3, 1.0, banks=(6, 7), stats_done=False)
            if b == NB - 1:
                for ri in range(2):
                    TR(PS[0][0:32, ri * 128:(ri + 1) * 128], hp[:, ri, :], identf, ['hp', 'cst'], ['B0'])
                CP('dve', R_t[0:32, 0:256], PS[0][0:32, 0:256], ['B0'], ['R0'])
                DMA('sp', o5rep, R_t[0:32, 0:128], ['R0'], [], 'st_h1')
                DMA('sp', o5imp, R_t[0:32, 128:256], ['R0'], [], 'st_h2')
            if b == 0:
                for ri, dst in enumerate((o5res, o5ims)):
                    for j in range(4):
                        TR(PS[1 + ri][:, j * 128:(j + 1) * 128], Hn[:, ri, j * 4:(j + 1) * 4, :].rearrange("p s k -> p (s k)"), identf,
                           ['H0', 'cst'], ['B%d' % (1 + ri)])
                    buf = R_t[:, 1024 * (1 + ri):1024 * (1 + ri) + 512]
                    nb_ = ['R%d' % (2 * (1 + ri))]
                    CP('dve', buf, PS[1 + ri][:, 0:512], ['B%d' % (1 + ri)], nb_)
                    for j in range(4):
                        DMA('sp', dst[j * 128:(j + 1) * 128, :], buf[:, j * 128:(j + 1) * 128], nb_, [], 'st_hs%d' % ri)

        xres_all = ['x']
        for b in range(NB):
            DMA('sp', x[:, :, 0:512], xTp.rearrange("(kc p) t -> p kc t", p=128)[:, :, b * 512:(b + 1) * 512], [], XR, 'ld_x')
            if b == 0:
                DMA('sp', x[:, :, 512:576], xTs.rearrange("(kc p) t -> p kc t", p=128), [], XR, 'ld_xs')
            stages = [lambda: ffn(b, 0, 0), lambda: gla(b), lambda: ffn(b, 0, 1),
                      lambda: ffn(b, 1, 0), lambda: s5(b), lambda: ffn(b, 1, 1)]
            for si, st in enumerate(stages):
                if stop_after is not None and si > stop_after:
                    continue
                st()
            if stop_after is not None:
                Wst.n = (b + 1) * len(tiles)
                Wst.issued = max(Wst.issued, Wst.n)
            DMA('sp', yTp.rearrange("(kc p) t -> p kc t", p=128)[:, :, b * 512:(b + 1) * 512], x[:, :, 0:512], XR, [], 'st_x')
            if b == 0:
                DMA('sp', yTs.rearrange("(kc p) t -> p kc t", p=128), x[:, :, 512:576], XR, [], 'st_xs')
        P.wait_all('sp')

        sems = {e: es.enter_context(nc.semaphore("s_" + e)) for e in ENGS}
        dsem = {k: es.enter_context(nc.semaphore("d_" + k)) for k in P.dma_cnt}
        block = es.enter_context(nc.Block())
        P.replay(block, sems, dsem)
    return nc


def _consts():
    c = np.zeros((128, 512), np.float32)
    c[:, 0:128] = np.eye(128, dtype=np.float32)
    s = np.arange(128)
    c[:, 128:256] = (s[:, None] <= s[None, :]).astype(np.float32)
    same = (s[:, None] // 4) == (s[None, :] // 4)
    c[:, 256:384] = ((s[:, None] <= s[None, :]) & same).astype(np.float32)
    c[:, 384:384 + LCH] = np.arange(LCH, dtype=np.float32)[None, :]
    c[:, 416:432] = ((s[:, None] // 4) == np.arange(16)[None, :]).astype(np.float32)
    return c


def prepare_inputs(x_prompt, x_sample, state_gla, state_s5_re, state_s5_im, norm_g, w_ffn_gu, w_ffn_down,
                   gla_w_in, gla_w_g2, gla_b_g, gla_g_onorm, gla_w_out,
                   s5_lam_re, s5_lam_im, s5_log_dt, s5_b_re, s5_b_im, s5_c_re, s5_c_im, s5_d, s5_w_glu, s5_b_glu):
    f = lambda a: np.asarray(a, dtype=np.float32)
    WSa = build_wstream(f(w_ffn_gu), f(w_ffn_down), f(gla_w_in), f(gla_w_out), f(s5_w_glu))
    prm = np.zeros((128, 128), np.float32)
    prm[:, 0:96] = f(norm_g).reshape(2, 6, KC, 128).transpose(3, 0, 1, 2).reshape(128, 96)
    prm[:, 96:98] = f(gla_g_onorm)[0].reshape(2, 128).T
    prm[:, 98:106] = f(s5_d)[0].reshape(KC, 128).T
    prm[:, 106:122] = f(s5_b_glu)[0].reshape(16, 128).T
    wg2a = np.concatenate([f(gla_w_g2)[0], f(gla_b_g)[0][None, :]], axis=0)
    cst = _consts()
    s5p = np.concatenate([f(s5_lam_re)[0].reshape(32, 128), f(s5_lam_im)[0].reshape(32, 128),
                          np.repeat(f(s5_log_dt)[0].reshape(32, 2), 64, axis=1)], axis=1)
    bpad = np.zeros((2, 128, 32, 128), np.float32)
    cpad = np.zeros((2, 128, 32, 128), np.float32)
    for ri, (bb, cc) in enumerate(((f(s5_b_re)[0], f(s5_c_re)[0]), (f(s5_b_im)[0], f(s5_c_im)[0]))):
        for k in range(32):
            for g2 in range(2):
                g = 2 * k + g2
                col = 32 * (k % 4) + 16 * g2
                bpad[ri, g2 * 64:(g2 + 1) * 64, k, col:col + 16] = bb[g]
                cpad[ri, g2 * 64:(g2 + 1) * 64, k, col:col + 16] = cc[g].T
    bpad = bpad.reshape(2, 128, 4096); cpad = cpad.reshape(2, 128, 4096)
    in_maps = []
    for c in range(NCORE):
        m = {
            "xTp": np.ascontiguousarray(f(x_prompt)[c].T),
            "xTs": np.ascontiguousarray(f(x_sample)[16 * c:16 * c + 16].reshape(NS, D).T),
            "WS": WSa, "prm": prm, "wg2a": wg2a, "cst": cst,
            "sgla": np.ascontiguousarray(f(state_gla)[0, 16 * c:16 * c + 16]),
            "s5re": np.ascontiguousarray(f(state_s5_re)[0, 16 * c:16 * c + 16].reshape(512, 128)),
            "s5im": np.ascontiguousarray(f(state_s5_im)[0, 16 * c:16 * c + 16].reshape(512, 128)),
            "s5p": np.ascontiguousarray(s5p), "bpad": bpad, "cpad": cpad,
        }
        in_maps.append(m)
    return in_maps


def assemble(results):
    y_p = np.stack([r["yTp"].T for r in results]).astype(np.float32)
    y_s = np.concatenate([r["yTs"].T.reshape(16, 4, D) for r in results]).astype(np.float32)
    gla_p = np.stack([r["oglap"] for r in results])[None].astype(np.float32)
    re_p = np.stack([r["o5rep"].reshape(64, 64) for r in results])[None].astype(np.float32)
    im_p = np.stack([r["o5imp"].reshape(64, 64) for r in results])[None].astype(np.float32)
    gla_s = np.concatenate([r["oglas"] for r in results])[None].astype(np.float32)
    re_s = np.concatenate([r["o5res"].reshape(16, 64, 64) for r in results])[None].astype(np.float32)
    im_s = np.concatenate([r["o5ims"].reshape(16, 64, 64) for r in results])[None].astype(np.float32)
    return (y_p, y_s, gla_p, re_p, im_p, gla_s, re_s, im_s)


_NC_CACHE = {}


def kernel(**inputs):
    in_maps = prepare_inputs(**inputs)
    if 'nc' not in _NC_CACHE:
        _NC_CACHE['nc'] = build_nc()
    res = run_bass_kernel_spmd(_NC_CACHE['nc'], in_maps, core_ids=list(range(NCORE)))
    return assemble(res.results)
```

```python
import numpy as np
from contextlib import ExitStack
import concourse.bass as bass
import concourse.mybir as mybir
from concourse.bass_utils import run_bass_kernel_spmd

F32 = mybir.dt.float32
BF16 = mybir.dt.bfloat16
AF = mybir.ActivationFunctionType
ALU = mybir.AluOpType
ENGS = ['pe', 'act', 'dve', 'pool', 'sp']

NCORE = 8
D = 1024; KC = 8; DFF = 2816; FC = 22
NB = 4; PB = 512; NS = 64; TBM = 576
EPS = 1e-6
LCH = 32
MAGIC = 12582912.0
TWO_PI = float(2 * np.pi)


class Prog:
    def __init__(self):
        self.ops = {e: [] for e in ENGS}
        self.last_w = {}
        self.reads = {}
        self.seen = {e: {} for e in ENGS}
        self.dma_cnt = {}
        self.flag = {e: set() for e in ENGS}
        self.defer = None
        self.bg = []
        self.atomic_depth = 0

    def mark(self):
        if self.defer is not None:
            self.defer.append('P')

    def atomic_begin(self):
        self.atomic_depth += 1

    def atomic_end(self):
        self.atomic_depth -= 1
        if self.defer is not None and not self.atomic_depth:
            self.defer.append('G')

    def bg_step(self, k):
        while k > 0 and self.bg:
            it = self.bg.pop(0)
            if it == 'G':
                k -= 1
            elif it != 'P':
                self.op(*it)

    def bg_to_mark(self):
        while self.bg:
            it = self.bg.pop(0)
            if it == 'P':
                return
            if it != 'G':
                self.op(*it)

    def bg_all(self):
        while self.bg:
            it = self.bg.pop(0)
            if it not in ('G', 'P'):
                self.op(*it)

    def _need(self, eng, tok, waits, is_dma, raw):
        if tok is None:
            return
        if tok[0] == 'eng':
            _, X, idx = tok
            if X == eng and not is_dma:
                if eng == 'pe' or not raw:
                    return
            key = ('eng', X)
            if self.seen[eng].get(key, -1) >= idx:
                return
            self.seen[eng][key] = idx
            self.flag[X].add(idx)
            waits.append(tok)
        else:
            _, k, val = tok
            key = ('dma', k)
            if self.seen[eng].get(key, -1) >= val:
                return
            self.seen[eng][key] = val
            waits.append(tok)

    def op(self, eng, fn, r=(), w=(), dma=None):
        if self.defer is not None:
            self.defer.append((eng, fn, tuple(r), tuple(w), dma))
            if not self.atomic_depth:
                self.defer.append('G')
            return None
        idx = len(self.ops[eng])
        is_dma = dma is not None
        if is_dma:
            self.dma_cnt[dma] = self.dma_cnt.get(dma, 0) + 1
            tok = ('dma', dma, self.dma_cnt[dma] * 16)
        else:
            tok = ('eng', eng, idx)
        waits = []
        cand = {}

        def add(t, raw):
            if t is None:
                return
            if t[0] == 'eng':
                if t[1] == eng and not is_dma and (eng == 'pe' or not raw):
                    return
                k = ('eng', t[1])
            else:
                k = ('dma', t[1])
            if k not in cand or cand[k][2] < t[2]:
                cand[k] = t
        for res in r:
            add(self.last_w.get(res), True)
            if res[0] == 'B' and res[1:].isdigit():
                for t in self.reads.get(res, ()):
                    if t[0] == 'eng' and t[1] != eng:
                        add(t, False)
        for res in w:
            add(self.last_w.get(res), False)
            for t in self.reads.get(res, ()):
                add(t, False)
        for t in cand.values():
            self._need(eng, t, waits, is_dma, True)
        for res in r:
            self.reads.setdefault(res, []).append(tok)
        for res in w:
            self.last_w[res] = tok
            self.reads[res] = []
        self.ops[eng].append(dict(fn=fn, waits=waits, dma=dma))
        return tok

    def wait_all(self, eng):
        waits = []
        for k, c in self.dma_cnt.items():
            self._need(eng, ('dma', k, c * 16), waits, True, True)
        for X in ENGS:
            if X != eng:
                for idx in range(len(self.ops[X]) - 1, -1, -1):
                    if self.ops[X][idx]['dma'] is None and self.ops[X][idx]['fn'] is not None:
                        self._need(eng, ('eng', X, idx), waits, True, True)
                        break
        self.ops[eng].append(dict(fn=None, waits=waits, dma=None))

    def replay(self, block, sems, dma_sems):
        rank = {}
        for e in ENGS:
            rank[e] = {idx: i + 1 for i, idx in enumerate(sorted(self.flag[e]))}
        ops = self.ops
        flag = self.flag

        def run(name, e):
            for idx, o in enumerate(ops[name]):
                for t in o['waits']:
                    if t[0] == 'eng':
                        e.wait_ge(sems[t[1]], rank[t[1]][t[2]])
                    else:
                        e.wait_ge(dma_sems[t[1]], t[2])
                if o['fn'] is None:
                    continue
                ins = o['fn'](e)
                if o['dma'] is not None:
                    ins.then_inc(dma_sems[o['dma']], 16)
                elif idx in flag[name]:
                    ins.then_inc(sems[name], 1)

        block.tensor(lambda e: run('pe', e))
        block.scalar(lambda e: run('act', e))
        block.vector(lambda e: run('dve', e))
        block.gpsimd(lambda e: run('pool', e))
        block.sync(lambda e: run('sp', e))


def weight_tiles():
    t = []
    for l in range(2):
        for f in range(2):
            if f == 1:
                if l == 0:
                    t += [('win%d' % i, 2048) for i in range(8)]
                    t += [('wv%d' % i, 2048) for i in range(4)]
                    t += [('wglr', 128)]
                    t += [('wout%d' % i, 2048) for i in range(4)]
                else:
                    t += [('wglu%d' % i, 2048) for i in range(8)]
            t += [('gu%d_%d_%d' % (l, f, j), 2048) for j in range(FC)]
            t += [('dn%d_%d_%d' % (l, f, m), 2816) for m in range(KC)]
    return t


def build_wstream(w_ffn_gu, w_ffn_down, gla_w_in, gla_w_out, s5_w_glu):
    def kmaj(w):
        C = w.shape[1]
        return w.reshape(KC, 128, C).transpose(1, 0, 2).reshape(128, KC * C)
    parts = []
    win = gla_w_in[0]
    for l in range(2):
        for f in range(2):
            if f == 1:
                if l == 0:
                    fm = np.concatenate([win[:, 0:1024], win[:, 2048:3072]], axis=1)
                    for i in range(8):
                        parts.append(kmaj(fm[:, i * 256:(i + 1) * 256]))
                    for i in range(4):
                        parts.append(kmaj(win[:, 1024 + i * 256:1024 + (i + 1) * 256]))
                    parts.append(kmaj(win[:, 3072:3088]))
                    for i in range(4):
                        parts.append(kmaj(gla_w_out[0][:, i * 256:(i + 1) * 256]))
                else:
                    wg = s5_w_glu[0]
                    for m in range(8):
                        parts.append(kmaj(np.concatenate([wg[:, m * 128:(m + 1) * 128],
                                                          wg[:, 1024 + m * 128:1024 + (m + 1) * 128]], axis=1)))
            gu = w_ffn_gu[l, f]
            for j in range(FC):
                parts.append(kmaj(np.concatenate([gu[:, j * 128:(j + 1) * 128],
                                                  gu[:, DFF + j * 128:DFF + (j + 1) * 128]], axis=1)))
            dn = w_ffn_down[l, f]
            for m in range(KC):
                parts.append(dn[:, m * 128:(m + 1) * 128].reshape(FC, 128, 128).transpose(1, 0, 2).reshape(128, FC * 128))
    return np.ascontiguousarray(np.concatenate(parts, axis=1), dtype=np.float32)


def build_nc(stop_after=None):
    nc = bass.Bass("TRN2", target_bir_lowering=False)
    tiles = weight_tiles()
    offs = np.cumsum([0] + [f for _, f in tiles]).tolist()
    TOT = offs[-1]

    def din(name, shape):
        return nc.dram_tensor(name, shape, F32, kind="ExternalInput").ap()

    def dout(name, shape):
        return nc.dram_tensor(name, shape, F32, kind="ExternalOutput").ap()

    xTp = din("xTp", [D, 2048]); xTs = din("xTs", [D, NS])
    WS = din("WS", [128, TOT])
    prm_d = din("prm", [128, 128]); wg2a_d = din("wg2a", [17, 512])
    cst_d = din("cst", [128, 512])
    sgla_d = din("sgla", [16, 4, 128, 256])
    s5re_d = din("s5re", [512, 128]); s5im_d = din("s5im", [512, 128])
    s5p_d = din("s5p", [32, 384])
    bpad_d = din("bpad", [2, 128, 4096]); cpad_d = din("cpad", [2, 128, 4096])
    yTp = dout("yTp", [D, 2048]); yTs = dout("yTs", [D, NS])
    oglap = dout("oglap", [4, 128, 256]); oglas = dout("oglas", [16, 4, 128, 256])
    o5rep = dout("o5rep", [32, 128]); o5imp = dout("o5imp", [32, 128])
    o5res = dout("o5res", [512, 128]); o5ims = dout("o5ims", [512, 128])

    P = Prog()
    es = ExitStack()
    with es:
        def sb(name, shape, dt=F32):
            return es.enter_context(nc.sbuf_tensor(name, shape, dt))

        x = sb("x", [128, KC, TBM])
        prm = sb("prm_s", [128, 128]); wg2a = sb("wg2a_s", [17, 512]); cst = sb("cst_s", [128, 512])
        identb = sb("identb", [128, 128], BF16); onesb = sb("onesb", [128, 128], BF16)
        cns = sb("cns", [128, 4])
        Sg = sb("Sg", [128, 4, 256]); Sgb = sb("Sgb", [128, 4, 256], BF16)
        WB = [sb("WBre", [128, 32, 128], BF16), sb("WBim", [128, 32, 128], BF16)]
        WC = [sb("WCre", [128, 32, 128], BF16), sb("WCim", [128, 32, 128], BF16)]
        cosT = sb("cosT", [128, 32, LCH], BF16); sinT = sb("sinT", [128, 32, LCH], BF16); sinN = sb("sinN", [128, 32, LCH], BF16)
        RT = sb("RT", [128, 32, LCH])
        s5s = sb("s5s", [128, 12, 32])
        hp = sb("hp", [128, 2, 32])
        ah = sb("ah", [128, 4, 256])
        H0 = sb("H0", [128, 2, 16, 32]); Hn = H0
        NSL = 3
        wsl = [sb("wsl%d" % i, [128, 2816], BF16) for i in range(NSL)]
        R_xn = sb("R_xn", [128, KC, TBM], BF16)
        R_h = sb("R_h", [128, FC * TBM], BF16)
        R_y = sb("R_y", [128, KC, TBM])
        R_s = sb("R_s", [128, 4, TBM])
        sq = sb("sq", [128, 2, TBM], BF16)
        rstd = sb("rstd", [128, TBM]); rsq = sb("rsq", [128, TBM]); tmpn = sb("tmpn", [128, 2, TBM])
        R_t = sb("R_t", [128, 7168])
        S0b = [sb("S0b%d" % i, [128, 4, 256]) for i in range(2)]
        Snb = S0b
        RTALL = ["R%d" % i for i in range(14)]
        def seg(a, b_):
            return ["R%d" % i for i in range(a // 512, (b_ + 511) // 512)]
        PS = [es.enter_context(nc.psum_tensor("B%d" % i, [128, 512], F32)) for i in range(8)]

        identf = cst[:, 0:128]; U = cst[:, 128:256]; Us = cst[:, 256:384]
        iotaj = cst[:, 384:384 + LCH]; Msk = cst[:, 416:432]
        g_ap = lambda l, n, kc: prm[:, (l * 6 + n) * 8 + kc:(l * 6 + n) * 8 + kc + 1]
        gon_ap = lambda vc: prm[:, 96 + vc:97 + vc]
        d_ap = lambda kc: prm[:, 98 + kc:99 + kc]
        bglu_ap = lambda i: prm[:, 106 + i:107 + i]

        def MM(out, lhsT, rhs, start, stop, r, w):
            P.op('pe', lambda e: e.matmul(out, lhsT=lhsT, rhs=rhs, start=start, stop=stop), r=r, w=w)

        def TR(out, in_, ident, r, w):
            P.op('pe', lambda e: e.transpose(out, in_, ident), r=r, w=w)

        def ACT(out, in_, func, r, w, bias=None, scale=None):
            kw = {}
            if bias is not None: kw['bias'] = bias
            if scale is not None: kw['scale'] = scale
            P.op('act', lambda e: e.activation(out=out, in_=in_, func=func, **kw), r=r, w=w)

        def STT(eng, out, in0, scalar, in1, op0, op1, r, w):
            eng = 'dve'
            P.op(eng, lambda e: e.scalar_tensor_tensor(out=out, in0=in0, scalar=scalar, in1=in1, op0=op0, op1=op1), r=r, w=w)

        def TT(eng, out, in0, in1, op, r, w):
            P.op(eng, lambda e: e.tensor_tensor(out=out, in0=in0, in1=in1, op=op), r=r, w=w)

        def TS(eng, out, in0, s1, s2, op0, op1, r, w):
            if s2 is None:
                P.op(eng, lambda e: e.tensor_single_scalar(out=out, in_=in0, scalar=s1, op=op0), r=r, w=w)
            else:
                P.op(eng, lambda e: e.tensor_scalar(out=out, in0=in0, scalar1=s1, scalar2=s2, op0=op0, op1=op1), r=r, w=w)

        def CP(eng, out, in_, r, w):
            if eng == 'act':
                P.op('act', lambda e: e.copy(out=out, in_=in_), r=r, w=w)
            else:
                P.op(eng, lambda e: e.tensor_copy(out=out, in_=in_), r=r, w=w)

        def MS(eng, ap, val, w):
            P.op(eng, lambda e: e.memset(ap, val), w=w)

        def DMA(eng, out, in_, r, w, key):
            P.op(eng, lambda e: e.dma_start(out=out, in_=in_), r=r, w=w, dma=key)

        def RECIP(out, in_, r, w):
            P.op('dve', lambda e: e.reciprocal(out=out, in_=in_), r=r, w=w)

        class WStream:
            def __init__(self):
                self.n = 0
                self.issued = 0
                self.total = NB * len(tiles)

            def _issue(self, i):
                ti = i % len(tiles)
                F = tiles[ti][1]
                s = i % NSL
                DMA('pool', wsl[s][:, 0:F], WS[:, offs[ti]:offs[ti] + F], r=[], w=['ws%d' % s], key='ws%d' % s)

            def get(self, name):
                i = self.n
                assert tiles[i % len(tiles)][0] == name, (tiles[i % len(tiles)][0], name)
                while self.issued < min(i + NSL, self.total):
                    self._issue(self.issued)
                    self.issued += 1
                self.n += 1
                s = i % NSL
                return wsl[s], 'ws%d' % s

        Wst = WStream()

        DMA('sp', prm[:], prm_d, [], ['prm'], 'ld_prm')
        DMA('sp', wg2a[:], wg2a_d, [], ['wg2a'], 'ld_wg2a')
        DMA('sp', cst[:], cst_d, [], ['cst'], 'ld_cst')
        CP('dve', identb[:], identf, ['cst'], ['identb'])
        MS('dve', onesb[:], 1.0, ['onesb'])
        MS('dve', cns[:, 0:1], EPS, ['cns'])
        MS('dve', cns[:, 1:2], 4 * EPS, ['cns'])
        XR = ['x%d' % k for k in range(KC)]
        MS('dve', Sg[:], 0.0, ['Sg%d' % h for h in range(4)])
        MS('dve', Sgb[:], 0.0, ['Sgb%d' % h for h in range(4)])
        MS('dve', hp[:], 0.0, ['hp'])
        eps_ap = cns[:, 0:1]

        def colt(b):
            return [(0, 512)] + ([(512, 64)] if b == 0 else [])

        def wide(banks):
            main, aux = banks
            return [(PS[main][:, 0:512], 'B%d' % main), (PS[aux][:, 0:64], 'B%d' % aux)]

        def s5_setup():
            sp = R_t
            lam_re = sp[0:32, 0:128]; lam_im = sp[0:32, 128:256]; ldt = sp[0:32, 256:384]
            DMA('sp', sp[0:32, 0:384], s5p_d, [], RTALL, 'ld_s5p')
            t = lambda i: sp[0:32, 384 + i * 128:384 + (i + 1) * 128]
            dt_, mag, ang, den, are, aim, fre, fim, tA, tB, sn, cs_ = [t(i) for i in range(12)]
            aLre, aLim, cL1, sL1, c3_, s3_, angx = [t(12 + i) for i in range(7)]
            rw = dict(r=RTALL, w=RTALL)
            ACT(dt_, ldt, AF.Exp, **rw)
            TT('dve', den, lam_re, dt_, ALU.mult, **rw)
            TS('dve', mag, den, 1.0 / 7, 1.0, ALU.mult, ALU.add, **rw)
            for c_ in (6, 5, 4, 3, 2, 1):
                TT('dve', mag, mag, den, ALU.mult, **rw)
                TS('dve', mag, mag, 1.0 / c_ if c_ > 1 else 1.0, 1.0, ALU.mult, ALU.add, **rw)
            TT('dve', ang, lam_im, dt_, ALU.mult, **rw)

            def sincos(dst, src, shift):
                TS('dve', tA, src, float(1 / TWO_PI), float(shift / TWO_PI), ALU.mult, ALU.add, **rw)
                TS('dve', tA, tA, MAGIC, None, ALU.add, None, **rw)
                TS('dve', tA, tA, -MAGIC, -TWO_PI, ALU.add, ALU.mult, **rw)
                STT('dve', tB, src, 1.0, tA, ALU.mult, ALU.add, **rw)
                if shift != 0.0:
                    TS('dve', tB, tB, float(shift), None, ALU.add, None, **rw)
                ACT(dst, tB, AF.Sin, **rw)
            sincos(sn, ang, 0.0)
            sincos(cs_, ang, float(np.pi / 2))
            TT('dve', are, mag, cs_, ALU.mult, **rw)
            TT('dve', aim, mag, sn, ALU.mult, **rw)
            TT('dve', den, lam_re, lam_re, ALU.mult, **rw)
            TT('dve', tA, lam_im, lam_im, ALU.mult, **rw)
            TT('dve', den, den, tA, ALU.add, **rw)
            RECIP(den, den, **rw)
            TS('dve', tB, are, -1.0, None, ALU.add, None, **rw)
            TT('dve', fre, tB, lam_re, ALU.mult, **rw)
            TT('dve', tA, aim, lam_im, ALU.mult, **rw)
            TT('dve', fre, fre, tA, ALU.add, **rw)
            TT('dve', fre, fre, den, ALU.mult, **rw)
            TT('dve', fim, aim, lam_re, ALU.mult, **rw)
            TT('dve', tA, tB, lam_im, ALU.mult, **rw)
            TT('dve', fim, fim, tA, ALU.subtract, **rw)
            TT('dve', fim, fim, den, ALU.mult, **rw)
            for mult_, (sdst, cdst) in ((float(LCH), (aLim, aLre)), (float(LCH - 1), (sL1, cL1)), (3.0, (s3_, c3_))):
                TS('dve', angx, ang, mult_, None, ALU.mult, None, **rw)
                sincos(sdst, angx, 0.0)
                sincos(cdst, angx, float(np.pi / 2))
            TT('dve', aLre, aLre, mag, ALU.mult, **rw)
            TT('dve', aLim, aLim, mag, ALU.mult, **rw)
            P.atomic_begin()
            for i, src in enumerate([are, aim, fre, fim, ang, mag, aLre, aLim, cL1, sL1, c3_, s3_]):
                TR(PS[0][:, i * 32:(i + 1) * 32], src, identf[0:32, 0:32], RTALL + ['cst'], ['B0'])
            CP('dve', s5s[:].rearrange("p a k -> p (a k)"), PS[0][:, 0:384], ['B0'], ['s5s'])
            P.atomic_end()
            P.mark()
            th = s5s[:, 4, :]; rr_ = s5s[:, 5, :]
            A3 = R_t[:, 0:1024].rearrange("p (k j) -> p k j", j=LCH)
            B3 = R_t[:, 1024:2048].rearrange("p (k j) -> p k j", j=LCH)
            C3 = R_t[:, 2048:3072].rearrange("p (k j) -> p k j", j=LCH)
            io3 = iotaj.unsqueeze(1).to_broadcast([128, 32, LCH])
            th3 = th.unsqueeze(2).to_broadcast([128, 32, LCH])
            TT('dve', A3, io3, th3, ALU.mult, ['cst', 's5s'] + RTALL, RTALL)
            for dst, shift in ((sinT, 0.0), (cosT, float(np.pi / 2))):
                TS('dve', B3, A3, float(1 / TWO_PI), float(shift / TWO_PI), ALU.mult, ALU.add, **rw)
                TS('dve', B3, B3, MAGIC, None, ALU.add, None, **rw)
                TS('dve', B3, B3, -MAGIC, -TWO_PI, ALU.add, ALU.mult, **rw)
                TT('dve', C3, A3, B3, ALU.add, **rw)
                if shift != 0.0:
                    TS('dve', C3, C3, float(shift), None, ALU.add, None, **rw)
                ACT(dst[:], C3, AF.Sin, RTALL, [dst is sinT and 'sinT' or 'cosT'])
                if dst is sinT:
                    ACT(sinN[:], C3, AF.Sin, RTALL, ['sinN'], scale=-1.0)
            r3 = rr_.unsqueeze(2).to_broadcast([128, 32, LCH])
            CP('dve', RT[:], r3, ['s5s'], ['RT'])
            MS('dve', RT[:, :, 0:1], 0.0, ['RT'])
            P.mark()
            bre = R_t[:, 0:1024]; bim = R_t[:, 1024:2048]; o1 = R_t[:, 2048:3072]; o2 = R_t[:, 3072:4096]
            for kg in range(4):
                DMA('sp', bre, bpad_d[0, :, kg * 1024:(kg + 1) * 1024], [], RTALL, 'ld_b0')
                DMA('sp', bim, bpad_d[1, :, kg * 1024:(kg + 1) * 1024], [], RTALL, 'ld_b1')
                v3 = lambda a: a.rearrange("p (k c) -> p k c", c=128)
                f_re3 = s5s[:, 2, kg * 8:(kg + 1) * 8].unsqueeze(2).to_broadcast([128, 8, 128])
                f_im3 = s5s[:, 3, kg * 8:(kg + 1) * 8].unsqueeze(2).to_broadcast([128, 8, 128])
                rs = RTALL + ['s5s']
                TT('dve', v3(o1), v3(bre), f_re3, ALU.mult, rs, RTALL)
                TT('dve', v3(o2), v3(bim), f_im3, ALU.mult, rs, RTALL)
                TT('dve', o1, o1, o2, ALU.subtract, rs, RTALL)
                TT('dve', v3(o2), v3(bre), f_im3, ALU.mult, rs, RTALL)
                TT('dve', v3(bre), v3(bim), f_re3, ALU.mult, rs, RTALL)
                TT('dve', o2, o2, bre, ALU.add, rs, RTALL)
                for ri, src in enumerate((o1, o2)):
                    P.atomic_begin()
                    WBB = ((2, 3), (4, 6))
                    for kk in range(8):
                        bi = WBB[ri][kk // 4]
                        TR(PS[bi][:, (kk % 4) * 128:(kk % 4 + 1) * 128], src[:, kk * 128:(kk + 1) * 128], identf,
                           RTALL + ['cst'], ['B%d' % bi])
                    for hh in range(2):
                        bi = WBB[ri][hh]
                        CP('act', WB[ri][:, kg * 8 + hh * 4:kg * 8 + hh * 4 + 4, :].rearrange("p k c -> p (k c)"),
                           PS[bi][:, 0:512], ['B%d' % bi], ['WB'])
                    P.atomic_end()
                P.mark()
            for kg in range(4):
                DMA('sp', bre, cpad_d[0, :, kg * 1024:(kg + 1) * 1024], [], RTALL, 'ld_b0')
                DMA('sp', bim, cpad_d[1, :, kg * 1024:(kg + 1) * 1024], [], RTALL, 'ld_b1')
                CP('act', WC[0][:, kg * 8:(kg + 1) * 8, :].rearrange("p k c -> p (k c)"), bre, RTALL, ['WC'])
                P.op('act', lambda e, kg=kg: e.mul(out=WC[1][:, kg * 8:(kg + 1) * 8, :].rearrange("p k c -> p (k c)"),
                                                    in_=bim, mul=-1.0), r=RTALL, w=['WC'])
                P.mark()
            for ri, src in enumerate((s5re_d, s5im_d)):
                bufs = [R_t[:, 4096 + j * 128:4096 + (j + 1) * 128] for j in range(4)]
                for j in range(4):
                    DMA('sp', bufs[j], src[j * 128:(j + 1) * 128, :], [], RTALL, 'ld_h0_%d' % j)
                P.atomic_begin()
                hb_ = (7, 0)[ri]
                for j in range(4):
                    TR(PS[hb_][:, j * 128:(j + 1) * 128], bufs[j], identf, RTALL + ['cst'], ['B%d' % hb_])
                CP('dve', H0[:, ri, :, :].rearrange("p s k -> p (s k)"), PS[hb_][:, 0:512], ['B%d' % hb_], ['H0'])
                P.atomic_end()
                P.mark()

        P.defer = []
        s5_setup()
        P.bg, P.defer = P.defer, None

        def RECIPF(out, in_, r, w):
            P.op('dve', lambda e: e.reciprocal_approx_fast(out=out, in_=in_), r=r, w=w)

        def stat_sq(kc, src_aps, src_res, b):
            s_ = kc % 2
            for ci, (c0, n) in enumerate(colt(b)):
                ACT(sq[:, s_, c0:c0 + n], src_aps[ci], AF.Square, src_res[ci], ['sq%d' % s_])

        def stat_mm(kc, b, banks):
            W = wide(banks)
            s_ = kc % 2
            for ci, (c0, n) in enumerate(colt(b)):
                MM(W[ci][0][:, 0:n], onesb[:], sq[:, s_, c0:c0 + n], kc == 0, kc == KC - 1,
                   ['onesb', 'sq%d' % s_], [W[ci][1]])

        def stat_fin(b, banks, scale):
            W = wide(banks)
            cts = colt(b)
            nc_ = sum(n for _, n in cts)
            bias = eps_ap if scale == 1.0 else cns[:, 1:2]
            assert scale in (1.0, 0.5)
            for ci, (c0, n) in enumerate(cts):
                ACT(rsq[:, c0:c0 + n], W[ci][0][:, 0:n], AF.Ln, [W[ci][1], 'cns'], ['rsq'], bias=bias,
                    scale=1.0 / (D * scale * scale))
            ACT(rstd[:, 0:nc_], rsq[:, 0:nc_], AF.Exp, ['rsq'], ['rstd'], scale=-0.5)

        def prenorm(b, l, n_idx):
            cts = colt(b)
            nc_ = sum(n for _, n in cts)
            for kc in range(KC):
                stat_sq(kc, [x[:, kc, c0:c0 + n] for c0, n in cts], [['x%d' % kc]] * len(cts), b)
                stat_mm(kc, b, (6, 7))
            stat_fin(b, (6, 7), 1.0)
            for kc in range(KC):
                STT('dve', R_xn[:, kc, 0:nc_], x[:, kc, 0:nc_], g_ap(l, n_idx, kc), rstd[:, 0:nc_],
                    ALU.mult, ALU.mult, ['x%d' % kc, 'prm', 'rstd'], ['xn%d' % kc])

        def postnorm_res(b, l, n_idx, scale, banks=(1, 5), stats_done=True):
            cts = colt(b)
            nc_ = sum(n for _, n in cts)
            if not stats_done:
                for kc in range(KC):
                    stat_sq(kc, [R_y[:, kc, c0:c0 + n] for c0, n in cts], [['y']] * len(cts), b)
                    stat_mm(kc, b, banks)
            stat_fin(b, banks, scale)
            for kc in range(KC):
                s_ = kc % 2
                STT('dve', tmpn[:, s_, 0:nc_], R_y[:, kc, 0:nc_], g_ap(l, n_idx, kc), rstd[:, 0:nc_], ALU.mult, ALU.mult,
                    ['y', 'prm', 'rstd'], ['tmpn%d' % s_])
                TT('dve', x[:, kc, 0:nc_], x[:, kc, 0:nc_], tmpn[:, s_, 0:nc_], ALU.add,
                   ['tmpn%d' % s_, 'x%d' % kc], ['x%d' % kc])

        def ffn(b, l, f):
            cts = colt(b)
            nc_ = sum(n for _, n in cts)
            prenorm(b, l, 0 if f == 0 else 4)
            hview = R_h[:, 0:FC * TBM].rearrange("p (j t) -> p j t", t=TBM)
            for j in range(FC):
                P.bg_step(4)
                wt, wres = Wst.get('gu%d_%d_%d' % (l, f, j))
                par = j % 2
                Wg = wide((0 + 2 * par, 4 + 2 * par)); Wu = wide((1 + 2 * par, 5 + 2 * par))
                for half, Wd in ((0, Wg), (1, Wu)):
                    for ci, (c0, n) in enumerate(cts):
                        for kc in range(KC):
                            MM(Wd[ci][0][:, 0:n], wt[:, kc * 256 + half * 128:kc * 256 + half * 128 + 128],
                               R_xn[:, kc, c0:c0 + n], kc == 0, kc == KC - 1, [wres, 'xn%d' % kc], [Wd[ci][1]])
                for ci, (c0, n) in enumerate(cts):
                    ACT(R_s[:, par, c0:c0 + n], Wg[ci][0][:, 0:n], AF.Silu, [Wg[ci][1]], ['s%d' % par])
                    TT('dve', hview[:, j, c0:c0 + n], R_s[:, par, c0:c0 + n], Wu[ci][0][:, 0:n], ALU.mult,
                       ['s%d' % par, Wu[ci][1]], ['h%d' % j])
            for m in range(KC):
                P.bg_step(4)
                wt, wres = Wst.get('dn%d_%d_%d' % (l, f, m))
                par = m % 2
                Wy = wide((0 + 2 * par, 4 + 2 * par))
                for ci, (c0, n) in enumerate(cts):
                    for j in range(FC):
                        MM(Wy[ci][0][:, 0:n], wt[:, j * 128:(j + 1) * 128], hview[:, j, c0:c0 + n], j == 0, j == FC - 1,
                           [wres, 'h%d' % j], [Wy[ci][1]])
                    CP('act', R_y[:, m, c0:c0 + n], Wy[ci][0][:, 0:n], [Wy[ci][1]], ['y'])
                stat_sq(m, [Wy[ci][0][:, 0:n] for ci, (c0, n) in enumerate(cts)], [[Wy[ci][1]] for ci in range(len(cts))], b)
                if m > 0:
                    stat_mm(m - 1, b, (1, 5))
            stat_mm(KC - 1, b, (1, 5))
            postnorm_res(b, l, 1 if f == 0 else 5, 0.5)

        def gla(b):
            P.bg_to_mark()
            cts = colt(b)
            nc_ = sum(n for _, n in cts)
            prenorm(b, 0, 2)
            hn = R_xn
            qT = R_h[:].bitcast(F32)[:, 0:4 * TBM].rearrange("p (h t) -> p h t", t=TBM)
            kT = R_h[:].bitcast(F32)[:, 4 * TBM:8 * TBM].rearrange("p (h t) -> p h t", t=TBM)
            glrA = R_h[:].bitcast(F32)[0:17, 8 * TBM:9 * TBM]
            sr = R_s[:].rearrange("p a t -> p (a t)").bitcast(BF16)[:, 0:KC * TBM].rearrange("p (c t) -> p c t", t=TBM)
            V = R_y[:].rearrange("p a t -> p (a t)").bitcast(BF16)[:, 0:5 * 1024].rearrange("p (a v) -> p a v", v=1024)
            RH = ['h%d' % j for j in range(FC)]
            SR = ['s0', 's1', 's2', 's3']
            XN = ['xn%d' % k for k in range(KC)]
            MS('pool', glrA, 1.0, RH[16:18])
            for i in range(8):
                wt, wres = Wst.get('win%d' % i)
                for cc in range(2):
                    ch = i * 2 + cc
                    par = ch % 2
                    Wd = wide((0 + 2 * par, 4 + 2 * par))
                    for ci, (c0, n) in enumerate(cts):
                        for kc in range(KC):
                            MM(Wd[ci][0][:, 0:n], wt[:, kc * 256 + cc * 128:kc * 256 + cc * 128 + 128], hn[:, kc, c0:c0 + n],
                               kc == 0, kc == KC - 1, [wres, 'xn%d' % kc], [Wd[ci][1]])
                        if ch < 4:
                            ACT(qT[:, ch, c0:c0 + n], Wd[ci][0][:, 0:n], AF.Copy, [Wd[ci][1]], RH[0:8], scale=float(128 ** -0.5))
                        elif ch < 8:
                            CP('act', kT[:, ch - 4, c0:c0 + n], Wd[ci][0][:, 0:n], [Wd[ci][1]], RH[8:16])
                        else:
                            ACT(sr[:, ch - 8, c0:c0 + n], Wd[ci][0][:, 0:n], AF.Silu, [Wd[ci][1]], SR)
            ttiles = [(i * 128, 128, False) for i in range(4)] + ([(512, 64, True)] if b == 0 else [])
            for vt in range(4):
                wt, wres = Wst.get('wv%d' % vt)
                for ti, (c0, n, smp) in enumerate(ttiles):
                    bank = 1 + (ti % 2) * 2
                    for kc in range(KC):
                        MM(PS[bank][0:n, 0:256], hn[:, kc, c0:c0 + n], wt[:, kc * 256:(kc + 1) * 256], kc == 0, kc == KC - 1,
                           [wres, 'xn%d' % kc], ['B%d' % bank])
                    CP('act' if ti % 2 == 0 else 'dve', V[0:n, ti, vt * 256:(vt + 1) * 256], PS[bank][0:n, 0:256], ['B%d' % bank], ['y'])
            wt, wres = Wst.get('wglr')
            for ci, (c0, n) in enumerate(cts):
                Wd = wide((0, 4))
                for kc in range(KC):
                    MM(Wd[ci][0][0:16, 0:n], wt[:, kc * 16:(kc + 1) * 16], hn[:, kc, c0:c0 + n], kc == 0, kc == KC - 1,
                       [wres, 'xn%d' % kc], [Wd[ci][1]])
                CP('act', glrA[0:16, c0:c0 + n], Wd[ci][0][0:16, 0:n], [Wd[ci][1]], RH[16:18])
            go = R_xn
            Rt = R_t
            Lt = Rt[:, 0:512]
            Eq = Rt[:, 512:1024].rearrange("p (h t) -> p h t", h=4)
            Ek = Rt[:, 1024:1536].rearrange("p (h t) -> p h t", h=4)
            Qf = Rt[:, 1536:2048].rearrange("p (h t) -> p h t", h=4)
            oT = Rt[:, 2048:3072].rearrange("p (c t) -> p c t", t=128)
            og = Rt[:, 3072:4096].rearrange("p (c t) -> p c t", t=128)
            ro = Rt[:, 4096:4608].rearrange("p (h t) -> p h t", h=4)
            bfv = Rt[:, 4608:6144].bitcast(BF16)
            Qt = bfv[:, 0:512].rearrange("p (h t) -> p h t", h=4)
            Kt = bfv[:, 512:1024].rearrange("p (h t) -> p h t", h=4)
            attm = bfv[:, 1024:1536].rearrange("p (h t) -> p h t", h=4)
            Ktok = bfv[:, 1536:2048].rearrange("p (h d) -> p h d", h=4)
            sqo = bfv[:, 2048:3072].rearrange("p (c t) -> p c t", t=128)
            Km = sq[:, 0, 0:512].rearrange("p (h d) -> p h d", h=4)
            tmpS = ah
            nLt, nEq, nEk, nQf, noT, nog, nro = ['R0'], ['R1'], ['R2'], ['R3'], ['R4', 'R5'], ['R6', 'R7'], ['R8']
            nQt, nKt, natt, nKtok, nsqo, nKm = ['R9'], ['R9'], ['R10'], ['R10'], ['R11'], ['sq0']
            nqT, nkT, nglr = RH[0:8], RH[8:16], RH[16:18]

            def stageA(ti):
                c0, n, smp = ttiles[ti]
                Um = Us if smp else U
                MM(PS[0][0:n, 0:512], glrA[0:17, c0:c0 + n], wg2a[0:17, :], True, True, nglr + ['wg2a'], ['B0'])
                ACT(Lt[0:n, :], PS[0][0:n, 0:512], AF.Exp, ['B0'], nLt, scale=-1.0)
                ACT(Lt[0:n, :], Lt[0:n, :], AF.Ln, nLt, nLt, bias=1.0)
                for h in range(4):
                    MM(PS[1][:, h * 128:h * 128 + n], Lt[0:n, h * 128:(h + 1) * 128], Um[0:n, 0:n], True, True,
                       nLt + ['cst'], ['B1'])
                cs3 = PS[1][:, 0:512].rearrange("p (h t) -> p h t", h=4)[:, :, 0:n]
                ACT(Eq[:, :, 0:n], cs3, AF.Exp, ['B1'], nEq, scale=-1.0 / 16)
                ACT(Ek[:, :, 0:n], cs3, AF.Exp, ['B1'], nEk, scale=1.0 / 16)
                TT('dve', Qt[:, :, 0:n], qT[:, :, c0:c0 + n], Eq[:, :, 0:n], ALU.mult, nqT + nEq, nQt)
                TT('dve', Kt[:, :, 0:n], kT[:, :, c0:c0 + n], Ek[:, :, 0:n], ALU.mult, nkT + nEk, nKt)
                if smp:
                    TT('dve', Qf[:, :, 0:n], qT[:, :, c0:c0 + n], Eq[:, :, 0:n], ALU.mult, nqT + nEq, nQf)
                for h in range(4):
                    MM(PS[2][0:n, h * 128:h * 128 + n], Kt[:, h, 0:n], Qt[:, h, 0:n], True, True, nKt + nQt, ['B2'])
                a3 = PS[2][0:n, 0:512].rearrange("p (h t) -> p h t", h=4)[:, :, 0:n]
                TT('dve', attm[0:n, :, 0:n], a3, Um[0:n, 0:n].unsqueeze(1).to_broadcast([n, 4, n]), ALU.mult,
                   ['B2', 'cst'], natt)
                p3b = PS[3][:].bitcast(BF16)
                for h in range(4):
                    TR(p3b[0:n, h * 128:(h + 1) * 128], Kt[:, h, 0:n], identb[:], nKt + ['identb'], ['B3'])
                CP('act', Ktok[0:n, :, :].rearrange("p h d -> p (h d)"), p3b[0:n, 0:512], ['B3'], nKtok)
            def stageB1(ti):
                c0, n, smp = ttiles[ti]
                Um = Us if smp else U
                if smp:
                    MS('dve', PS[4][:, 0:512], 0.0, ['B4'])
                    MS('dve', PS[5][:, 0:512], 0.0, ['B5'])
                for h in range(4):
                    for vc in range(2):
                        c8 = h * 2 + vc
                        bank = 4 + c8 // 4
                        o_ap = PS[bank][:, (c8 % 4) * 128:(c8 % 4) * 128 + n]
                        MM(o_ap, V[0:n, ti, h * 256 + vc * 128:h * 256 + vc * 128 + 128], attm[0:n, h, 0:n], not smp, smp,
                           ['y'] + natt, ['B%d' % bank])
                        if not smp:
                            MM(o_ap, Sgb[:, h, vc * 128:(vc + 1) * 128], Qt[:, h, 0:n], False, True, ['Sgb%d' % h] + nQt, ['B%d' % bank])
                if smp:
                    def s0names(s_):
                        return ['S0b%d_%d' % (s_ % 2, h_) for h_ in range(4)]

                    def s0load(s_):
                        DMA('sp', S0b[s_ % 2][:], sgla_d[s_].rearrange("h d v -> d h v"), [], s0names(s_), 'dm_S0b%d' % (s_ % 2))
                    s0load(0); s0load(1)
                    for s in range(16):
                        sb_ = S0b[s % 2]
                        for h in range(4):
                            for vc in range(2):
                                c8 = h * 2 + vc
                                bank = 4 + c8 // 4
                                MM(PS[bank][:, (c8 % 4) * 128 + s * 4:(c8 % 4) * 128 + s * 4 + 4],
                                   sb_[:, h, vc * 128:(vc + 1) * 128], Qf[:, h, s * 4:s * 4 + 4], False, True,
                                   ['S0b%d_%d' % (s % 2, h)] + nQf, ['B%d' % bank])
                        TT('pool', Km[0:n, :, :], Ktok[0:n, :, :], Msk[0:n, s:s + 1].unsqueeze(1).to_broadcast([n, 4, 128]),
                           ALU.mult, nKtok + ['cst'], nKm)
                        sbank = 6 + (s % 2)
                        sn_ = Snb[s % 2]
                        for h in range(4):
                            for hv in range(2):
                                pass
                        for h in range(4):
                            bk = 6 + h // 2
                            MM(PS[bk][:, (h % 2) * 256:(h % 2 + 1) * 256], Km[0:n, h, :], V[0:n, ti, h * 256:(h + 1) * 256],
                               True, True, nKm + ['y'], ['B%d' % bk])
                        for h in range(4):
                            bk = 6 + h // 2
                            eb = Eq[:, h, s * 4 + 3:s * 4 + 4]
                            ACT(tmpS[:, h, :], sb_[:, h, :], AF.Copy, ['S0b%d_%d' % (s % 2, h)] + nEq, ['ah%d' % h], scale=eb)
                            STT('dve', sn_[:, h, :], PS[bk][:, (h % 2) * 256:(h % 2 + 1) * 256], eb, tmpS[:, h, :], ALU.mult, ALU.add,
                                ['B%d' % bk, 'ah%d' % h] + nEq, ['S0b%d_%d' % (s % 2, h)])
                        DMA('sp', oglas[s].rearrange("h d v -> d h v"), sn_[:], s0names(s), [], 'dm_S0b%d' % (s % 2))
                        if s + 2 < 16:
                            s0load(s + 2)
                else:
                    for h in range(4):
                        bk = 6 + h // 2
                        MM(PS[bk][:, (h % 2) * 256:(h % 2 + 1) * 256], Ktok[0:n, h, :], V[0:n, ti, h * 256:(h + 1) * 256],
                           True, True, nKtok + ['y'], ['B%d' % bk])
                    for h in range(4):
                        eb = Eq[:, h, n - 1:n]
                        TS('dve', tmpS[:, h, :], Sg[:, h, :], eb, None, ALU.mult, None, ['Sg%d' % h] + nEq, ['ah%d' % h])
                    for h in range(4):
                        bk = 6 + h // 2
                        eb = Eq[:, h, n - 1:n]
                        STT('dve', Sg[:, h, :], PS[bk][:, (h % 2) * 256:(h % 2 + 1) * 256], eb, tmpS[:, h, :], ALU.mult, ALU.add,
                            ['B%d' % bk, 'ah%d' % h] + nEq, ['Sg%d' % h])
                        CP('dve', Sgb[:, h, :], Sg[:, h, :], ['Sg%d' % h], ['Sgb%d' % h])
            def stageB2(ti):
                c0, n, smp = ttiles[ti]
                Um = Us if smp else U
                o4 = lambda bk: PS[bk][:, 0:512].rearrange("p (c t) -> p c t", c=4)[:, :, 0:n]
                for bk in (4, 5):
                    CP('act', oT[:, (bk - 4) * 4:(bk - 4) * 4 + 4, 0:n], o4(bk), ['B%d' % bk], noT)
                    ACT(sqo[:, (bk - 4) * 4:(bk - 4) * 4 + 4, 0:n], o4(bk), AF.Square, ['B%d' % bk], nsqo)
                for h in range(4):
                    for vc in range(2):
                        MM(PS[0][:, h * 128:h * 128 + n], onesb[:], sqo[:, h * 2 + vc, 0:n], vc == 0, vc == 1, ['onesb'] + nsqo, ['B0'])
                n3 = PS[0][:, 0:512].rearrange("p (h t) -> p h t", h=4)[:, :, 0:n]
                ACT(ro[:, :, 0:n], n3, AF.Ln, ['B0', 'cns'], nro, bias=eps_ap, scale=1.0 / 256)
                ACT(ro[:, :, 0:n], ro[:, :, 0:n], AF.Exp, nro, nro, scale=-0.5)
                oT4 = oT.rearrange("p (h v) t -> p h v t", v=2)
                og4 = og.rearrange("p (h v) t -> p h v t", v=2)
                for vc in range(2):
                    STT('dve', og4[:, :, vc, 0:n], oT4[:, :, vc, 0:n], gon_ap(vc), ro[:, :, 0:n], ALU.mult, ALU.mult,
                        noT + ['prm'] + nro, nog)
                TT('pool', go[:, :, c0:c0 + n], og[:, :, 0:n], sr[:, :, c0:c0 + n], ALU.mult, nog + SR, XN)
            stageA(0)
            for ti in range(len(ttiles)):
                stageB1(ti)
                if ti + 1 < len(ttiles):
                    stageA(ti + 1)
                stageB2(ti)
            for i in range(4):
                wt, wres = Wst.get('wout%d' % i)
                for cc in range(2):
                    m = i * 2 + cc
                    par = m % 2
                    Wy = wide((0 + 2 * par, 4 + 2 * par))
                    for ci, (c0, n) in enumerate(cts):
                        for kc in range(KC):
                            MM(Wy[ci][0][:, 0:n], wt[:, kc * 256 + cc * 128:kc * 256 + cc * 128 + 128], go[:, kc, c0:c0 + n],
                               kc == 0, kc == KC - 1, [wres, 'xn%d' % kc], [Wy[ci][1]])
                        CP('act', R_y[:, m, c0:c0 + n], Wy[ci][0][:, 0:n], [Wy[ci][1]], ['y'])
                    stat_sq(m, [Wy[ci][0][:, 0:n] for ci, (c0, n) in enumerate(cts)], [[Wy[ci][1]] for ci in range(len(cts))], b)
                    if m > 0:
                        stat_mm(m - 1, b, (1, 5))
            stat_mm(KC - 1, b, (1, 5))
            postnorm_res(b, 0, 3, 1.0)
            if b == NB - 1:
                DMA('sp', oglap.rearrange("h d v -> d h v"), Sg[:], ['Sg%d' % h for h in range(4)], [], 'st_Sg')

        def s5(b):
            P.bg_all()
            cts = colt(b)
            nc_ = sum(n for _, n in cts)
            prenorm(b, 1, 2)
            u = R_xn
            RH = ['h%d' % j for j in range(FC)]
            nyS = RH[0:16]
            nzb = ['s0', 's1', 's2', 's3']
            yS = R_h[:].bitcast(F32)[:, 0:KC * TBM].rearrange("p (c t) -> p c t", t=TBM)
            zb = R_s[:].rearrange("p a t -> p (a t)").bitcast(BF16)[:, 0:KC * TBM].rearrange("p (c t) -> p c t", t=TBM)
            XT = [R_t[:, 0:1024], R_t[:, 1024:2048]]
            nXT = [['R0', 'R1'], ['R2', 'R3']]
            qv = R_t[:, 2048:4096].bitcast(BF16)
            Q = [qv[:, i * 1024:(i + 1) * 1024] for i in range(4)]
            nQ = [['R%d' % (4 + i)] for i in range(4)]
            Rper = R_t[:, 2048:3072]
            bv = R_t[:, 4096:6144].bitcast(BF16)
            BUb = [[bv[:, (p_ * 2 + ri) * 1024:(p_ * 2 + ri + 1) * 1024] for ri in range(2)] for p_ in range(2)]
            nBUb = [[['R%d' % (8 + p_ * 2 + ri)] for ri in range(2)] for p_ in range(2)]
            tv = R_t[:, 6144:7168].bitcast(BF16)
            T1 = tv[:, 0:1024]; T2 = tv[:, 1024:2048]
            nT1, nT2 = ['R12'], ['R13']
            HTb = [R_h[:, 16 * TBM:16 * TBM + 1024], R_h[:, 18 * TBM:18 * TBM + 1024]]
            nHTb = [['h16', 'h17'], ['h18', 'h19']]
            v3 = lambda a: a.rearrange("p (k j) -> p k j", j=LCH)
            v4 = lambda a: a.rearrange("p (k s j) -> p k s j", s=8, j=4)
            cosf = cosT[:].rearrange("p k j -> p (k j)"); sinf = sinT[:].rearrange("p k j -> p (k j)")
            sinNf = sinN[:].rearrange("p k j -> p (k j)")
            RTf = RT[:].rearrange("p k j -> p (k j)")
            chunks = [(i * LCH, False, 0) for i in range(PB // LCH)] + ([(512, True, 0), (544, True, 1)] if b == 0 else [])
            are = s5s[:, 0, :]; aim = s5s[:, 1, :]
            aLre = s5s[:, 6, :]; aLim = s5s[:, 7, :]
            cL1 = s5s[:, 8, :]; sL1 = s5s[:, 9, :]; c3_ = s5s[:, 10, :]; s3_ = s5s[:, 11, :]

            def stage_bu(ci_):
                c0, smp, half = chunks[ci_]
                par = ci_ % 2
                for ri in range(2):
                    for k in range(32):
                        bank = ri * 2 + k // 16
                        MM(PS[bank][:, (k % 16) * 32:(k % 16 + 1) * 32], WB[ri][:, k, :], u[:, k // 4, c0:c0 + LCH], True, True,
                           ['WB', 'xn%d' % (k // 4)], ['B%d' % bank])
                for ri in range(2):
                    for hh in range(2):
                        CP('act', BUb[par][ri][:, hh * 512:(hh + 1) * 512], PS[ri * 2 + hh][:, 0:512], ['B%d' % (ri * 2 + hh)], nBUb[par][ri])

            def stage_dve(ci_):
                c0, smp, half = chunks[ci_]
                par = ci_ % 2
                br = BUb[par][0]; bi = BUb[par][1]
                nbr = nBUb[par][0]; nbi = nBUb[par][1]
                if smp:
                    def tab(t3):
                        return t3[:, :, 0:4].unsqueeze(2).to_broadcast([128, 32, 8, 4])
                    cs_t, sn_t, snn_t = tab(cosT), tab(sinT), tab(sinN)
                    vv = v4
                else:
                    cs_t, sn_t, snn_t = cosf, sinf, sinNf
                    vv = lambda a: a
                TT('dve', vv(T1), vv(br), cs_t, ALU.mult, nbr + ['cosT'], nT1)
                TT('dve', vv(T2), vv(bi), sn_t, ALU.mult, nbi + ['sinT'], nT2)
                TT('dve', XT[0], T1, T2, ALU.add, nT1 + nT2, nXT[0])
                TT('dve', vv(T1), vv(bi), cs_t, ALU.mult, nbi + ['cosT'], nT1)
                TT('dve', vv(T2), vv(br), sn_t, ALU.mult, nbr + ['sinT'], nT2)
                TT('dve', XT[1], T1, T2, ALU.subtract, nT1 + nT2, nXT[1])
                if smp:
                    hr = H0[:, 0, half * 8:(half + 1) * 8, :].rearrange("p s k -> p k s")
                    hi = H0[:, 1, half * 8:(half + 1) * 8, :].rearrange("p s k -> p k s")
                    sh = [128, 32, 8]
                    ar3 = are.unsqueeze(2).to_broadcast(sh); ai3 = aim.unsqueeze(2).to_broadcast(sh)
                    w_ = lambda i: ah[:, i, :].rearrange("p (k s) -> p k s", s=8)
                    TT('pool', w_(0), hr, ar3, ALU.mult, ['H0', 's5s'], ['ah0'])
                    TT('pool', w_(1), hi, ai3, ALU.mult, ['H0', 's5s'], ['ah1'])
                    TT('pool', w_(0), w_(0), w_(1), ALU.subtract, ['ah0', 'ah1'], ['ah0'])
                    TT('pool', w_(2), hi, ar3, ALU.mult, ['H0', 's5s'], ['ah2'])
                    TT('pool', w_(3), hr, ai3, ALU.mult, ['H0', 's5s'], ['ah3'])
                    TT('pool', w_(2), w_(2), w_(3), ALU.add, ['ah2', 'ah3'], ['ah2'])
                    Ai = v4(XT[0])[:, :, :, 0]; Ci = v4(XT[1])[:, :, :, 0]
                    TT('dve', Ai, Ai, w_(0), ALU.add, nXT[0] + ['ah0'], nXT[0])
                    TT('dve', Ci, Ci, w_(2), ALU.add, nXT[1] + ['ah2'], nXT[1])
                    CP('pool', v4(Rper), RT[:, :, 0:4].unsqueeze(2).to_broadcast([128, 32, 8, 4]), ['RT'], nQ[0] + nQ[1])
                    rsc = Rper; rres = nQ[0] + nQ[1]
                else:
                    Ai = v3(XT[0])[:, :, 0]; Ci = v3(XT[1])[:, :, 0]
                    TT('dve', Ai, Ai, hp[:, 0, :], ALU.add, nXT[0] + ['hp'], nXT[0])
                    TT('dve', Ci, Ci, hp[:, 1, :], ALU.add, nXT[1] + ['hp'], nXT[1])
                    rsc = RTf; rres = ['RT']
                for ri in range(2):
                    P.op('dve', lambda e, o=XT[ri], d0=rsc: e.tensor_tensor_scan(out=o, data0=d0, data1=o, initial=0.0,
                                                                                  op0=ALU.mult, op1=ALU.add),
                         r=rres + nXT[ri], w=nXT[ri])
                    CP('act', HTb[ri], XT[ri], nXT[ri], nHTb[ri])
                if smp:
                    sh = [128, 32, 8]
                    c3b = c3_.unsqueeze(2).to_broadcast(sh); s3b = s3_.unsqueeze(2).to_broadcast(sh)
                    w_ = lambda i: ah[:, i, :].rearrange("p (k s) -> p k s", s=8)
                    hr3 = v4(XT[0])[:, :, :, 3]; hi3 = v4(XT[1])[:, :, :, 3]
                    Hr = Hn[:, 0, half * 8:(half + 1) * 8, :].rearrange("p s k -> p k s")
                    Hi = Hn[:, 1, half * 8:(half + 1) * 8, :].rearrange("p s k -> p k s")
                    TT('pool', w_(0), hr3, c3b, ALU.mult, nXT[0] + ['s5s'], ['ah0'])
                    TT('pool', w_(1), hi3, s3b, ALU.mult, nXT[1] + ['s5s'], ['ah1'])
                    TT('pool', Hr, w_(0), w_(1), ALU.subtract, ['ah0', 'ah1'], ['H0'])
                    TT('pool', w_(2), hi3, c3b, ALU.mult, nXT[1] + ['s5s'], ['ah2'])
                    TT('pool', w_(3), hr3, s3b, ALU.mult, nXT[0] + ['s5s'], ['ah3'])
                    TT('pool', Hi, w_(2), w_(3), ALU.add, ['ah2', 'ah3'], ['H0'])
                else:
                    w_ = lambda i: ah[:, i, 0:32]
                    hrl = v3(XT[0])[:, :, LCH - 1]; hil = v3(XT[1])[:, :, LCH - 1]
                    last = (b == NB - 1 and ci_ == PB // LCH - 1)
                    cre, cim = (cL1, sL1) if last else (aLre, aLim)
                    TT('pool', w_(0), hrl, cre, ALU.mult, nXT[0] + ['s5s'], ['ah0'])
                    TT('pool', w_(1), hil, cim, ALU.mult, nXT[1] + ['s5s'], ['ah1'])
                    TT('pool', hp[:, 0, :], w_(0), w_(1), ALU.subtract, ['ah0', 'ah1'], ['hp'])
                    TT('pool', w_(2), hil, cre, ALU.mult, nXT[1] + ['s5s'], ['ah2'])
                    TT('pool', w_(3), hrl, cim, ALU.mult, nXT[0] + ['s5s'], ['ah3'])
                    TT('pool', hp[:, 1, :], w_(2), w_(3), ALU.add, ['ah2', 'ah3'], ['hp'])
                TT('dve', vv(Q[0]), vv(HTb[0]), cs_t, ALU.mult, nHTb[0] + ['cosT'], nQ[0])
                TT('dve', vv(Q[3]), vv(HTb[0]), sn_t, ALU.mult, nHTb[0] + ['sinT'], nQ[3])
                TT('dve', vv(Q[1]), vv(HTb[1]), snn_t, ALU.mult, nHTb[1] + ['sinN'], nQ[1])
                TT('dve', vv(Q[2]), vv(HTb[1]), cs_t, ALU.mult, nHTb[1] + ['cosT'], nQ[2])

            def stage_y(ci_):
                c0, smp, half = chunks[ci_]
                ybank = 4 + ci_ % 2
                for kc in range(KC):
                    for q in range(4):
                        k = kc * 4 + q
                        for qi, wc in ((0, 0), (1, 0), (2, 1), (3, 1)):
                            MM(PS[ybank][:, kc * 32:(kc + 1) * 32], WC[wc][:, k, :], v3(Q[qi])[:, k, :],
                               q == 0 and qi == 0, q == 3 and qi == 3, ['WC'] + nQ[qi], ['B%d' % ybank])
                CP('act', yS[:, :, c0:c0 + LCH], PS[ybank][:, 0:256].rearrange("p (c j) -> p c j", j=LCH), ['B%d' % ybank], nyS)

            stage_bu(0)
            for ci_ in range(len(chunks)):
                if ci_ + 1 < len(chunks):
                    stage_bu(ci_ + 1)
                stage_dve(ci_)
                stage_y(ci_)
            for kc in range(KC):
                s = kc % 2
                STT('dve', tmpn[:, s, 0:nc_], x[:, kc, 0:nc_], g_ap(1, 2, kc), rstd[:, 0:nc_], ALU.mult, ALU.mult,
                    ['x%d' % kc, 'prm', 'rstd'], ['tmpn%d' % s])
                STT('dve', zb[:, kc, 0:nc_], tmpn[:, s, 0:nc_], d_ap(kc), yS[:, kc, 0:nc_], ALU.mult, ALU.add,
                    ['tmpn%d' % s, 'prm'] + nyS, nzb)
            sgb = tmpn
            for m in range(KC):
                wt, wres = Wst.get('wglu%d' % m)
                par = m % 2
                Wv = wide((0 + 2 * par, 4 + 2 * par)); Wg = wide((1 + 2 * par, 5 + 2 * par))
                for half_, Wd in ((0, Wv), (1, Wg)):
                    for ci, (c0, n) in enumerate(cts):
                        for kc in range(KC):
                            MM(Wd[ci][0][:, 0:n], wt[:, kc * 256 + half_ * 128:kc * 256 + half_ * 128 + 128], zb[:, kc, c0:c0 + n],
                               kc == 0, kc == KC - 1, [wres] + nzb, [Wd[ci][1]])
                for ci, (c0, n) in enumerate(cts):
                    ACT(sgb[:, par, c0:c0 + n], Wg[ci][0][:, 0:n], AF.Sigmoid, [Wg[ci][1], 'prm'], ['tmpn%d' % par], bias=bglu_ap(8 + m))
                    STT('dve', R_y[:, m, c0:c0 + n], Wv[ci][0][:, 0:n], bglu_ap(m), sgb[:, par, c0:c0 + n], ALU.add, ALU.mult,
                        [Wv[ci][1], 'prm', 'tmpn%d' % par], ['y'])
            postnorm_res(b, 1, 3, 1.0, banks=(6, 7), stats_done=False)
            if b == NB - 1:
                for ri in range(2):
                    TR(PS[0][0:32, ri * 128:(ri + 1) * 128], hp[:, ri, :], identf, ['hp', 'cst'], ['B0'])
                CP('dve', R_t[0:32, 0:256], PS[0][0:32, 0:256], ['B0'], ['R0'])
                DMA('sp', o5rep, R_t[0:32, 0:128], ['R0'], [], 'st_h1')
                DMA('sp', o5imp, R_t[0:32, 128:256], ['R0'], [], 'st_h2')
            if b == 0:
                for ri, dst in enumerate((o5res, o5ims)):
                    for j in range(4):
                        TR(PS[1 + ri][:, j * 128:(j + 1) * 128], Hn[:, ri, j * 4:(j + 1) * 4, :].rearrange("p s k -> p (s k)"), identf,
                           ['H0', 'cst'], ['B%d' % (1 + ri)])
                    buf = R_t[:, 1024 * (1 + ri):1024 * (1 + ri) + 512]
                    nb_ = ['R%d' % (2 * (1 + ri))]
                    CP('dve', buf, PS[1 + ri][:, 0:512], ['B%d' % (1 + ri)], nb_)
                    for j in range(4):
                        DMA('sp', dst[j * 128:(j + 1) * 128, :], buf[:, j * 128:(j + 1) * 128], nb_, [], 'st_hs%d' % ri)

        xres_all = ['x']
        for b in range(NB):
            DMA('sp', x[:, :, 0:512], xTp.rearrange("(kc p) t -> p kc t", p=128)[:, :, b * 512:(b + 1) * 512], [], XR, 'ld_x')
            if b == 0:
                DMA('sp', x[:, :, 512:576], xTs.rearrange("(kc p) t -> p kc t", p=128), [], XR, 'ld_xs')
            stages = [lambda: ffn(b, 0, 0), lambda: gla(b), lambda: ffn(b, 0, 1),
                      lambda: ffn(b, 1, 0), lambda: s5(b), lambda: ffn(b, 1, 1)]
            for si, st in enumerate(stages):
                if stop_after is not None and si > stop_after:
                    continue
                st()
            if stop_after is not None:
                Wst.n = (b + 1) * len(tiles)
                Wst.issued = max(Wst.issued, Wst.n)
            DMA('sp', yTp.rearrange("(kc p) t -> p kc t", p=128)[:, :, b * 512:(b + 1) * 512], x[:, :, 0:512], XR, [], 'st_x')
            if b == 0:
                DMA('sp', yTs.rearrange("(kc p) t -> p kc t", p=128), x[:, :, 512:576], XR, [], 'st_xs')
        P.wait_all('sp')

        sems = {e: es.enter_context(nc.semaphore("s_" + e)) for e in ENGS}
        dsem = {k: es.enter_context(nc.semaphore("d_" + k)) for k in P.dma_cnt}
        block = es.enter_context(nc.Block())
        P.replay(block, sems, dsem)
    return nc


def _consts():
    c = np.zeros((128, 512), np.float32)
    c[:, 0:128] = np.eye(128, dtype=np.float32)
    s = np.arange(128)
    c[:, 128:256] = (s[:, None] <= s[None, :]).astype(np.float32)
    same = (s[:, None] // 4) == (s[None, :] // 4)
    c[:, 256:384] = ((s[:, None] <= s[None, :]) & same).astype(np.float32)
    c[:, 384:384 + LCH] = np.arange(LCH, dtype=np.float32)[None, :]
    c[:, 416:432] = ((s[:, None] // 4) == np.arange(16)[None, :]).astype(np.float32)
    return c


def prepare_inputs(x_prompt, x_sample, state_gla, state_s5_re, state_s5_im, norm_g, w_ffn_gu, w_ffn_down,
                   gla_w_in, gla_w_g2, gla_b_g, gla_g_onorm, gla_w_out,
                   s5_lam_re, s5_lam_im, s5_log_dt, s5_b_re, s5_b_im, s5_c_re, s5_c_im, s5_d, s5_w_glu, s5_b_glu):
    f = lambda a: np.asarray(a, dtype=np.float32)
    WSa = build_wstream(f(w_ffn_gu), f(w_ffn_down), f(gla_w_in), f(gla_w_out), f(s5_w_glu))
    prm = np.zeros((128, 128), np.float32)
    prm[:, 0:96] = f(norm_g).reshape(2, 6, KC, 128).transpose(3, 0, 1, 2).reshape(128, 96)
    prm[:, 96:98] = f(gla_g_onorm)[0].reshape(2, 128).T
    prm[:, 98:106] = f(s5_d)[0].reshape(KC, 128).T
    prm[:, 106:122] = f(s5_b_glu)[0].reshape(16, 128).T
    wg2a = np.concatenate([f(gla_w_g2)[0], f(gla_b_g)[0][None, :]], axis=0)
    cst = _consts()
    s5p = np.concatenate([f(s5_lam_re)[0].reshape(32, 128), f(s5_lam_im)[0].reshape(32, 128),
                          np.repeat(f(s5_log_dt)[0].reshape(32, 2), 64, axis=1)], axis=1)
    bpad = np.zeros((2, 128, 32, 128), np.float32)
    cpad = np.zeros((2, 128, 32, 128), np.float32)
    for ri, (bb, cc) in enumerate(((f(s5_b_re)[0], f(s5_c_re)[0]), (f(s5_b_im)[0], f(s5_c_im)[0]))):
        for k in range(32):
            for g2 in range(2):
                g = 2 * k + g2
                col = 32 * (k % 4) + 16 * g2
                bpad[ri, g2 * 64:(g2 + 1) * 64, k, col:col + 16] = bb[g]
                cpad[ri, g2 * 64:(g2 + 1) * 64, k, col:col + 16] = cc[g].T
    bpad = bpad.reshape(2, 128, 4096); cpad = cpad.reshape(2, 128, 4096)
    in_maps = []
    for c in range(NCORE):
        m = {
            "xTp": np.ascontiguousarray(f(x_prompt)[c].T),
            "xTs": np.ascontiguousarray(f(x_sample)[16 * c:16 * c + 16].reshape(NS, D).T),
            "WS": WSa, "prm": prm, "wg2a": wg2a, "cst": cst,
            "sgla": np.ascontiguousarray(f(state_gla)[0, 16 * c:16 * c + 16]),
            "s5re": np.ascontiguousarray(f(state_s5_re)[0, 16 * c:16 * c + 16].reshape(512, 128)),
            "s5im": np.ascontiguousarray(f(state_s5_im)[0, 16 * c:16 * c + 16].reshape(512, 128)),
            "s5p": np.ascontiguousarray(s5p), "bpad": bpad, "cpad": cpad,
        }
        in_maps.append(m)
    return in_maps


def assemble(results):
    y_p = np.stack([r["yTp"].T for r in results]).astype(np.float32)
    y_s = np.concatenate([r["yTs"].T.reshape(16, 4, D) for r in results]).astype(np.float32)
    gla_p = np.stack([r["oglap"] for r in results])[None].astype(np.float32)
    re_p = np.stack([r["o5rep"].reshape(64, 64) for r in results])[None].astype(np.float32)
    im_p = np.stack([r["o5imp"].reshape(64, 64) for r in results])[None].astype(np.float32)
    gla_s = np.concatenate([r["oglas"] for r in results])[None].astype(np.float32)
    re_s = np.concatenate([r["o5res"].reshape(16, 64, 64) for r in results])[None].astype(np.float32)
    im_s = np.concatenate([r["o5ims"].reshape(16, 64, 64) for r in results])[None].astype(np.float32)
    return (y_p, y_s, gla_p, re_p, im_p, gla_s, re_s, im_s)


_NC_CACHE = {}


def kernel(**inputs):
    in_maps = prepare_inputs(**inputs)
    if 'nc' not in _NC_CACHE:
        _NC_CACHE['nc'] = build_nc()
    res = run_bass_kernel_spmd(_NC_CACHE['nc'], in_maps, core_ids=list(range(NCORE)))
    return assemble(res.results)
```
